# Optimizing a Trainium2 kernel written in Bass

```python
import jax, jax.numpy as jnp
from jax import lax
import numpy as np

D_MODEL = 1024
BATCH = 4
SEQ = 4096
DEPTH = 1

HG_WIDTH = D_MODEL // 2
HG_HEAD_DIM = 128
HG_HEADS = HG_WIDTH // HG_HEAD_DIM
ML_WIDTH = D_MODEL - HG_WIDTH
ML_V_DIM = 128
ML_HEADS = ML_WIDTH // ML_V_DIM
ML_QK_DIM = ML_V_DIM // 2
D_MIX = HG_WIDTH + ML_WIDTH
CONV_WIDTH = 4
D_FF = 2816
CHUNK = 64
EPS = 1e-6
SPLIT_SIZES = (HG_WIDTH, HG_WIDTH, HG_WIDTH, HG_WIDTH,
               ML_HEADS * ML_QK_DIM, ML_HEADS * ML_QK_DIM,
               ML_WIDTH, ML_WIDTH,
               ML_HEADS, ML_HEADS)
D_IN = 4 * HG_WIDTH + 2 * ML_HEADS * ML_QK_DIM + 2 * ML_WIDTH + 2 * ML_HEADS

kernel_name = "hymba_hgrn2_mlstm_macaron_sandwich"


def rms_norm(x, g):
    xf = x.astype(jnp.float32)
    y = xf * lax.rsqrt(jnp.mean(xf * xf, axis=-1, keepdims=True) + EPS)
    return (y * g.astype(jnp.float32)).astype(x.dtype)


def swiglu_ffn(x, w_in, w_out):
    gate, up = jnp.split(x @ w_in, 2, axis=-1)
    return (jax.nn.silu(gate) * up) @ w_out


def split_cols(x, sizes):
    outs, start = [], 0
    for s in sizes:
        outs.append(x[..., start:start + s])
        start += s
    return outs


def causal_depthwise_conv(x, w, b):
    y = lax.conv_general_dilated(
        x, w[:, None, :].astype(x.dtype), window_strides=(1,),
        padding=[(CONV_WIDTH - 1, 0)], dimension_numbers=("NWC", "WIO", "NWC"),
        feature_group_count=x.shape[-1])
    return y + b.astype(x.dtype)


def to_chunks(x):
    B, T, H, d = x.shape
    return x.reshape(B, T // CHUNK, CHUNK, H, d).transpose(1, 0, 3, 2, 4)


def scalar_to_chunks(x):
    B, T, H = x.shape
    return x.reshape(B, T // CHUNK, CHUNK, H).transpose(1, 0, 3, 2)


def from_chunks(y):
    NC, B, H, L, d = y.shape
    return y.transpose(1, 0, 3, 2, 4).reshape(B, NC * L, H, d)


def hgrn2_chunkwise(q, k, v, log_f):
    B, T, H, dk = q.shape
    dv = v.shape[-1]
    causal = jnp.tril(jnp.ones((CHUNK, CHUNK), dtype=bool))[:, :, None]

    def step(S, inp):
        qc, kc, vc, gc = inp
        G = jnp.cumsum(gc, axis=2)
        o_inter = jnp.einsum("bhtk,bhkv->bhtv", qc * jnp.exp(G), S)
        rel = G[:, :, :, None, :] - G[:, :, None, :, :]
        decay = jnp.exp(jnp.where(causal, rel, -jnp.inf))
        scores = jnp.sum(qc[:, :, :, None, :] * kc[:, :, None, :, :] * decay, axis=-1)
        o = o_inter + jnp.einsum("bhts,bhsv->bhtv", scores, vc)
        G_end = G[:, :, -1:, :]
        S_new = jnp.exp(G_end[:, :, 0, :])[..., None] * S + jnp.einsum(
            "bhsk,bhsv->bhkv", kc * jnp.exp(G_end - G), vc)
        return S_new, o

    S0 = jnp.zeros((B, H, dk, dv), jnp.float32)
    _, o = lax.scan(step, S0, (to_chunks(q), to_chunks(k), to_chunks(v), to_chunks(log_f)))
    return from_chunks(o)


def mlstm_chunkwise(q, k, v, log_i, log_f):
    B, T, H, dqk = q.shape
    dv = v.shape[-1]
    causal = jnp.tril(jnp.ones((CHUNK, CHUNK), dtype=bool))

    def step(carry, inp):
        C, n, m = carry
        qc, kc, vc, ic, fc = inp
        b = jnp.cumsum(fc, axis=-1)
        a = b + m[..., None]
        D = jnp.where(causal, b[..., :, None] - b[..., None, :] + ic[..., None, :], -jnp.inf)
        m_t = jnp.maximum(a, jnp.max(D, axis=-1))
        inter = jnp.exp(a - m_t)
        P = jnp.einsum("bhtd,bhsd->bhts", qc, kc) * jnp.exp(D - m_t[..., None])
        num = inter[..., None] * jnp.einsum("bhtd,bhdv->bhtv", qc, C) + jnp.einsum("bhts,bhsv->bhtv", P, vc)
        den = inter * jnp.einsum("bhtd,bhd->bht", qc, n) + jnp.sum(P, axis=-1)
        h = num / jnp.maximum(jnp.abs(den), jnp.exp(-m_t))[..., None]
        m_new = m_t[..., -1]
        w = jnp.exp(b[..., -1:] - b + ic - m_new[..., None])
        keep = inter[..., -1]
        C_new = keep[..., None, None] * C + jnp.einsum("bhs,bhsd,bhsv->bhdv", w, kc, vc)
        n_new = keep[..., None] * n + jnp.einsum("bhs,bhsd->bhd", w, kc)
        return (C_new, n_new, m_new), h

    init = (jnp.zeros((B, H, dqk, dv), jnp.float32), jnp.zeros((B, H, dqk), jnp.float32),
            jnp.zeros((B, H), jnp.float32))
    _, h = lax.scan(step, init, (to_chunks(q), to_chunks(k), to_chunks(v),
                                 scalar_to_chunks(log_i), scalar_to_chunks(log_f)))
    return from_chunks(h)


def token_mixer(h, w_in, conv_w, conv_b, gate_bias, lb, hg_norm, ml_norm, w_out):
    B, T, _ = h.shape
    f32 = jnp.float32
    hq, hf, hi, hgate, mq, mk, mv, mo, mig, mfg = split_cols(h @ w_in, SPLIT_SIZES)
    hs = (B, T, HG_HEADS, HG_HEAD_DIM)
    lbh = lb.reshape(HG_HEADS, HG_HEAD_DIM)
    forget = lbh + (1.0 - lbh) * jax.nn.sigmoid(hf.astype(f32).reshape(hs))
    hg_o = hgrn2_chunkwise(jax.nn.silu(hq.astype(f32)).reshape(hs), 1.0 - forget,
                           hi.astype(f32).reshape(hs), jnp.log(forget))
    hg_o = rms_norm(hg_o, hg_norm.reshape(HG_HEADS, HG_HEAD_DIM)).reshape(B, T, HG_WIDTH)
    hg_o = hg_o * jax.nn.silu(hgate.astype(f32))
    qk = jax.nn.silu(causal_depthwise_conv(jnp.concatenate([mq, mk], axis=-1), conv_w, conv_b))
    mq, mk = jnp.split(qk.astype(f32), 2, axis=-1)
    qs = (B, T, ML_HEADS, ML_QK_DIM)
    gates = (jnp.concatenate([mig, mfg], axis=-1) + gate_bias).astype(f32)
    log_i = gates[..., :ML_HEADS]
    log_f = jax.nn.log_sigmoid(gates[..., ML_HEADS:])
    ml_o = mlstm_chunkwise(mq.reshape(qs) * (ML_QK_DIM ** -0.5), mk.reshape(qs),
                           mv.astype(f32).reshape(B, T, ML_HEADS, ML_V_DIM), log_i, log_f)
    ml_o = rms_norm(ml_o, ml_norm.reshape(ML_HEADS, ML_V_DIM)).reshape(B, T, ML_WIDTH)
    ml_o = ml_o * jax.nn.sigmoid(mo.astype(f32))
    mix = jnp.concatenate([hg_o, ml_o], axis=-1).astype(h.dtype)
    return mix @ w_out


def setup_inputs(seed: int = 0) -> dict:
    key = jax.random.key(seed)
    ks = jax.random.split(key, 16)
    nrm = jax.random.normal
    gate_bias = jnp.concatenate([
        0.1 * nrm(ks[8], (DEPTH, ML_HEADS)),
        jnp.linspace(3.0, 6.0, ML_HEADS)[None, :] + 0.1 * nrm(ks[9], (DEPTH, ML_HEADS))], axis=-1)
    return {
        "x": nrm(ks[0], (BATCH, SEQ, D_MODEL), jnp.float32),
        "norm_pre": 1.0 + 0.05 * nrm(ks[1], (DEPTH, 3, D_MODEL)),
        "norm_post": 1.0 + 0.05 * nrm(ks[2], (DEPTH, 3, D_MODEL)),
        "ffn1_w_in": nrm(ks[3], (DEPTH, D_MODEL, 2 * D_FF)) * D_MODEL ** -0.5,
        "ffn1_w_out": nrm(ks[4], (DEPTH, D_FF, D_MODEL)) * D_FF ** -0.5,
        "w_mix_in": nrm(ks[5], (DEPTH, D_MODEL, D_IN)) * D_MODEL ** -0.5,
        "conv_w": nrm(ks[6], (DEPTH, CONV_WIDTH, 2 * ML_HEADS * ML_QK_DIM)) * CONV_WIDTH ** -0.5,
        "conv_b": 0.02 * nrm(ks[7], (DEPTH, 2 * ML_HEADS * ML_QK_DIM)),
        "mlstm_gate_bias": gate_bias,
        "hgrn_lb_logits": 0.5 * nrm(ks[10], (DEPTH + 1, HG_WIDTH)),
        "hgrn_norm": 1.0 + 0.05 * nrm(ks[11], (DEPTH, HG_WIDTH)),
        "mlstm_norm": 1.0 + 0.05 * nrm(ks[12], (DEPTH, ML_WIDTH)),
        "w_mix_out": nrm(ks[13], (DEPTH, D_MIX, D_MODEL)) * D_MIX ** -0.5,
        "ffn2_w_in": nrm(ks[14], (DEPTH, D_MODEL, 2 * D_FF)) * D_MODEL ** -0.5,
        "ffn2_w_out": nrm(ks[15], (DEPTH, D_FF, D_MODEL)) * D_FF ** -0.5,
    }


def reference(x, norm_pre, norm_post, ffn1_w_in, ffn1_w_out, w_mix_in, conv_w, conv_b,
              mlstm_gate_bias, hgrn_lb_logits, hgrn_norm, mlstm_norm, w_mix_out,
              ffn2_w_in, ffn2_w_out):
    lower_bounds = jnp.cumsum(jax.nn.softmax(hgrn_lb_logits.astype(jnp.float32), axis=0), axis=0)
    for l in range(DEPTH):
        x = x + 0.5 * rms_norm(swiglu_ffn(rms_norm(x, norm_pre[l, 0]), ffn1_w_in[l], ffn1_w_out[l]),
                               norm_post[l, 0])
        mixed = token_mixer(rms_norm(x, norm_pre[l, 1]), w_mix_in[l], conv_w[l], conv_b[l],
                            mlstm_gate_bias[l], lower_bounds[l], hgrn_norm[l], mlstm_norm[l],
                            w_mix_out[l])
        x = x + rms_norm(mixed, norm_post[l, 1])
        x = x + 0.5 * rms_norm(swiglu_ffn(rms_norm(x, norm_pre[l, 2]), ffn2_w_in[l], ffn2_w_out[l]),
                               norm_post[l, 2])
    return x
```

```python
import contextlib
import numpy as np
import concourse.bass as bass
import concourse.mybir as mybir
from concourse.bass_utils import run_bass_kernel_spmd

F32 = mybir.dt.float32
BF16 = mybir.dt.bfloat16
AF = mybir.ActivationFunctionType
ALU = mybir.AluOpType

ENGS = ("pe", "act", "dve", "pool", "sp")
EPS = 1e-6
NTOK = 2048
D = 1024
DFF = 2816
NJ = DFF // 128
DIN = 3592
C_HQ, C_HF, C_HI, C_HG, C_MQ, C_MK, C_MV, C_MO, C_MIG, C_MFG = 0, 512, 1024, 1536, 2048, 2304, 2560, 3072, 3584, 3588
NST = 4 * 128 + 4 * 129 + 12
NCST = 648
TTK = 256
NS4 = TTK // 128


class Buf:
    __slots__ = ("name", "last_w", "readers")

    def __init__(self, name=""):
        self.name = name
        self.last_w = None
        self.readers = {}


class Op:
    __slots__ = ("idx", "eng", "fn", "dma", "deps", "signal", "count", "waits", "dsem", "dval", "cc")

    def __init__(self, idx, eng, fn, dma):
        self.idx = idx
        self.eng = eng
        self.fn = fn
        self.dma = dma
        self.deps = ()
        self.signal = False
        self.count = 0
        self.waits = []
        self.dsem = None
        self.dval = 0
        self.cc = False


class Prog:
    def __init__(self, nc, n_dma_sems=8):
        self.nc = nc
        self.ops = []
        self.K = n_dma_sems
        self.phase = Buf("phase")
        self._rec = None

    def begin_defer(self):
        self._rec = []

    def end_defer(self):
        r, self._rec = self._rec, None
        return r

    def replay(self, *seqs):
        seqs = [q for q in seqs if q]
        pos = [0] * len(seqs)
        while any(pos[i] < len(q) for i, q in enumerate(seqs)):
            i = min((i for i, q in enumerate(seqs) if pos[i] < len(q)), key=lambda i: pos[i] / len(seqs[i]))
            self.add(*seqs[i][pos[i]])
            pos[i] += 1

    def add(self, eng, fn, reads=(), writes=(), dma=False):
        if self._rec is not None:
            self._rec.append((eng, fn, tuple(reads), tuple(writes), dma))
            return None
        idx = len(self.ops)
        op = Op(idx, eng, fn, dma)
        deps = set()
        if self.phase not in writes:
            reads = list(reads) + [self.phase]
        for b in reads:
            if b.last_w is not None:
                deps.add(b.last_w)
        for b in writes:
            if b.last_w is not None:
                deps.add(b.last_w)
            deps.update(b.readers.values())
        deps.discard(idx)
        op.deps = deps
        for b in reads:
            b.readers[idx if dma else eng] = idx
        for b in writes:
            b.last_w = idx
            b.readers = {}
        self.ops.append(op)
        return op

    def pe(self, fn, r=(), w=()):
        return self.add("pe", fn, r, w)

    def act(self, fn, r=(), w=()):
        return self.add("act", fn, r, w)

    def dve(self, fn, r=(), w=()):
        return self.add("dve", fn, r, w)

    def pool(self, fn, r=(), w=()):
        return self.add("pool", fn, r, w)

    def dma(self, eng, out, in_, r=(), w=(), **kw):
        return self.add(eng, lambda e: e.dma_start(out=out, in_=in_, **kw), r, w, dma=True)

    def barrier(self, fn):
        return self.add("dve", fn, (), [self.phase])

    def cc(self, fn, r=(), w=()):
        op = self.add("pool", fn, r, w, dma=True)
        op.cc = True
        return op

    def emit(self, final_wait_ops=()):
        nc = self.nc
        ops = self.ops
        for op in ops:
            for d in op.deps:
                Dp = ops[d]
                if Dp.eng == "pe" and op.eng == "pe" and not Dp.dma:
                    continue
                Dp.signal = True
        for o in final_wait_ops:
            o.signal = True
        cnt = {e: 0 for e in ENGS}
        dcnt = {e: 0 for e in ENGS}
        for op in ops:
            if op.cc:
                op.dsem = ("c", op.idx)
                op.dval = 1
            elif op.dma:
                n = dcnt[op.eng]
                dcnt[op.eng] += 1
                op.dsem = (op.eng, n % self.K)
                op.dval = 16 * (n // self.K + 1)
            elif op.signal:
                cnt[op.eng] += 1
                op.count = cnt[op.eng]
        waited = {e: {} for e in ENGS}
        for op in ops:
            need = {}
            for d in op.deps:
                Dp = ops[d]
                if Dp.dma:
                    key = ("d",) + Dp.dsem
                    val = Dp.dval
                else:
                    if Dp.eng == "pe" and op.eng == "pe":
                        continue
                    key = ("e", Dp.eng)
                    val = Dp.count
                if need.get(key, 0) < val:
                    need[key] = val
            if op.dma and not op.cc:
                prev = op.dval - 16
                if prev > 0:
                    key = ("d",) + op.dsem
                    if need.get(key, 0) < prev:
                        need[key] = prev
            w = waited[op.eng]
            for key, val in need.items():
                if w.get(key, 0) < val:
                    w[key] = val
                    op.waits.append((key, val))
        fence = []
        w = waited["sp"]
        for o in final_wait_ops:
            if o.dma:
                key = ("d",) + o.dsem
                val = o.dval
            else:
                key = ("e", o.eng)
                val = o.count
            if w.get(key, 0) < val:
                w[key] = val
                fence.append((key, val))
        self.stats = {e: sum(1 for o in ops if o.eng == e) for e in ENGS}
        self.stats["waits"] = sum(len(o.waits) for o in ops)
        self.stats["maxcnt"] = dict(cnt)
        with contextlib.ExitStack() as st:
            sems = {}
            for e in ENGS:
                sems[("e", e)] = st.enter_context(nc.semaphore("s_" + e))
            for e in ENGS:
                for k in range(min(self.K, dcnt[e])):
                    sems[("d", e, k)] = st.enter_context(nc.semaphore("d_%s_%d" % (e, k)))
            for op in ops:
                if op.cc:
                    sems[("d",) + op.dsem] = st.enter_context(nc.semaphore("c_%d" % op.idx))
            block = st.enter_context(nc.Block())

            def run(engobj, ename, extra=()):
                for op in ops:
                    if op.eng != ename:
                        continue
                    for key, val in op.waits:
                        engobj.wait_ge(sems[key], val)
                    if op.fn is None:
                        continue
                    ins = op.fn(engobj)
                    if op.cc:
                        ins.then_inc(sems[("d",) + op.dsem], 1)
                    elif op.dma:
                        ins.then_inc(sems[("d",) + op.dsem], 16)
                    elif op.signal:
                        ins.then_inc(sems[("e", ename)], 1)
                for key, val in extra:
                    engobj.wait_ge(sems[key], val)

            @block.tensor
            def _(e):
                run(e, "pe")

            @block.scalar
            def _(e):
                run(e, "act")

            @block.vector
            def _(e):
                run(e, "dve")

            @block.gpsimd
            def _(e):
                run(e, "pool")

            @block.sync
            def _(e):
                run(e, "sp", fence)


class Arena:
    def __init__(self, nc, base, limit):
        self.nc = nc
        self.off = base
        self.limit = limit
        self.n = 0

    def alloc(self, shape, dt, name=None):
        nb = int(np.prod(shape[1:])) * (4 if dt == F32 else 2)
        nb = (nb + 31) // 32 * 32
        off = self.off
        self.off += nb
        assert self.off <= self.limit, ("SBUF overflow", name, self.off, self.limit)
        self.n += 1
        return self.nc.alloc_sbuf_tensor_at("%s_%d_%d" % (name or "t", off, self.n), list(shape), dt, offset=off)

    def fork(self):
        return Arena(self.nc, self.off, self.limit)


def MM(out, lhsT, rhs, start=True, stop=True):
    return lambda e: e.matmul(out, lhsT=lhsT, rhs=rhs, start=start, stop=stop)


def TR(out, in_, ident):
    return lambda e: e.transpose(out=out, in_=in_, identity=ident)


def ACTF(out, in_, func, **kw):
    return lambda e: e.activation(out=out, in_=in_, func=func, **kw)


def TT(out, a, b, op):
    return lambda e: e.tensor_tensor(out=out, in0=a, in1=b, op=op)


def TS(out, a, s1, s2, op0, op1=None):
    if op1 is None:
        return lambda e: e.tensor_scalar(out=out, in0=a, scalar1=s1, scalar2=None, op0=op0)
    return lambda e: e.tensor_scalar(out=out, in0=a, scalar1=s1, scalar2=s2, op0=op0, op1=op1)


def STT(out, in0, scalar, in1, op0, op1):
    return lambda e: e.scalar_tensor_tensor(out=out, in0=in0, scalar=scalar, in1=in1, op0=op0, op1=op1)


def CP(out, in_):
    return lambda e: e.tensor_copy(out=out, in_=in_)


def AC(out, in_):
    return lambda e: e.copy(out=out, in_=in_)


def MS(out, val):
    return lambda e: e.memset(out, val)


def build_program(stage=3):
    nc = bass.Bass("TRN2", target_bir_lowering=False)
    di = lambda n, s: nc.dram_tensor(n, list(s), F32, kind="ExternalInput").ap()
    x_d = di("x", [NTOK, D])
    w1i_d = di("w1i", [D, 2 * DFF]); w1o_d = di("w1o", [DFF, D])
    w2i_d = di("w2i", [D, 2 * DFF]); w2o_d = di("w2o", [DFF, D])
    wmi_d = di("wmi", [D, DIN]); wmo_d = di("wmo", [D, D])
    gpre_d = di("gpre", [128, 3 * D]); gpost_d = di("gpost", [128, 3 * D])
    lbl_d = di("lbl", [128, 2 * 512]); hgn_d = di("hgn", [128, 4]); mln_d = di("mln", [128, 512])
    cw_d = di("cw", [128, 16]); cb_d = di("cb", [128, 4]); gb_d = di("gb", [128, 8]); flag_d = di("flag", [128, 1])
    cst_d = di("cst", [128, NCST])
    out_d = nc.dram_tensor("out", [NTOK, D], F32, kind="ExternalOutput").ap()
    cc_src = nc.dram_tensor("cc_src", [128, NST], F32, kind="Internal").ap()
    cc_dst = nc.dram_tensor("cc_dst", [256, NST], F32, kind="Internal").ap()

    P = Prog(nc)
    A = Arena(nc, 16512, 229376)
    x_sb = A.alloc([128, 16, D], F32, "x")
    gpre_sb = A.alloc([128, D], F32, "gpre"); gpost_sb = A.alloc([128, D], F32, "gpost")
    cst_sb = A.alloc([128, NCST], F32, "cst")
    ident_bf = A.alloc([128, 128], BF16, "ident")
    ones_bf = A.alloc([128, 128], BF16, "ones_bf")
    junk = A.alloc([128, D], BF16, "junk")
    ybf = [A.alloc([128, D], BF16, "ybf%d" % i) for i in range(2)]
    stat = A.alloc([128, 64], F32, "stat")
    Bx = [Buf("x%d" % s) for s in range(16)]
    Bgpre, Bgpost, Bcst, Bident, Bjunk = Buf(), Buf(), Buf(), Buf(), Buf()
    Bybf = [Buf(), Buf()]
    Bstat = [Buf() for _ in range(16)]
    TRI = cst_sb[:, 128:256]; DM = cst_sb[:, 256:384]; EM = cst_sb[:, 384:512]; ONES = cst_sb[:, 512:640]
    SELC = cst_sb[:, 640:644]; MAB = cst_sb[:, 644:646]
    psum = [nc.alloc_psum_tensor("ps%d" % i, [128, 512], F32) for i in range(6)]
    psum_b = [nc.alloc_psum_tensor("psb%d" % i, [128, 1024], BF16) for i in range(2)]
    Bpt = [Buf("pt0"), Buf("pt1")]
    tb_ctr = [0]
    tb_pool = [[0, 1]]

    def nbt():
        pl = tb_pool[0]
        i = pl[tb_ctr[0] % len(pl)]
        tb_ctr[0] += 1
        return i

    Bps = [Buf("ps%d" % i) for i in range(8)]
    bank_ctr = [0]
    bank_pool = [list(range(6))]

    def nb():
        pl = bank_pool[0]
        i = pl[bank_ctr[0] % len(pl)]
        bank_ctr[0] += 1
        return i

    P.dma("sp", cst_sb[:], cst_d[:], w=[Bcst])
    xv = x_d.rearrange("(s p) d -> p s d", p=128)
    ov = out_d.rearrange("(s p) d -> p s d", p=128)
    Bxchain = Buf("xchain")
    for q in range(4):
        P.dma("sp", x_sb[:, 4 * q:4 * q + 4, :], xv[:, 4 * q:4 * q + 4, :], w=Bx[4 * q:4 * q + 4] + [Bxchain])
    P.dve(CP(ident_bf[:], cst_sb[:, 0:128]), r=[Bcst], w=[Bident])
    P.dve(CP(ones_bf[:], cst_sb[:, 512:640]), r=[Bcst], w=[Bident])

    def prenorm_T(s, dstT, dcols, Bdst, k):
        yb = ybf[k % 2]; By = Bybf[k % 2]
        st_ = stat[:, 4 * s:4 * s + 4]
        P.act(ACTF(junk[:], x_sb[:, s, :], AF.Square, accum_out=st_[:, 0:1]), r=[Bx[s]], w=[Bjunk, Bstat[s]])
        P.dve(TS(st_[:, 1:2], st_[:, 0:1], 1.0 / D, EPS, ALU.mult, ALU.add), r=[Bstat[s]], w=[Bstat[s]])
        P.act(ACTF(st_[:, 2:3], st_[:, 1:2], AF.Ln), r=[Bstat[s]], w=[Bstat[s]])
        P.act(ACTF(st_[:, 2:3], st_[:, 2:3], AF.Exp, scale=-0.5), r=[Bstat[s]], w=[Bstat[s]])
        P.dve(STT(yb[:], x_sb[:, s, :], st_[:, 2:3], gpre_sb[:], ALU.mult, ALU.mult), r=[Bx[s], Bstat[s], Bgpre], w=[By])
        b = nbt()
        pv = psum_b[b][:].rearrange("p (a c) -> p a c", a=8)
        for dc in range(8):
            P.pe(TR(pv[:, dc, :], yb[:, dc * 128:(dc + 1) * 128], ident_bf[:]), r=[By, Bident], w=[Bpt[b]])
        P.act(AC(dstT[:, :, dcols], pv), r=[Bpt[b]], w=[Bdst])

    def postnorm_res(s, banks, gscale, k, tmp, Btmp):
        st_ = stat[:, 4 * s:4 * s + 4]
        for hh in range(2):
            P.act(ACTF(junk[:, 0:512], psum[banks[hh]][:], AF.Square, accum_out=st_[:, hh:hh + 1]),
                  r=[Bps[banks[hh]]], w=[Bjunk, Bstat[s]])
        P.dve(TT(st_[:, 2:3], st_[:, 0:1], st_[:, 1:2], ALU.add), r=[Bstat[s]], w=[Bstat[s]])
        P.dve(TS(st_[:, 2:3], st_[:, 2:3], 1.0 / D, EPS, ALU.mult, ALU.add), r=[Bstat[s]], w=[Bstat[s]])
        P.act(ACTF(st_[:, 3:4], st_[:, 2:3], AF.Ln), r=[Bstat[s]], w=[Bstat[s]])
        P.act(ACTF(st_[:, 3:4], st_[:, 3:4], AF.Exp, scale=-0.5), r=[Bstat[s]], w=[Bstat[s]])
        P.dve(TS(st_[:, 3:4], st_[:, 3:4], gscale, None, ALU.mult), r=[Bstat[s]], w=[Bstat[s]])
        for hh in range(2):
            t = tmp[(2 * k + hh) % len(tmp)]; Bt = Btmp[(2 * k + hh) % len(tmp)]
            P.dve(STT(t[:], psum[banks[hh]][:], st_[:, 3:4], gpost_sb[:, hh * 512:(hh + 1) * 512], ALU.mult, ALU.mult),
                  r=[Bps[banks[hh]], Bstat[s], Bgpost], w=[Bt])
            P.dve(TT(x_sb[:, s, hh * 512:(hh + 1) * 512], x_sb[:, s, hh * 512:(hh + 1) * 512], t[:], ALU.add),
                   r=[Bt, Bx[s]], w=[Bx[s]])

    def ffn(wi_d, wo_d, gi, is_last):
        bank_pool[0] = list(range(6))
        F = A.fork()
        yT = F.alloc([128, 8, 1024], BF16, "yT"); ByT = [Buf() for _ in range(8)]
        actT = F.alloc([128, NJ, 1024], BF16, "actT"); Bact = [[Buf(), Buf()] for _ in range(NJ)]
        wout = F.alloc([128, NJ, D], BF16, "wout"); Bwout = [Buf(), Buf()]
        win = [F.alloc([128, 8, 512], BF16, "win%d" % i) for i in range(2)]; Bwin = [[Buf(), Buf()], [Buf(), Buf()]]
        sg = [F.alloc([128, 512], BF16, "sg%d" % i) for i in range(2)]; Bsg = [Buf(), Buf()]
        tmp = [F.alloc([128, 512], F32, "tmp%d" % i) for i in range(2)]; Btmp = [Buf(), Buf()]
        P.dma("sp", gpre_sb[:], gpre_d[:, gi * D:(gi + 1) * D], w=[Bgpre])
        P.dma("sp", gpost_sb[:], gpost_d[:, gi * D:(gi + 1) * D], w=[Bgpost])
        wiv = wi_d.rearrange("(c p) n -> p c n", p=128)
        wov = wo_d.rearrange("(j p) n -> p j n", p=128)
        outs = []
        wout_loaded = False
        for s8 in range(8):
            prenorm_T(s8, yT, slice(s8 * 128, (s8 + 1) * 128), ByT[s8], s8)
        for ps_ in range(2):
            for jp in range(NJ // 2):
                k = ps_ * (NJ // 2) + jp
                w_ = win[k % 2]; Bw = Bwin[k % 2]
                P.dma("pool", w_[:, :, 0:256], wiv[:, :, jp * 256:(jp + 1) * 256], w=[Bw[0]])
                P.dma("pool", w_[:, :, 256:512], wiv[:, :, DFF + jp * 256:DFF + (jp + 1) * 256], w=[Bw[1]])
                if not wout_loaded and jp == 4:
                    P.dma("pool", wout[:, 0:11, :], wov[:, 0:11, :], w=[Bwout[0]])
                    P.dma("pool", wout[:, 11:22, :], wov[:, 11:22, :], w=[Bwout[1]])
                    wout_loaded = True
                for jj in range(2):
                    j = jp * 2 + jj
                    bg = [nb(), nb()]
                    bu = [nb(), nb()]
                    for which, banks, coff in ((0, bg, jj * 128), (1, bu, 256 + jj * 128)):
                        for tt in range(2):
                            for dc in range(8):
                                P.pe(MM(psum[banks[tt]][:], w_[:, dc, coff:coff + 128], yT[:, dc, tt * 512:(tt + 1) * 512],
                                        start=(dc == 0), stop=(dc == 7)),
                                     r=[Bw[which]] + ByT[tt * 4:tt * 4 + 4], w=[Bps[banks[tt]]])
                    for tt in range(2):
                        kk_ = (2 * j + tt) % 2
                        P.act(ACTF(sg[kk_][:], psum[bg[tt]][:], AF.Silu), r=[Bps[bg[tt]]], w=[Bsg[kk_]])
                        P.dve(TT(actT[:, j, tt * 512:(tt + 1) * 512], sg[kk_][:], psum[bu[tt]][:], ALU.mult),
                              r=[Bsg[kk_], Bps[bu[tt]]], w=[Bact[j][tt]])
            for s8 in range(8):
                s = ps_ * 8 + s8
                banks = [nb(), nb()]
                for hh in range(2):
                    for j in range(NJ):
                        P.pe(MM(psum[banks[hh]][:], actT[:, j, s8 * 128:(s8 + 1) * 128], wout[:, j, hh * 512:(hh + 1) * 512],
                                start=(j == 0), stop=(j == NJ - 1)),
                             r=[Bact[j][s8 // 4], Bwout[j // 11]], w=[Bps[banks[hh]]])
                if ps_ == 0:
                    prenorm_T(8 + s8, yT, slice(s8 * 128, (s8 + 1) * 128), ByT[s8], 8 + s8)
                postnorm_res(s, banks, 0.5, s8, tmp, Btmp)
                if is_last:
                    outs.append(P.dma("sp", ov[:, s, :], x_sb[:, s, :], r=[Bx[s]]))
        return outs

    def mixer():
        bank_pool[0] = list(range(6))
        M = A.fork()
        wmi = M.alloc([128, 8, DIN], BF16, "wmi"); Bwmi = [Buf() for _ in range(4)]
        wmo_off = M.off
        wmo = M.alloc([128, 8, D], BF16, "wmo"); Bwmo = Buf()
        ALT = Arena(nc, wmo_off, M.off)
        lb_bc = M.alloc([128, 512], F32, "lb")
        mln_sb = M.alloc([128, 512], F32, "mln")
        small = M.alloc([128, 64], F32, "small")
        hgn_sb = small[:, 0:4]; cw_sb = small[:, 4:20]; cb_sb = small[:, 20:24]; gb_sb = small[:, 24:32]; flag_sb = small[:, 32:33]
        Bpar = Buf()
        S_sb = M.alloc([128, 4, 128], F32, "S"); C_sb = M.alloc([128, 4, 129], F32, "C")
        BS = [Buf() for _ in range(4)]; BC = [Buf() for _ in range(4)]
        qk_pre = M.alloc([128, 4, TTK + 3], F32, "qkpre"); Bqk = [Buf() for _ in range(4)]
        Bgath = Buf()
        hT = M.alloc([128, 8, TTK], BF16, "hT"); BhT = [Buf() for _ in range(NS4)]
        NB = 1
        fF = [M.alloc([128, 512], F32, "fF0"), ALT.alloc([128, 512], F32, "fF1")]; BfF = [Buf(), Buf()]
        lgF = [M.alloc([128, 512], F32, "lgF0"), ALT.alloc([128, 512], F32, "lgF1")]; BlgF = [Buf(), Buf()]
        eF = [M.alloc([128, 512], BF16, "eF%d" % i) for i in range(NB)]; BeF = [Buf() for _ in range(NB)]
        e2F = [M.alloc([128, 512], BF16, "e2F%d" % i) for i in range(NB)]; Be2F = [Buf() for _ in range(NB)]
        qF = [M.alloc([128, 512], BF16, "qF%d" % i) for i in range(NB)]; BqF = [Buf() for _ in range(NB)]
        kdd = [M.alloc([128, 512], BF16, "kdd0"), ALT.alloc([128, 512], BF16, "kdd1")]; Bkdd = [Buf(), Buf()]
        qd, Bqd, kd, Bkd = qF, BqF, e2F, Be2F
        vbf = [M.alloc([128, 512], BF16, "vbf0"), ALT.alloc([128, 512], BF16, "vbf1")]; Bvbf = [Buf(), Buf()]
        goff = M.off
        qdT = M.alloc([128, 4, TTK], BF16, "qdT"); BqdT = [Buf() for _ in range(NS4)]
        kdT = M.alloc([128, 4, TTK], BF16, "kdT"); BkdT = [Buf() for _ in range(NS4)]
        hgT = M.alloc([128, 4, TTK], BF16, "hgT"); BhgT = Buf()
        assert M.off - goff >= NST * 4
        gath = nc.alloc_sbuf_tensor_at("gath_alias", [128, NST], F32, offset=goff)
        eS = [M.alloc([128, 16], F32, "eS0"), ALT.alloc([128, 16], F32, "eS1")]; BeS = [Buf(), Buf()]
        Sb = [[M.alloc([128, 128], BF16, "Sb%d_%d" % (i, c)) for c in range(2)] for i in range(4)]
        BSb = [[Buf(), Buf()] for _ in range(4)]
        scm = [M.alloc([128, 128], BF16, "scm%d" % i) for i in range(4)]; Bscm = [Buf() for _ in range(4)]
        sqF = [M.alloc([128, 512], F32, "sqF%d" % i) for i in range(1)]; BsqF = [Buf()]
        rsF = [M.alloc([128, 512], F32, "rsF%d" % i) for i in range(1)]; BrsF = [Buf()]
        mixT = M.alloc([128, 8, TTK], BF16, "mixT"); BmixT = [Buf() for _ in range(NS4)]
        tmp = [sqF[0], rsF[0]]; Btmp = [BsqF[0], BrsF[0]]
        qT = M.alloc([128, 2, TTK], BF16, "qT"); BqT = Buf()
        qTA = M.alloc([128, 2, TTK], BF16, "qTA"); qTB = M.alloc([128, 2, TTK], BF16, "qTB"); BqTAB = Buf()
        kT = M.alloc([128, 2, TTK], BF16, "kT"); BkT = Buf()
        caccs = [(sqF[0][:, 0:TTK], BsqF[0]), (rsF[0][:, 0:TTK], BrsF[0])]
        mg = [M.alloc([128, 40], F32, "mg0"), ALT.alloc([128, 40], F32, "mg1")]; Bmg = [Buf(), Buf()]
        vp = [M.alloc([128, 4, 129], BF16, "vp0"), ALT.alloc([128, 4, 129], BF16, "vp1")]; Bvp = [Buf(), Buf()]
        kw = [M.alloc([128, 4, 64], BF16, "kw0"), ALT.alloc([128, 4, 64], BF16, "kw1")]; Bkw = [Buf(), Buf()]
        sgm = [M.alloc([128, 512], BF16, "sgm%d" % i) for i in range(NB)]; Bsgm = [Buf() for _ in range(NB)]
        Cb = [M.alloc([128, 4, 129], BF16, "Cb%d" % c) for c in range(2)]; BCb = [[Buf(), Buf()] for _ in range(4)]
        ebL = [M.alloc([128, 8], F32, "ebL0"), ALT.alloc([128, 8], F32, "ebL1")]; BebL = [Buf(), Buf()]
        mlo = [M.alloc([128, 512], BF16, "mlo%d" % i) for i in range(NB)]; Bmlo = [Buf() for _ in range(NB)]
        hjunk = junk[:, 0:128]; Bhjunk = Bjunk
        print("mixer arena end", M.off, "limit", M.limit)

        P.dma("sp", gpre_sb[:], gpre_d[:, D:2 * D], w=[Bgpre])
        P.dma("sp", gpost_sb[:], gpost_d[:, D:2 * D], w=[Bgpost])
        P.dma("sp", fF[0][:], lbl_d[:, 0:512], w=[BfF[0]])
        P.dma("sp", lgF[0][:], lbl_d[:, 512:1024], w=[BlgF[0]])
        P.dma("sp", mln_sb[:], mln_d[:], w=[Bpar])
        P.dma("sp", hgn_sb, hgn_d[:], w=[Bpar]); P.dma("sp", cw_sb, cw_d[:], w=[Bpar]); P.dma("sp", cb_sb, cb_d[:], w=[Bpar])
        P.dma("sp", gb_sb, gb_d[:], w=[Bpar]); P.dma("sp", flag_sb, flag_d[:], w=[Bpar])
        wmiv = wmi_d.rearrange("(c p) n -> p c n", p=128)
        colsplit = [0, 1024, 2048, 3072, DIN]
        Bwchain = Buf("wmichain")
        for i in (2, 0, 1, 3):
            P.dma("pool", wmi[:, :, colsplit[i]:colsplit[i + 1]], wmiv[:, :, colsplit[i]:colsplit[i + 1]], w=[Bwmi[i], Bwchain])

        def wbuf(c0, c1):
            return [Bwmi[i] for i in range(4) if colsplit[i] < c1 and colsplit[i + 1] > c0]

        l0, l1, lm = fF[0], lgF[0], sqF[0]
        Bl0, Bl1, Blm = BfF[0], BlgF[0], BsqF[0]
        P.dve(TT(lm[:], l0[:], l1[:], ALU.max), r=[Bl0, Bl1], w=[Blm])
        P.dve(TT(l0[:], l0[:], lm[:], ALU.subtract), r=[Blm, Bl0], w=[Bl0])
        P.dve(TT(l1[:], l1[:], lm[:], ALU.subtract), r=[Blm, Bl1], w=[Bl1])
        P.act(ACTF(l0[:], l0[:], AF.Exp), r=[Bl0], w=[Bl0])
        P.act(ACTF(l1[:], l1[:], AF.Exp), r=[Bl1], w=[Bl1])
        P.dve(TT(l1[:], l0[:], l1[:], ALU.add), r=[Bl0, Bl1], w=[Bl1])
        P.dve(lambda e: e.reciprocal(out=l1[:], in_=l1[:]), r=[Bl1], w=[Bl1])
        P.dve(TT(lb_bc[:], l0[:], l1[:], ALU.mult), r=[Bl0, Bl1], w=[Bpar])

        def run_pass(pass2):
            if not pass2:
                for h in range(4):
                    P.pool(MS(S_sb[:, h, :], 0.0), w=[BS[h]])
                    P.pool(MS(C_sb[:, h, :], 0.0), w=[BC[h]])
                for g in range(4):
                    P.pool(MS(qk_pre[:, g, :], 0.0), w=[Bqk[g]])
            else:
                for h in range(4):
                    P.dve(TS(S_sb[:, h, :], gath[:, h * 128:(h + 1) * 128], flag_sb, None, ALU.mult), r=[Bgath, Bpar], w=[BS[h]])
                    P.dve(TS(C_sb[:, h, :], gath[:, 512 + h * 129:512 + (h + 1) * 129], flag_sb, None, ALU.mult), r=[Bgath, Bpar], w=[BC[h]])
                for g in range(4):
                    P.dve(TS(qk_pre[:, g, 0:3], gath[:, 1028 + 3 * g:1028 + 3 * g + 3], flag_sb, None, ALU.mult), r=[Bgath, Bpar], w=[Bqk[g]])
            for tt in range(NTOK // TTK):
                run_tile(tt, pass2)

        def proj_tok(s4, c0, n, bank, col0=0):
            for dc in range(8):
                P.pe(MM(psum[bank][:, col0:col0 + n], hT[:, dc, s4 * 128:(s4 + 1) * 128], wmi[:, dc, c0:c0 + n],
                        start=(dc == 0), stop=(dc == 7)),
                     r=[BhT[s4]] + wbuf(c0, c0 + n), w=[Bps[bank]])

        def proj_feat(c0, bank):
            for dc in range(8):
                P.pe(MM(psum[bank][:, 0:TTK], wmi[:, dc, c0:c0 + 128], hT[:, dc, :], start=(dc == 0), stop=(dc == 7)),
                     r=BhT + wbuf(c0, c0 + 128), w=[Bps[bank]])

        def tile_prenorm(tt):
            for s4 in range(NS4):
                s = tt * NS4 + s4
                prenorm_T(s, hT, slice(s4 * 128, (s4 + 1) * 128), BhT[s4], s)

        def run_tile(tt, pass2):
            if tt == 0:
                tile_prenorm(0)
            groups = (0, 1, 2, 3)
            for g in groups:
                if g < 2 and not pass2 and tt != NTOK // TTK - 1:
                    continue
                b = nb()
                proj_feat((C_MQ if g < 2 else C_MK) + (g % 2) * 128, b)
                P.act(lambda e, o=qk_pre[:, g, 3:TTK + 3], i=psum[b][:, 0:TTK]: e.copy(out=o, in_=i), r=[Bps[b]], w=[Bqk[g]])
                if g >= 2 or pass2:
                    cacc, Bcacc = caccs[g % 2]
                    P.dve(TS(cacc, qk_pre[:, g, 0:TTK], cw_sb[:, 4 * g:4 * g + 1], cb_sb[:, g:g + 1], ALU.mult, ALU.add),
                           r=[Bqk[g], Bpar], w=[Bcacc])
                    for j in range(1, 4):
                        P.dve(STT(cacc, qk_pre[:, g, j:j + TTK], cw_sb[:, 4 * g + j:4 * g + j + 1], cacc, ALU.mult, ALU.add),
                               r=[Bqk[g], Bpar, Bcacc], w=[Bcacc])
                    dst, Bd = (qT, BqT) if g < 2 else (kT, BkT)
                    P.act(ACTF(dst[:, g % 2, :], cacc, AF.Sigmoid), r=[Bcacc], w=[Bd])
                    P.dve(TT(dst[:, g % 2, :], cacc, dst[:, g % 2, :], ALU.mult), r=[Bcacc, Bd], w=[Bd])
                P.pool(CP(qk_pre[:, g, 0:3], qk_pre[:, g, TTK:TTK + 3]), r=[Bqk[g]], w=[Bqk[g]])
            if pass2:
                qv = qT[:].rearrange("p g (s c t) -> p g s c t", s=NS4, c=2)
                qav = qTA[:].rearrange("p g (s c t) -> p g s c t", s=NS4, c=2)
                qbv = qTB[:].rearrange("p g (s c t) -> p g s c t", s=NS4, c=2)
                for g in range(2):
                    P.pool(MS(qTA[:, g, :], 0.0), w=[BqTAB])
                    P.pool(MS(qTB[:, g, :], 0.0), w=[BqTAB])
                    P.pool(CP(qav[:, g, :, 0, :], qv[:, g, :, 0, :]), r=[BqT], w=[BqTAB])
                    P.pool(CP(qbv[:, g, :, 1, :], qv[:, g, :, 1, :]), r=[BqT], w=[BqTAB])
                for h in range(4):
                    b = nb()
                    proj_feat(C_HG + h * 128, b)
                    P.act(ACTF(tmp[h % 2][:, 0:TTK], psum[b][:, 0:TTK], AF.Sigmoid), r=[Bps[b]], w=[Btmp[h % 2]])
                    P.dve(STT(hgT[:, h, :], psum[b][:, 0:TTK], hgn_sb[:, h:h + 1], tmp[h % 2][:, 0:TTK], ALU.mult, ALU.mult),
                          r=[Bps[b], Btmp[h % 2], Bpar], w=[BhgT])
            for s4 in range(NS4):
                nxt = (lambda t=tt + 1: tile_prenorm(t)) if (s4 == NS4 - 1 and tt + 1 < NTOK // TTK) else None
                run_subtile(tt, s4, pass2, nxt)

        def run_subtile(tt, s4, pass2, after_prep=None):
            s = tt * NS4 + s4
            k = 0 if pass2 else s % 2
            tok = slice(s4 * 128, (s4 + 1) * 128)
            b_hf = nb(); proj_tok(s4, C_HF, 512, b_hf)
            P.act(ACTF(fF[k][:], psum[b_hf][:], AF.Sigmoid, scale=-1.0), r=[Bps[b_hf]], w=[BfF[k]])
            if pass2:
                b_hq = nb(); proj_tok(s4, C_HQ, 512, b_hq)
                P.act(ACTF(qF[k][:], psum[b_hq][:], AF.Sigmoid), r=[Bps[b_hq]], w=[BqF[k]])
                b_mo = nb(); proj_tok(s4, C_MO, 512, b_mo)
                P.act(ACTF(sgm[k][:], psum[b_mo][:], AF.Sigmoid), r=[Bps[b_mo]], w=[Bsgm[k]])
                P.dve(TT(qF[k][:], psum[b_hq][:], qF[k][:], ALU.mult), r=[Bps[b_hq], BqF[k]], w=[BqF[k]])
            P.begin_defer()
            bank_pool[0] = [0, 1, 2]; tb_pool[0] = [0]
            P.dve(STT(fF[k][:], lb_bc[:], 1.0, fF[k][:], ALU.subtract, ALU.mult), r=[Bpar, BfF[k]], w=[BfF[k]])
            P.act(ACTF(lgF[k][:], fF[k][:], AF.Ln, bias=1.0), r=[BfF[k]], w=[BlgF[k]])
            b_hi = nb(); proj_tok(s4, C_HI, 512, b_hi)
            P.act(lambda e, o=vbf[k][:], i=psum[b_hi][:]: e.copy(out=o, in_=i), r=[Bps[b_hi]], w=[Bvbf[k]])
            b_gl = nb()
            P.pe(MM(psum[b_gl][:], EM, lgF[k][:]), r=[Bcst, BlgF[k]], w=[Bps[b_gl]])
            P.act(ACTF(kdd[k][:], psum[b_gl][:], AF.Exp), r=[Bps[b_gl]], w=[Bkdd[k]])
            P.dve(STT(kdd[k][:], fF[k][:], -1.0, kdd[k][:], ALU.mult, ALU.mult), r=[BfF[k], Bkdd[k]], w=[Bkdd[k]])
            b_sc = nb()
            for h in range(4):
                P.pe(MM(psum[b_sc][:, 4 * h:4 * h + 4], lgF[k][:, h * 128:(h + 1) * 128], SELC), r=[Bcst, BlgF[k]], w=[Bps[b_sc]])
            P.act(ACTF(eS[k][:], psum[b_sc][:, 0:16], AF.Exp), r=[Bps[b_sc]], w=[BeS[k]])
            if pass2:
                b_gm = nb()
                P.pe(MM(psum[b_gm][:], DM, lgF[k][:]), r=[Bcst, BlgF[k]], w=[Bps[b_gm]])
                P.act(ACTF(eF[k][:], psum[b_gm][:], AF.Exp), r=[Bps[b_gm]], w=[BeF[k]])
                P.act(ACTF(e2F[k][:], psum[b_gm][:], AF.Exp, scale=-1.0), r=[Bps[b_gm]], w=[Be2F[k]])
                P.dve(TT(qd[k][:], qF[k][:], eF[k][:], ALU.mult), r=[BqF[k], BeF[k]], w=[Bqd[k]])
                P.dve(STT(kd[k][:], fF[k][:], -1.0, e2F[k][:], ALU.mult, ALU.mult), r=[BfF[k], Be2F[k]], w=[Bkd[k]])
                for src, Bsrc, dstT, BdT in ((qd[k], Bqd[k], qdT, BqdT), (kd[k], Bkd[k], kdT, BkdT)):
                    b = nbt()
                    pv = psum_b[b][:, 0:512].rearrange("p (a c) -> p a c", a=4)
                    for h in range(4):
                        P.pe(TR(pv[:, h, :], src[:, h * 128:(h + 1) * 128], ident_bf[:]), r=[Bsrc, Bident], w=[Bpt[b]])
                    P.dve(CP(dstT[:, :, tok], pv), r=[Bpt[b]], w=[BdT[s4]])
            seq_hg = P.end_defer()
            P.begin_defer()
            bank_pool[0] = [3, 4, 5]; tb_pool[0] = [1]
            g_ = mg[k]
            b_g = nb(); proj_tok(s4, C_MIG, 8, b_g)
            P.dve(TT(g_[:, 0:8], psum[b_g][:, 0:8], gb_sb, ALU.add), r=[Bps[b_g], Bpar], w=[Bmg[k]])
            P.act(ACTF(g_[:, 8:12], g_[:, 4:8], AF.Exp, scale=-1.0), r=[Bmg[k]], w=[Bmg[k]])
            P.act(ACTF(g_[:, 8:12], g_[:, 8:12], AF.Ln, bias=1.0), r=[Bmg[k]], w=[Bmg[k]])
            P.dve(TS(g_[:, 8:12], g_[:, 8:12], -1.0, None, ALU.mult), r=[Bmg[k]], w=[Bmg[k]])
            P.pe(MM(psum[b_g][:, 8:12], TRI, g_[:, 8:12]), r=[Bcst, Bmg[k]], w=[Bps[b_g]])
            P.pe(MM(psum[b_g][:, 12:16], EM, g_[:, 8:12]), r=[Bcst, Bmg[k]], w=[Bps[b_g]])
            P.dve(TT(g_[:, 12:16], psum[b_g][:, 12:16], g_[:, 0:4], ALU.add), r=[Bps[b_g], Bmg[k]], w=[Bmg[k]])
            P.act(ACTF(g_[:, 12:16], g_[:, 12:16], AF.Exp), r=[Bmg[k]], w=[Bmg[k]])
            if pass2:
                P.act(ACTF(g_[:, 16:20], psum[b_g][:, 8:12], AF.Exp), r=[Bps[b_g]], w=[Bmg[k]])
                P.dve(TS(g_[:, 16:20], g_[:, 16:20], 0.125, None, ALU.mult), r=[Bmg[k]], w=[Bmg[k]])
                P.dve(TT(g_[:, 20:24], g_[:, 0:4], psum[b_g][:, 8:12], ALU.subtract), r=[Bps[b_g], Bmg[k]], w=[Bmg[k]])
                P.act(ACTF(g_[:, 20:24], g_[:, 20:24], AF.Exp), r=[Bmg[k]], w=[Bmg[k]])
            P.dve(TS(g_[:, 24:28], g_[:, 8:12], MAB[:, 0:1], None, ALU.mult), r=[Bmg[k], Bcst], w=[Bmg[k]])
            P.dve(TS(g_[:, 28:32], g_[:, 8:12], MAB[:, 1:2], None, ALU.mult), r=[Bmg[k], Bcst], w=[Bmg[k]])
            P.pe(MM(psum[b_g][:, 16:24], ONES, g_[:, 24:32]), r=[Bcst, Bmg[k]], w=[Bps[b_g]])
            P.act(ACTF(ebL[k][:], psum[b_g][:, 16:24], AF.Exp), r=[Bps[b_g]], w=[BebL[k]])
            b_v = nb(); proj_tok(s4, C_MV, 512, b_v)
            P.pool(MS(vp[k][:, :, 128:129], 1.0), w=[Bvp[k]])
            P.act(lambda e, o=vp[k][:, :, 0:128], i=psum[b_v][:].rearrange("p (h v) -> p h v", h=4): e.copy(out=o, in_=i),
                  r=[Bps[b_v]], w=[Bvp[k]])
            for g in range(2):
                b = nbt()
                pv = psum_b[b][:, 0:128]
                P.pe(TR(pv, kT[:, g, tok], ident_bf[:]), r=[BkT, Bident], w=[Bpt[b]])
                for hh in range(2):
                    h = 2 * g + hh
                    P.dve(TS(kw[k][:, h, :], pv[:, hh * 64:(hh + 1) * 64], g_[:, 12 + h:13 + h], None, ALU.mult),
                          r=[Bpt[b], Bmg[k]], w=[Bkw[k]])
            seq_ml = P.end_defer()
            bank_pool[0] = list(range(6)); tb_pool[0] = [0, 1]
            P.replay(seq_hg, seq_ml)
            if after_prep is not None:
                after_prep()
            if pass2:
                b_s = nb()
                for h in range(4):
                    P.pe(MM(psum[b_s][:, h * 128:(h + 1) * 128], kdT[:, h, tok], qdT[:, h, tok]), r=[BkdT[s4], BqdT[s4]], w=[Bps[b_s]])
                for h in range(4):
                    P.dve(TT(scm[h][:], psum[b_s][:, h * 128:(h + 1) * 128], TRI, ALU.mult), r=[Bps[b_s], Bcst], w=[Bscm[h]])
            for c in range(2):
                cs = slice(c * 64, (c + 1) * 64)
                if pass2:
                    for h in range(4):
                        P.act(ACTF(Sb[h][c][:], S_sb[:, h, :], AF.Copy, scale=eS[k][:, 4 * h + c:4 * h + c + 1]),
                              r=[BS[h], BeS[k]], w=[BSb[h][c]])
                b_u = nb()
                for h in range(4):
                    hs = slice(h * 128, (h + 1) * 128)
                    P.pe(MM(psum[b_u][:, hs], kdd[k][cs, hs], vbf[k][cs, hs]), r=[Bkdd[k], Bvbf[k]], w=[Bps[b_u]])
                for h in range(4):
                    hs = slice(h * 128, (h + 1) * 128)
                    P.dve(STT(S_sb[:, h, :], S_sb[:, h, :], eS[k][:, 4 * h + 2 + c:4 * h + 3 + c], psum[b_u][:, hs], ALU.mult, ALU.add),
                          r=[BS[h], BeS[k], Bps[b_u]], w=[BS[h]])
            HP = [((h % 2) * 64, h // 2) for h in range(4)]
            if pass2:
                for h in range(4):
                    hp, g = HP[h]
                    P.act(AC(Cb[0][hp:hp + 64, h, :], C_sb[hp:hp + 64, h, :]), r=[BC[h]], w=[BCb[h][0]])
            for c in range(2):
                cs = slice(c * 64, (c + 1) * 64)
                bu = [nb(), nb()]
                for h in range(4):
                    hp, g = HP[h]
                    P.pe(MM(psum[bu[h // 2]][:, (h % 2) * 129:(h % 2) * 129 + 129],
                            kw[k][cs, g * 2:g * 2 + 2, :].rearrange("p a b -> p (a b)"), vp[k][cs, h, :]),
                         r=[Bkw[k], Bvp[k]], w=[Bps[bu[h // 2]]])
                for h in range(4):
                    hp, g = HP[h]
                    hsl = slice(hp, hp + 64)
                    P.dve(STT(C_sb[hsl, h, :], C_sb[hsl, h, :], ebL[k][hsl, 4 * c + h:4 * c + h + 1],
                              psum[bu[h // 2]][hsl, (h % 2) * 129:(h % 2) * 129 + 129], ALU.mult, ALU.add),
                          r=[BC[h], BebL[k], Bps[bu[h // 2]]], w=[BC[h]])
                if pass2 and c == 0:
                    for h in range(4):
                        hp, g = HP[h]
                        P.act(AC(Cb[1][hp:hp + 64, h, :], C_sb[hp:hp + 64, h, :]), r=[BC[h]], w=[BCb[h][1]])
            if pass2:
                b_o = nb()
                for h in range(4):
                    hs = slice(h * 128, (h + 1) * 128)
                    for c in range(2):
                        cs = slice(c * 64, (c + 1) * 64)
                        ctok = slice(s4 * 128 + c * 64, s4 * 128 + (c + 1) * 64)
                        oo = psum[b_o][:, h * 128 + c * 64:h * 128 + (c + 1) * 64]
                        P.pe(MM(oo, vbf[k][:, hs], scm[h][:, cs], start=True, stop=False), r=[Bvbf[k], Bscm[h]], w=[Bps[b_o]])
                        P.pe(MM(oo, Sb[h][c][:], qdT[:, h, ctok], start=False, stop=True), r=[BSb[h][c], BqdT[s4]], w=[Bps[b_o]])
                b_oo = [nb(), nb()]
                bsx = [nb(), nb()]
                for h in range(4):
                    hp, g = HP[h]
                    P.pe(MM(psum[bsx[h % 2]][:, g * 128:(g + 1) * 128], kT[hp:hp + 64, g, tok], qT[hp:hp + 64, g, tok]),
                         r=[BkT, BqT], w=[Bps[bsx[h % 2]]])
                for h in range(4):
                    hp, g = HP[h]
                    P.dve(STT(scm[h][:], psum[bsx[h % 2]][:, g * 128:(g + 1) * 128], g_[:, 20 + h:21 + h], TRI, ALU.mult, ALU.mult),
                          r=[Bps[bsx[h % 2]], Bmg[k], Bcst], w=[Bscm[h]])
                for h in range(4):
                    hp, g = HP[h]
                    oo = psum[b_oo[h // 2]][:, (h % 2) * 129:(h % 2) * 129 + 129]
                    P.pe(MM(oo, scm[h][:], vp[k][:, h, :], start=True, stop=False), r=[Bscm[h], Bvp[k]], w=[Bps[b_oo[h // 2]]])
                    P.pe(MM(oo, qTA[hp:hp + 64, g, tok], Cb[0][hp:hp + 64, h, :], start=False, stop=False),
                         r=[BqTAB, BCb[h][0]], w=[Bps[b_oo[h // 2]]])
                    P.pe(MM(oo, qTB[hp:hp + 64, g, tok], Cb[1][hp:hp + 64, h, :], start=False, stop=True),
                         r=[BqTAB, BCb[h][1]], w=[Bps[b_oo[h // 2]]])
            if pass2:
                P.begin_defer()
                P.act(ACTF(eF[k][:], psum[b_o][:], AF.Square), r=[Bps[b_o]], w=[BeF[k]])
                b_q = nb()
                P.pe(MM(psum[b_q][:], ones_bf[:], eF[k][:]), r=[Bident, BeF[k]], w=[Bps[b_q]])
                P.dve(TS(rsF[0][:], psum[b_q][:], 1.0 / 128, EPS, ALU.mult, ALU.add), r=[Bps[b_q]], w=[BrsF[0]])
                P.act(ACTF(rsF[0][:], rsF[0][:], AF.Ln), r=[BrsF[0]], w=[BrsF[0]])
                P.act(ACTF(rsF[0][:], rsF[0][:], AF.Exp, scale=-0.5), r=[BrsF[0]], w=[BrsF[0]])
                P.dve(TT(rsF[0][:], psum[b_o][:], rsF[0][:], ALU.mult), r=[Bps[b_o], BrsF[0]], w=[BrsF[0]])
                P.dve(TT(mixT[:, 0:4, tok], rsF[0][:].rearrange("p (h t) -> p h t", h=4), hgT[:, :, tok], ALU.mult),
                      r=[BrsF[0], BhgT], w=[BmixT[s4]])
            if pass2:
                seq_ho = P.end_defer()
                P.begin_defer()
                for h in range(4):
                    ob = psum[b_oo[h // 2]]
                    o0 = (h % 2) * 129
                    P.dve(TT(g_[:, 32 + h:33 + h], ob[:, o0 + 128:o0 + 129], g_[:, 16 + h:17 + h], ALU.mult),
                          r=[Bps[b_oo[h // 2]], Bmg[k]], w=[Bmg[k]])
                P.dve(TS(g_[:, 36:40], g_[:, 32:36], -1.0, None, ALU.mult), r=[Bmg[k]], w=[Bmg[k]])
                P.dve(TT(g_[:, 32:36], g_[:, 32:36], g_[:, 36:40], ALU.max), r=[Bmg[k]], w=[Bmg[k]])
                P.dve(TS(g_[:, 32:36], g_[:, 32:36], 1.0, None, ALU.max), r=[Bmg[k]], w=[Bmg[k]])
                P.dve(lambda e, o=g_[:, 32:36], i=g_[:, 32:36]: e.reciprocal(out=o, in_=i), r=[Bmg[k]], w=[Bmg[k]])
                P.dve(TT(g_[:, 32:36], g_[:, 32:36], g_[:, 16:20], ALU.mult), r=[Bmg[k]], w=[Bmg[k]])
                for h in range(4):
                    ob = psum[b_oo[h // 2]]
                    o0 = (h % 2) * 129
                    P.act(ACTF(hjunk, ob[:, o0:o0 + 128], AF.Square, scale=g_[:, 32 + h:33 + h], accum_out=g_[:, 36 + h:37 + h]),
                          r=[Bps[b_oo[h // 2]], Bmg[k]], w=[Bhjunk, Bmg[k]])
                P.dve(TS(g_[:, 36:40], g_[:, 36:40], 1.0 / 128, EPS, ALU.mult, ALU.add), r=[Bmg[k]], w=[Bmg[k]])
                P.act(ACTF(g_[:, 36:40], g_[:, 36:40], AF.Ln), r=[Bmg[k]], w=[Bmg[k]])
                P.act(ACTF(g_[:, 36:40], g_[:, 36:40], AF.Exp, scale=-0.5), r=[Bmg[k]], w=[Bmg[k]])
                P.dve(TT(g_[:, 36:40], g_[:, 36:40], g_[:, 32:36], ALU.mult), r=[Bmg[k]], w=[Bmg[k]])
                for h in range(4):
                    ob = psum[b_oo[h // 2]]
                    o0 = (h % 2) * 129
                    P.dve(STT(mlo[k][:, h * 128:(h + 1) * 128], ob[:, o0:o0 + 128], g_[:, 36 + h:37 + h], mln_sb[:, h * 128:(h + 1) * 128],
                              ALU.mult, ALU.mult), r=[Bps[b_oo[h // 2]], Bmg[k], Bpar], w=[Bmlo[k]])
                P.dve(TT(mlo[k][:], mlo[k][:], sgm[k][:], ALU.mult), r=[Bsgm[k], Bmlo[k]], w=[Bmlo[k]])
                seq_mo = P.end_defer()
                P.replay(seq_ho, seq_mo)
                b = nbt()
                pv = psum_b[b][:, 0:512].rearrange("p (a c) -> p a c", a=4)
                for h in range(4):
                    P.pe(TR(pv[:, h, :], mlo[k][:, h * 128:(h + 1) * 128], ident_bf[:]), r=[Bmlo[k], Bident], w=[Bpt[b]])
                P.act(lambda e, o=mixT[:, 4:8, tok], i=pv: e.copy(out=o, in_=i), r=[Bpt[b]], w=[BmixT[s4]])
                banks = [nb(), nb()]
                for hh in range(2):
                    for jc in range(8):
                        P.pe(MM(psum[banks[hh]][:], mixT[:, jc, tok], wmo[:, jc, hh * 512:(hh + 1) * 512], start=(jc == 0), stop=(jc == 7)),
                             r=[BmixT[s4], Bwmo], w=[Bps[banks[hh]]])
                postnorm_res(s, banks, 1.0, s4, tmp, Btmp)

        run_pass(False)
        P.barrier(MS(stat[:, 63:64], 0.0))
        P.dma("pool", wmo[:], wmo_d.rearrange("(c p) n -> p c n", p=128), w=[Bwmo])
        Bex = [Buf() for _ in range(12)]
        for h in range(4):
            P.dma("sp", cc_src[:, h * 128:(h + 1) * 128], S_sb[:, h, :], r=[BS[h]], w=[Bex[3 * h]])
            P.dma("sp", cc_src[:, 512 + h * 129:512 + (h + 1) * 129], C_sb[:, h, :], r=[BC[h]], w=[Bex[3 * h + 1]])
            P.dma("sp", cc_src[:, 1028 + 3 * h:1028 + 3 * h + 3], qk_pre[:, h, 0:3], r=[Bqk[h]], w=[Bex[3 * h + 2]])
        Bdst = Buf()
        P.cc(lambda e: e.collective_compute("AllGather", ALU.bypass, replica_groups=[[0, 1], [2, 3], [4, 5], [6, 7]],
                                            ins=[cc_src.opt()], outs=[cc_dst.opt()]), Bex, [Bdst])
        P.dma("sp", gath[:], cc_dst[0:128, :], r=[Bdst], w=[Bgath])
        run_pass(True)

    outs = []
    if stage >= 1:
        o1 = ffn(w1i_d, w1o_d, 0, stage == 1)
        outs = o1
    if stage >= 2:
        P.barrier(MS(stat[:, 63:64], 0.0))
        mixer()
        if stage == 2:
            outs = [P.dma("sp", ov[:, s, :], x_sb[:, s, :], r=[Bx[s]]) for s in range(16)]
    if stage >= 3:
        P.barrier(MS(stat[:, 63:64], 0.0))
        outs = ffn(w2i_d, w2o_d, 2, True)
    P.emit(final_wait_ops=outs)
    print("ops", P.stats)
    return nc


def _consts():
    c = np.zeros((128, NCST), np.float32)
    s = np.arange(128)[:, None]
    t = np.arange(128)[None, :]
    same = (s // 64) == (t // 64)
    c[:, 0:128] = np.eye(128, dtype=np.float32)
    tri = (same & (s <= t)).astype(np.float32)
    sel = (same & ((s % 64) <= 31)).astype(np.float32)
    c[:, 128:256] = tri
    c[:, 256:384] = tri - sel
    c[:, 384:512] = (same & (s > t)).astype(np.float32)
    c[:, 512:640] = 1.0
    sv = np.arange(128)
    c[:, 640] = (sv <= 31)
    c[:, 641] = (sv >= 64) & (sv <= 95)
    c[:, 642] = (sv < 64)
    c[:, 643] = (sv >= 64)
    c[:, 644] = (sv < 64)
    c[:, 645] = (sv >= 64)
    return c


_NC_CACHE = {}


def _prep_inputs(inputs):
    f = lambda a: np.ascontiguousarray(np.asarray(a, dtype=np.float32))
    x = f(inputs["x"])
    rep = lambda v: np.ascontiguousarray(np.broadcast_to(np.asarray(v, np.float32).reshape(1, -1), (128, np.asarray(v).size)))
    common = {
        "w1i": f(inputs["ffn1_w_in"][0]), "w1o": f(inputs["ffn1_w_out"][0]),
        "w2i": f(inputs["ffn2_w_in"][0]), "w2o": f(inputs["ffn2_w_out"][0]),
        "wmi": f(inputs["w_mix_in"][0]), "wmo": f(inputs["w_mix_out"][0]),
        "gpre": rep(inputs["norm_pre"][0]), "gpost": rep(inputs["norm_post"][0]),
        "lbl": rep(inputs["hgrn_lb_logits"]),
        "hgn": f(np.asarray(inputs["hgrn_norm"][0]).reshape(4, 128).T),
        "mln": rep(inputs["mlstm_norm"][0]),
        "cw": f(np.asarray(inputs["conv_w"][0]).reshape(4, 4, 128).transpose(2, 1, 0).reshape(128, 16)),
        "cb": f(np.asarray(inputs["conv_b"][0]).reshape(4, 128).T),
        "gb": rep(inputs["mlstm_gate_bias"][0]),
        "cst": _consts(),
    }
    in_maps = []
    for c in range(8):
        b, half = c // 2, c % 2
        m = dict(common)
        m["x"] = np.ascontiguousarray(x[b, half * NTOK:(half + 1) * NTOK, :])
        m["flag"] = np.full((128, 1), float(half), np.float32)
        in_maps.append(m)
    return in_maps


def kernel(**inputs):
    stage = int(inputs.pop("_stage", 3)) if "_stage" in inputs else 3
    if stage not in _NC_CACHE:
        _NC_CACHE[stage] = build_program(stage)
    nc = _NC_CACHE[stage]
    in_maps = _prep_inputs(inputs)
    res = run_bass_kernel_spmd(nc, in_maps, core_ids=list(range(8)))
    out = np.empty((4, 2 * NTOK, D), np.float32)
    for c in range(8):
        b, half = c // 2, c % 2
        out[b, half * NTOK:(half + 1) * NTOK, :] = res.results[c]["out"]
    return out
```

```python
import contextlib
import numpy as np
import concourse.bass as bass
import concourse.mybir as mybir
from concourse.bass_utils import run_bass_kernel_spmd

F32 = mybir.dt.float32
BF16 = mybir.dt.bfloat16
AF = mybir.ActivationFunctionType
ALU = mybir.AluOpType

ENGS = ("pe", "act", "dve", "pool", "sp")
EPS = 1e-6
NTOK = 2048
D = 1024
DFF = 2816
NJ = DFF // 128
DIN = 3592
C_HQ, C_HF, C_HI, C_HG, C_MQ, C_MK, C_MV, C_MO, C_MIG, C_MFG = 0, 512, 1024, 1536, 2048, 2304, 2560, 3072, 3584, 3588
NST = 4 * 128 + 4 * 129 + 12
NCST = 648
TTK = 256
NS4 = TTK // 128


class Buf:
    __slots__ = ("name", "last_w", "readers")

    def __init__(self, name=""):
        self.name = name
        self.last_w = None
        self.readers = {}


class Op:
    __slots__ = ("idx", "eng", "fn", "dma", "deps", "signal", "count", "waits", "dsem", "dval", "cc")

    def __init__(self, idx, eng, fn, dma):
        self.idx = idx
        self.eng = eng
        self.fn = fn
        self.dma = dma
        self.deps = ()
        self.signal = False
        self.count = 0
        self.waits = []
        self.dsem = None
        self.dval = 0
        self.cc = False


class Prog:
    def __init__(self, nc, n_dma_sems=8):
        self.nc = nc
        self.ops = []
        self.K = n_dma_sems
        self.phase = Buf("phase")
        self._rec = None

    def begin_defer(self):
        self._rec = []

    def end_defer(self):
        r, self._rec = self._rec, None
        return r

    def replay(self, *seqs):
        seqs = [q for q in seqs if q]
        pos = [0] * len(seqs)
        while any(pos[i] < len(q) for i, q in enumerate(seqs)):
            i = min((i for i, q in enumerate(seqs) if pos[i] < len(q)), key=lambda i: pos[i] / len(seqs[i]))
            self.add(*seqs[i][pos[i]])
            pos[i] += 1

    def add(self, eng, fn, reads=(), writes=(), dma=False):
        if self._rec is not None:
            self._rec.append((eng, fn, tuple(reads), tuple(writes), dma))
            return None
        idx = len(self.ops)
        op = Op(idx, eng, fn, dma)
        deps = set()
        if self.phase not in writes:
            reads = list(reads) + [self.phase]
        for b in reads:
            if b.last_w is not None:
                deps.add(b.last_w)
        for b in writes:
            if b.last_w is not None:
                deps.add(b.last_w)
            deps.update(b.readers.values())
        deps.discard(idx)
        op.deps = deps
        for b in reads:
            b.readers[idx if dma else eng] = idx
        for b in writes:
            b.last_w = idx
            b.readers = {}
        self.ops.append(op)
        return op

    def pe(self, fn, r=(), w=()):
        return self.add("pe", fn, r, w)

    def act(self, fn, r=(), w=()):
        return self.add("act", fn, r, w)

    def dve(self, fn, r=(), w=()):
        return self.add("dve", fn, r, w)

    def pool(self, fn, r=(), w=()):
        return self.add("pool", fn, r, w)

    def dma(self, eng, out, in_, r=(), w=(), **kw):
        return self.add(eng, lambda e: e.dma_start(out=out, in_=in_, **kw), r, w, dma=True)

    def barrier(self, fn):
        return self.add("dve", fn, (), [self.phase])

    def cc(self, fn, r=(), w=()):
        op = self.add("pool", fn, r, w, dma=True)
        op.cc = True
        return op

    def emit(self, final_wait_ops=()):
        nc = self.nc
        ops = self.ops
        for op in ops:
            for d in op.deps:
                Dp = ops[d]
                if Dp.eng == "pe" and op.eng == "pe" and not Dp.dma:
                    continue
                Dp.signal = True
        for o in final_wait_ops:
            o.signal = True
        cnt = {e: 0 for e in ENGS}
        dcnt = {e: 0 for e in ENGS}
        for op in ops:
            if op.cc:
                op.dsem = ("c", op.idx)
                op.dval = 1
            elif op.dma:
                n = dcnt[op.eng]
                dcnt[op.eng] += 1
                op.dsem = (op.eng, n % self.K)
                op.dval = 16 * (n // self.K + 1)
            elif op.signal:
                cnt[op.eng] += 1
                op.count = cnt[op.eng]
        waited = {e: {} for e in ENGS}
        for op in ops:
            need = {}
            for d in op.deps:
                Dp = ops[d]
                if Dp.dma:
                    key = ("d",) + Dp.dsem
                    val = Dp.dval
                else:
                    if Dp.eng == "pe" and op.eng == "pe":
                        continue
                    key = ("e", Dp.eng)
                    val = Dp.count
                if need.get(key, 0) < val:
                    need[key] = val
            if op.dma and not op.cc:
                prev = op.dval - 16
                if prev > 0:
                    key = ("d",) + op.dsem
                    if need.get(key, 0) < prev:
                        need[key] = prev
            w = waited[op.eng]
            for key, val in need.items():
                if w.get(key, 0) < val:
                    w[key] = val
                    op.waits.append((key, val))
        fence = []
        w = waited["sp"]
        for o in final_wait_ops:
            if o.dma:
                key = ("d",) + o.dsem
                val = o.dval
            else:
                key = ("e", o.eng)
                val = o.count
            if w.get(key, 0) < val:
                w[key] = val
                fence.append((key, val))
        self.stats = {e: sum(1 for o in ops if o.eng == e) for e in ENGS}
        self.stats["waits"] = sum(len(o.waits) for o in ops)
        self.stats["maxcnt"] = dict(cnt)
        with contextlib.ExitStack() as st:
            sems = {}
            for e in ENGS:
                sems[("e", e)] = st.enter_context(nc.semaphore("s_" + e))
            for e in ENGS:
                for k in range(min(self.K, dcnt[e])):
                    sems[("d", e, k)] = st.enter_context(nc.semaphore("d_%s_%d" % (e, k)))
            for op in ops:
                if op.cc:
                    sems[("d",) + op.dsem] = st.enter_context(nc.semaphore("c_%d" % op.idx))
            block = st.enter_context(nc.Block())

            def run(engobj, ename, extra=()):
                for op in ops:
                    if op.eng != ename:
                        continue
                    for key, val in op.waits:
                        engobj.wait_ge(sems[key], val)
                    if op.fn is None:
                        continue
                    ins = op.fn(engobj)
                    if op.cc:
                        ins.then_inc(sems[("d",) + op.dsem], 1)
                    elif op.dma:
                        ins.then_inc(sems[("d",) + op.dsem], 16)
                    elif op.signal:
                        ins.then_inc(sems[("e", ename)], 1)
                for key, val in extra:
                    engobj.wait_ge(sems[key], val)

            @block.tensor
            def _(e):
                run(e, "pe")

            @block.scalar
            def _(e):
                run(e, "act")

            @block.vector
            def _(e):
                run(e, "dve")

            @block.gpsimd
            def _(e):
                run(e, "pool")

            @block.sync
            def _(e):
                run(e, "sp", fence)


class Arena:
    def __init__(self, nc, base, limit):
        self.nc = nc
        self.off = base
        self.limit = limit
        self.n = 0

    def alloc(self, shape, dt, name=None):
        nb = int(np.prod(shape[1:])) * (4 if dt == F32 else 2)
        nb = (nb + 31) // 32 * 32
        off = self.off
        self.off += nb
        assert self.off <= self.limit, ("SBUF overflow", name, self.off, self.limit)
        self.n += 1
        return self.nc.alloc_sbuf_tensor_at("%s_%d_%d" % (name or "t", off, self.n), list(shape), dt, offset=off)

    def fork(self):
        return Arena(self.nc, self.off, self.limit)


def MM(out, lhsT, rhs, start=True, stop=True):
    return lambda e: e.matmul(out, lhsT=lhsT, rhs=rhs, start=start, stop=stop)


def TR(out, in_, ident):
    return lambda e: e.transpose(out=out, in_=in_, identity=ident)


def ACTF(out, in_, func, **kw):
    return lambda e: e.activation(out=out, in_=in_, func=func, **kw)


def TT(out, a, b, op):
    return lambda e: e.tensor_tensor(out=out, in0=a, in1=b, op=op)


def TS(out, a, s1, s2, op0, op1=None):
    if op1 is None:
        return lambda e: e.tensor_scalar(out=out, in0=a, scalar1=s1, scalar2=None, op0=op0)
    return lambda e: e.tensor_scalar(out=out, in0=a, scalar1=s1, scalar2=s2, op0=op0, op1=op1)


def STT(out, in0, scalar, in1, op0, op1):
    return lambda e: e.scalar_tensor_tensor(out=out, in0=in0, scalar=scalar, in1=in1, op0=op0, op1=op1)


def CP(out, in_):
    return lambda e: e.tensor_copy(out=out, in_=in_)


def AC(out, in_):
    return lambda e: e.copy(out=out, in_=in_)


def MS(out, val):
    return lambda e: e.memset(out, val)


def build_program(stage=3):
    nc = bass.Bass("TRN2", target_bir_lowering=False)
    di = lambda n, s: nc.dram_tensor(n, list(s), F32, kind="ExternalInput").ap()
    x_d = di("x", [NTOK, D])
    w1i_d = di("w1i", [D, 2 * DFF]); w1o_d = di("w1o", [DFF, D])
    w2i_d = di("w2i", [D, 2 * DFF]); w2o_d = di("w2o", [DFF, D])
    wmi_d = di("wmi", [D, DIN]); wmo_d = di("wmo", [D, D])
    gpre_d = di("gpre", [128, 3 * D]); gpost_d = di("gpost", [128, 3 * D])
    lbl_d = di("lbl", [128, 2 * 512]); hgn_d = di("hgn", [128, 4]); mln_d = di("mln", [128, 512])
    cw_d = di("cw", [128, 16]); cb_d = di("cb", [128, 4]); gb_d = di("gb", [128, 8]); flag_d = di("flag", [128, 1])
    cst_d = di("cst", [128, NCST])
    out_d = nc.dram_tensor("out", [NTOK, D], F32, kind="ExternalOutput").ap()
    cc_src = nc.dram_tensor("cc_src", [128, NST], F32, kind="Internal").ap()
    cc_dst = nc.dram_tensor("cc_dst", [256, NST], F32, kind="Internal").ap()

    P = Prog(nc)
    A = Arena(nc, 16512, 229376)
    x_sb = A.alloc([128, 16, D], F32, "x")
    gpre_sb = A.alloc([128, D], F32, "gpre"); gpost_sb = A.alloc([128, D], F32, "gpost")
    cst_sb = A.alloc([128, NCST], F32, "cst")
    ident_bf = A.alloc([128, 128], BF16, "ident")
    ones_bf = A.alloc([128, 128], BF16, "ones_bf")
    junk = A.alloc([128, D], BF16, "junk")
    ybf = [A.alloc([128, D], BF16, "ybf%d" % i) for i in range(2)]
    stat = A.alloc([128, 64], F32, "stat")
    Bx = [Buf("x%d" % s) for s in range(16)]
    Bgpre, Bgpost, Bcst, Bident, Bjunk = Buf(), Buf(), Buf(), Buf(), Buf()
    Bybf = [Buf(), Buf()]
    Bstat = [Buf() for _ in range(16)]
    TRI = cst_sb[:, 128:256]; DM = cst_sb[:, 256:384]; EM = cst_sb[:, 384:512]; ONES = cst_sb[:, 512:640]
    SELC = cst_sb[:, 640:644]; MAB = cst_sb[:, 644:646]
    psum = [nc.alloc_psum_tensor("ps%d" % i, [128, 512], F32) for i in range(6)]
    psum_b = [nc.alloc_psum_tensor("psb%d" % i, [128, 1024], BF16) for i in range(2)]
    Bpt = [Buf("pt0"), Buf("pt1")]
    tb_ctr = [0]
    tb_pool = [[0, 1]]

    def nbt():
        pl = tb_pool[0]
        i = pl[tb_ctr[0] % len(pl)]
        tb_ctr[0] += 1
        return i

    Bps = [Buf("ps%d" % i) for i in range(8)]
    bank_ctr = [0]
    bank_pool = [list(range(6))]

    def nb():
        pl = bank_pool[0]
        i = pl[bank_ctr[0] % len(pl)]
        bank_ctr[0] += 1
        return i

    P.dma("sp", cst_sb[:], cst_d[:], w=[Bcst])
    xv = x_d.rearrange("(s p) d -> p s d", p=128)
    ov = out_d.rearrange("(s p) d -> p s d", p=128)
    for q in range(4):
        P.dma("sp", x_sb[:, 4 * q:4 * q + 4, :], xv[:, 4 * q:4 * q + 4, :], w=Bx[4 * q:4 * q + 4])
    P.dve(CP(ident_bf[:], cst_sb[:, 0:128]), r=[Bcst], w=[Bident])
    P.dve(CP(ones_bf[:], cst_sb[:, 512:640]), r=[Bcst], w=[Bident])

    def prenorm_T(s, dstT, dcols, Bdst, k):
        yb = ybf[k % 2]; By = Bybf[k % 2]
        st_ = stat[:, 4 * s:4 * s + 4]
        P.act(ACTF(junk[:], x_sb[:, s, :], AF.Square, accum_out=st_[:, 0:1]), r=[Bx[s]], w=[Bjunk, Bstat[s]])
        P.dve(TS(st_[:, 1:2], st_[:, 0:1], 1.0 / D, EPS, ALU.mult, ALU.add), r=[Bstat[s]], w=[Bstat[s]])
        P.act(ACTF(st_[:, 2:3], st_[:, 1:2], AF.Ln), r=[Bstat[s]], w=[Bstat[s]])
        P.act(ACTF(st_[:, 2:3], st_[:, 2:3], AF.Exp, scale=-0.5), r=[Bstat[s]], w=[Bstat[s]])
        P.dve(STT(yb[:], x_sb[:, s, :], st_[:, 2:3], gpre_sb[:], ALU.mult, ALU.mult), r=[Bx[s], Bstat[s], Bgpre], w=[By])
        b = nbt()
        pv = psum_b[b][:].rearrange("p (a c) -> p a c", a=8)
        for dc in range(8):
            P.pe(TR(pv[:, dc, :], yb[:, dc * 128:(dc + 1) * 128], ident_bf[:]), r=[By, Bident], w=[Bpt[b]])
        P.act(AC(dstT[:, :, dcols], pv), r=[Bpt[b]], w=[Bdst])

    def postnorm_res(s, banks, gscale, k, tmp, Btmp):
        st_ = stat[:, 4 * s:4 * s + 4]
        for hh in range(2):
            P.act(ACTF(junk[:, 0:512], psum[banks[hh]][:], AF.Square, accum_out=st_[:, hh:hh + 1]),
                  r=[Bps[banks[hh]]], w=[Bjunk, Bstat[s]])
        P.dve(TT(st_[:, 2:3], st_[:, 0:1], st_[:, 1:2], ALU.add), r=[Bstat[s]], w=[Bstat[s]])
        P.dve(TS(st_[:, 2:3], st_[:, 2:3], 1.0 / D, EPS, ALU.mult, ALU.add), r=[Bstat[s]], w=[Bstat[s]])
        P.act(ACTF(st_[:, 3:4], st_[:, 2:3], AF.Ln), r=[Bstat[s]], w=[Bstat[s]])
        P.act(ACTF(st_[:, 3:4], st_[:, 3:4], AF.Exp, scale=-0.5), r=[Bstat[s]], w=[Bstat[s]])
        P.dve(TS(st_[:, 3:4], st_[:, 3:4], gscale, None, ALU.mult), r=[Bstat[s]], w=[Bstat[s]])
        for hh in range(2):
            t = tmp[(2 * k + hh) % len(tmp)]; Bt = Btmp[(2 * k + hh) % len(tmp)]
            P.dve(STT(t[:], psum[banks[hh]][:], st_[:, 3:4], gpost_sb[:, hh * 512:(hh + 1) * 512], ALU.mult, ALU.mult),
                  r=[Bps[banks[hh]], Bstat[s], Bgpost], w=[Bt])
            P.dve(TT(x_sb[:, s, hh * 512:(hh + 1) * 512], x_sb[:, s, hh * 512:(hh + 1) * 512], t[:], ALU.add),
                   r=[Bt, Bx[s]], w=[Bx[s]])

    def ffn(wi_d, wo_d, gi, is_last):
        bank_pool[0] = list(range(6))
        F = A.fork()
        yT = F.alloc([128, 8, 1024], BF16, "yT"); ByT = [Buf() for _ in range(8)]
        actT = F.alloc([128, NJ, 1024], BF16, "actT"); Bact = [[Buf(), Buf()] for _ in range(NJ)]
        wout = F.alloc([128, NJ, D], BF16, "wout"); Bwout = [Buf(), Buf()]
        win = [F.alloc([128, 8, 512], BF16, "win%d" % i) for i in range(2)]; Bwin = [[Buf(), Buf()], [Buf(), Buf()]]
        sg = [F.alloc([128, 512], BF16, "sg%d" % i) for i in range(2)]; Bsg = [Buf(), Buf()]
        tmp = [F.alloc([128, 512], F32, "tmp%d" % i) for i in range(2)]; Btmp = [Buf(), Buf()]
        P.dma("sp", gpre_sb[:], gpre_d[:, gi * D:(gi + 1) * D], w=[Bgpre])
        P.dma("sp", gpost_sb[:], gpost_d[:, gi * D:(gi + 1) * D], w=[Bgpost])
        wiv = wi_d.rearrange("(c p) n -> p c n", p=128)
        wov = wo_d.rearrange("(j p) n -> p j n", p=128)
        outs = []
        wout_loaded = False
        for s8 in range(8):
            prenorm_T(s8, yT, slice(s8 * 128, (s8 + 1) * 128), ByT[s8], s8)
        for ps_ in range(2):
            for jp in range(NJ // 2):
                k = ps_ * (NJ // 2) + jp
                w_ = win[k % 2]; Bw = Bwin[k % 2]
                P.dma("pool", w_[:, :, 0:256], wiv[:, :, jp * 256:(jp + 1) * 256], w=[Bw[0]])
                P.dma("pool", w_[:, :, 256:512], wiv[:, :, DFF + jp * 256:DFF + (jp + 1) * 256], w=[Bw[1]])
                if not wout_loaded and jp == 1:
                    P.dma("pool", wout[:, 0:11, :], wov[:, 0:11, :], w=[Bwout[0]])
                    P.dma("pool", wout[:, 11:22, :], wov[:, 11:22, :], w=[Bwout[1]])
                    wout_loaded = True
                for jj in range(2):
                    j = jp * 2 + jj
                    bg = [nb(), nb()]
                    bu = [nb(), nb()]
                    for which, banks, coff in ((0, bg, jj * 128), (1, bu, 256 + jj * 128)):
                        for tt in range(2):
                            for dc in range(8):
                                P.pe(MM(psum[banks[tt]][:], w_[:, dc, coff:coff + 128], yT[:, dc, tt * 512:(tt + 1) * 512],
                                        start=(dc == 0), stop=(dc == 7)),
                                     r=[Bw[which]] + ByT[tt * 4:tt * 4 + 4], w=[Bps[banks[tt]]])
                    for tt in range(2):
                        kk_ = (2 * j + tt) % 2
                        P.act(ACTF(sg[kk_][:], psum[bg[tt]][:], AF.Silu), r=[Bps[bg[tt]]], w=[Bsg[kk_]])
                        P.dve(TT(actT[:, j, tt * 512:(tt + 1) * 512], sg[kk_][:], psum[bu[tt]][:], ALU.mult),
                              r=[Bsg[kk_], Bps[bu[tt]]], w=[Bact[j][tt]])
            for s8 in range(8):
                s = ps_ * 8 + s8
                banks = [nb(), nb()]
                for hh in range(2):
                    for j in range(NJ):
                        P.pe(MM(psum[banks[hh]][:], actT[:, j, s8 * 128:(s8 + 1) * 128], wout[:, j, hh * 512:(hh + 1) * 512],
                                start=(j == 0), stop=(j == NJ - 1)),
                             r=[Bact[j][s8 // 4], Bwout[j // 11]], w=[Bps[banks[hh]]])
                if ps_ == 0:
                    prenorm_T(8 + s8, yT, slice(s8 * 128, (s8 + 1) * 128), ByT[s8], 8 + s8)
                postnorm_res(s, banks, 0.5, s8, tmp, Btmp)
                if is_last:
                    outs.append(P.dma("sp", ov[:, s, :], x_sb[:, s, :], r=[Bx[s]]))
        return outs

    def mixer():
        bank_pool[0] = list(range(6))
        M = A.fork()
        wmi = M.alloc([128, 8, DIN], BF16, "wmi"); Bwmi = [Buf() for _ in range(4)]
        wmo_off = M.off
        wmo = M.alloc([128, 8, D], BF16, "wmo"); Bwmo = Buf()
        ALT = Arena(nc, wmo_off, M.off)
        lb_bc = M.alloc([128, 512], F32, "lb")
        mln_sb = M.alloc([128, 512], F32, "mln")
        small = M.alloc([128, 64], F32, "small")
        hgn_sb = small[:, 0:4]; cw_sb = small[:, 4:20]; cb_sb = small[:, 20:24]; gb_sb = small[:, 24:32]; flag_sb = small[:, 32:33]
        Bpar = Buf()
        S_sb = M.alloc([128, 4, 128], F32, "S"); C_sb = M.alloc([128, 4, 129], F32, "C")
        BS = [Buf() for _ in range(4)]; BC = [Buf() for _ in range(4)]
        qk_pre = M.alloc([128, 4, TTK + 3], F32, "qkpre"); Bqk = [Buf() for _ in range(4)]
        Bgath = Buf()
        hT = M.alloc([128, 8, TTK], BF16, "hT"); BhT = [Buf() for _ in range(NS4)]
        NB = 1
        fF = [M.alloc([128, 512], F32, "fF0"), ALT.alloc([128, 512], F32, "fF1")]; BfF = [Buf(), Buf()]
        lgF = [M.alloc([128, 512], F32, "lgF0"), ALT.alloc([128, 512], F32, "lgF1")]; BlgF = [Buf(), Buf()]
        eF = [M.alloc([128, 512], BF16, "eF%d" % i) for i in range(NB)]; BeF = [Buf() for _ in range(NB)]
        e2F = [M.alloc([128, 512], BF16, "e2F%d" % i) for i in range(NB)]; Be2F = [Buf() for _ in range(NB)]
        qF = [M.alloc([128, 512], BF16, "qF%d" % i) for i in range(NB)]; BqF = [Buf() for _ in range(NB)]
        kdd = [M.alloc([128, 512], BF16, "kdd0"), ALT.alloc([128, 512], BF16, "kdd1")]; Bkdd = [Buf(), Buf()]
        qd, Bqd, kd, Bkd = qF, BqF, e2F, Be2F
        vbf = [M.alloc([128, 512], BF16, "vbf0"), ALT.alloc([128, 512], BF16, "vbf1")]; Bvbf = [Buf(), Buf()]
        goff = M.off
        qdT = M.alloc([128, 4, TTK], BF16, "qdT"); BqdT = [Buf() for _ in range(NS4)]
        kdT = M.alloc([128, 4, TTK], BF16, "kdT"); BkdT = [Buf() for _ in range(NS4)]
        hgT = M.alloc([128, 4, TTK], BF16, "hgT"); BhgT = Buf()
        assert M.off - goff >= NST * 4
        gath = nc.alloc_sbuf_tensor_at("gath_alias", [128, NST], F32, offset=goff)
        eS = [M.alloc([128, 16], F32, "eS0"), ALT.alloc([128, 16], F32, "eS1")]; BeS = [Buf(), Buf()]
        Sb = [[M.alloc([128, 128], BF16, "Sb%d_%d" % (i, c)) for c in range(2)] for i in range(4)]
        BSb = [[Buf(), Buf()] for _ in range(4)]
        scm = [M.alloc([128, 128], BF16, "scm%d" % i) for i in range(4)]; Bscm = [Buf() for _ in range(4)]
        sqF = [M.alloc([128, 512], F32, "sqF%d" % i) for i in range(1)]; BsqF = [Buf()]
        rsF = [M.alloc([128, 512], F32, "rsF%d" % i) for i in range(1)]; BrsF = [Buf()]
        mixT = M.alloc([128, 8, TTK], BF16, "mixT"); BmixT = [Buf() for _ in range(NS4)]
        tmp = [sqF[0], rsF[0]]; Btmp = [BsqF[0], BrsF[0]]
        qT = M.alloc([128, 2, TTK], BF16, "qT"); BqT = Buf()
        qTA = M.alloc([128, 2, TTK], BF16, "qTA"); qTB = M.alloc([128, 2, TTK], BF16, "qTB"); BqTAB = Buf()
        kT = M.alloc([128, 2, TTK], BF16, "kT"); BkT = Buf()
        caccs = [(sqF[0][:, 0:TTK], BsqF[0]), (rsF[0][:, 0:TTK], BrsF[0])]
        mg = [M.alloc([128, 40], F32, "mg0"), ALT.alloc([128, 40], F32, "mg1")]; Bmg = [Buf(), Buf()]
        vp = [M.alloc([128, 4, 129], BF16, "vp0"), ALT.alloc([128, 4, 129], BF16, "vp1")]; Bvp = [Buf(), Buf()]
        kw = [M.alloc([128, 4, 64], BF16, "kw0"), ALT.alloc([128, 4, 64], BF16, "kw1")]; Bkw = [Buf(), Buf()]
        sgm = [M.alloc([128, 512], BF16, "sgm%d" % i) for i in range(NB)]; Bsgm = [Buf() for _ in range(NB)]
        Cb = [M.alloc([128, 4, 129], BF16, "Cb%d" % c) for c in range(2)]; BCb = [[Buf(), Buf()] for _ in range(4)]
        ebL = [M.alloc([128, 8], F32, "ebL0"), ALT.alloc([128, 8], F32, "ebL1")]; BebL = [Buf(), Buf()]
        mlo = [M.alloc([128, 512], BF16, "mlo%d" % i) for i in range(NB)]; Bmlo = [Buf() for _ in range(NB)]
        hjunk = junk[:, 0:128]; Bhjunk = Bjunk
        print("mixer arena end", M.off, "limit", M.limit)

        P.dma("sp", gpre_sb[:], gpre_d[:, D:2 * D], w=[Bgpre])
        P.dma("sp", gpost_sb[:], gpost_d[:, D:2 * D], w=[Bgpost])
        P.dma("sp", fF[0][:], lbl_d[:, 0:512], w=[BfF[0]])
        P.dma("sp", lgF[0][:], lbl_d[:, 512:1024], w=[BlgF[0]])
        P.dma("sp", mln_sb[:], mln_d[:], w=[Bpar])
        P.dma("sp", hgn_sb, hgn_d[:], w=[Bpar]); P.dma("sp", cw_sb, cw_d[:], w=[Bpar]); P.dma("sp", cb_sb, cb_d[:], w=[Bpar])
        P.dma("sp", gb_sb, gb_d[:], w=[Bpar]); P.dma("sp", flag_sb, flag_d[:], w=[Bpar])
        wmiv = wmi_d.rearrange("(c p) n -> p c n", p=128)
        colsplit = [0, 1024, 2048, 3072, DIN]
        Bwchain = Buf("wmichain")
        for i in (2, 0, 1, 3):
            P.dma("pool", wmi[:, :, colsplit[i]:colsplit[i + 1]], wmiv[:, :, colsplit[i]:colsplit[i + 1]], w=[Bwmi[i], Bwchain])

        def wbuf(c0, c1):
            return [Bwmi[i] for i in range(4) if colsplit[i] < c1 and colsplit[i + 1] > c0]

        l0, l1, lm = fF[0], lgF[0], sqF[0]
        Bl0, Bl1, Blm = BfF[0], BlgF[0], BsqF[0]
        P.dve(TT(lm[:], l0[:], l1[:], ALU.max), r=[Bl0, Bl1], w=[Blm])
        P.dve(TT(l0[:], l0[:], lm[:], ALU.subtract), r=[Blm, Bl0], w=[Bl0])
        P.dve(TT(l1[:], l1[:], lm[:], ALU.subtract), r=[Blm, Bl1], w=[Bl1])
        P.act(ACTF(l0[:], l0[:], AF.Exp), r=[Bl0], w=[Bl0])
        P.act(ACTF(l1[:], l1[:], AF.Exp), r=[Bl1], w=[Bl1])
        P.dve(TT(l1[:], l0[:], l1[:], ALU.add), r=[Bl0, Bl1], w=[Bl1])
        P.dve(lambda e: e.reciprocal(out=l1[:], in_=l1[:]), r=[Bl1], w=[Bl1])
        P.dve(TT(lb_bc[:], l0[:], l1[:], ALU.mult), r=[Bl0, Bl1], w=[Bpar])

        def run_pass(pass2):
            if not pass2:
                for h in range(4):
                    P.pool(MS(S_sb[:, h, :], 0.0), w=[BS[h]])
                    P.pool(MS(C_sb[:, h, :], 0.0), w=[BC[h]])
                for g in range(4):
                    P.pool(MS(qk_pre[:, g, :], 0.0), w=[Bqk[g]])
            else:
                for h in range(4):
                    P.dve(TS(S_sb[:, h, :], gath[:, h * 128:(h + 1) * 128], flag_sb, None, ALU.mult), r=[Bgath, Bpar], w=[BS[h]])
                    P.dve(TS(C_sb[:, h, :], gath[:, 512 + h * 129:512 + (h + 1) * 129], flag_sb, None, ALU.mult), r=[Bgath, Bpar], w=[BC[h]])
                for g in range(4):
                    P.dve(TS(qk_pre[:, g, 0:3], gath[:, 1028 + 3 * g:1028 + 3 * g + 3], flag_sb, None, ALU.mult), r=[Bgath, Bpar], w=[Bqk[g]])
            for tt in range(NTOK // TTK):
                run_tile(tt, pass2)

        def proj_tok(s4, c0, n, bank, col0=0):
            for dc in range(8):
                P.pe(MM(psum[bank][:, col0:col0 + n], hT[:, dc, s4 * 128:(s4 + 1) * 128], wmi[:, dc, c0:c0 + n],
                        start=(dc == 0), stop=(dc == 7)),
                     r=[BhT[s4]] + wbuf(c0, c0 + n), w=[Bps[bank]])

        def proj_feat(c0, bank):
            for dc in range(8):
                P.pe(MM(psum[bank][:, 0:TTK], wmi[:, dc, c0:c0 + 128], hT[:, dc, :], start=(dc == 0), stop=(dc == 7)),
                     r=BhT + wbuf(c0, c0 + 128), w=[Bps[bank]])

        def tile_prenorm(tt):
            for s4 in range(NS4):
                s = tt * NS4 + s4
                prenorm_T(s, hT, slice(s4 * 128, (s4 + 1) * 128), BhT[s4], s)

        def run_tile(tt, pass2):
            if tt == 0:
                tile_prenorm(0)
            groups = (0, 1, 2, 3)
            for g in groups:
                if g < 2 and not pass2 and tt != NTOK // TTK - 1:
                    continue
                b = nb()
                proj_feat((C_MQ if g < 2 else C_MK) + (g % 2) * 128, b)
                P.act(lambda e, o=qk_pre[:, g, 3:TTK + 3], i=psum[b][:, 0:TTK]: e.copy(out=o, in_=i), r=[Bps[b]], w=[Bqk[g]])
                if g >= 2 or pass2:
                    cacc, Bcacc = caccs[g % 2]
                    P.dve(TS(cacc, qk_pre[:, g, 0:TTK], cw_sb[:, 4 * g:4 * g + 1], cb_sb[:, g:g + 1], ALU.mult, ALU.add),
                           r=[Bqk[g], Bpar], w=[Bcacc])
                    for j in range(1, 4):
                        P.dve(STT(cacc, qk_pre[:, g, j:j + TTK], cw_sb[:, 4 * g + j:4 * g + j + 1], cacc, ALU.mult, ALU.add),
                               r=[Bqk[g], Bpar, Bcacc], w=[Bcacc])
                    dst, Bd = (qT, BqT) if g < 2 else (kT, BkT)
                    P.act(ACTF(dst[:, g % 2, :], cacc, AF.Sigmoid), r=[Bcacc], w=[Bd])
                    P.dve(TT(dst[:, g % 2, :], cacc, dst[:, g % 2, :], ALU.mult), r=[Bcacc, Bd], w=[Bd])
                P.pool(CP(qk_pre[:, g, 0:3], qk_pre[:, g, TTK:TTK + 3]), r=[Bqk[g]], w=[Bqk[g]])
            if pass2:
                qv = qT[:].rearrange("p g (s c t) -> p g s c t", s=NS4, c=2)
                qav = qTA[:].rearrange("p g (s c t) -> p g s c t", s=NS4, c=2)
                qbv = qTB[:].rearrange("p g (s c t) -> p g s c t", s=NS4, c=2)
                for g in range(2):
                    P.pool(MS(qTA[:, g, :], 0.0), w=[BqTAB])
                    P.pool(MS(qTB[:, g, :], 0.0), w=[BqTAB])
                    P.pool(CP(qav[:, g, :, 0, :], qv[:, g, :, 0, :]), r=[BqT], w=[BqTAB])
                    P.pool(CP(qbv[:, g, :, 1, :], qv[:, g, :, 1, :]), r=[BqT], w=[BqTAB])
                for h in range(4):
                    b = nb()
                    proj_feat(C_HG + h * 128, b)
                    P.act(ACTF(tmp[h % 2][:, 0:TTK], psum[b][:, 0:TTK], AF.Sigmoid), r=[Bps[b]], w=[Btmp[h % 2]])
                    P.dve(STT(hgT[:, h, :], psum[b][:, 0:TTK], hgn_sb[:, h:h + 1], tmp[h % 2][:, 0:TTK], ALU.mult, ALU.mult),
                          r=[Bps[b], Btmp[h % 2], Bpar], w=[BhgT])
            for s4 in range(NS4):
                nxt = (lambda t=tt + 1: tile_prenorm(t)) if (s4 == NS4 - 1 and tt + 1 < NTOK // TTK) else None
                run_subtile(tt, s4, pass2, nxt)

        def run_subtile(tt, s4, pass2, after_prep=None):
            s = tt * NS4 + s4
            k = 0 if pass2 else s % 2
            tok = slice(s4 * 128, (s4 + 1) * 128)
            b_hf = nb(); proj_tok(s4, C_HF, 512, b_hf)
            P.act(ACTF(fF[k][:], psum[b_hf][:], AF.Sigmoid, scale=-1.0), r=[Bps[b_hf]], w=[BfF[k]])
            if pass2:
                b_hq = nb(); proj_tok(s4, C_HQ, 512, b_hq)
                P.act(ACTF(qF[k][:], psum[b_hq][:], AF.Sigmoid), r=[Bps[b_hq]], w=[BqF[k]])
                b_mo = nb(); proj_tok(s4, C_MO, 512, b_mo)
                P.act(ACTF(sgm[k][:], psum[b_mo][:], AF.Sigmoid), r=[Bps[b_mo]], w=[Bsgm[k]])
                P.dve(TT(qF[k][:], psum[b_hq][:], qF[k][:], ALU.mult), r=[Bps[b_hq], BqF[k]], w=[BqF[k]])
            P.begin_defer()
            bank_pool[0] = [0, 1, 2]; tb_pool[0] = [0]
            P.dve(STT(fF[k][:], lb_bc[:], 1.0, fF[k][:], ALU.subtract, ALU.mult), r=[Bpar, BfF[k]], w=[BfF[k]])
            P.act(ACTF(lgF[k][:], fF[k][:], AF.Ln, bias=1.0), r=[BfF[k]], w=[BlgF[k]])
            b_hi = nb(); proj_tok(s4, C_HI, 512, b_hi)
            P.act(lambda e, o=vbf[k][:], i=psum[b_hi][:]: e.copy(out=o, in_=i), r=[Bps[b_hi]], w=[Bvbf[k]])
            b_gl = nb()
            P.pe(MM(psum[b_gl][:], EM, lgF[k][:]), r=[Bcst, BlgF[k]], w=[Bps[b_gl]])
            P.act(ACTF(kdd[k][:], psum[b_gl][:], AF.Exp), r=[Bps[b_gl]], w=[Bkdd[k]])
            P.dve(STT(kdd[k][:], fF[k][:], -1.0, kdd[k][:], ALU.mult, ALU.mult), r=[BfF[k], Bkdd[k]], w=[Bkdd[k]])
            b_sc = nb()
            for h in range(4):
                P.pe(MM(psum[b_sc][:, 4 * h:4 * h + 4], lgF[k][:, h * 128:(h + 1) * 128], SELC), r=[Bcst, BlgF[k]], w=[Bps[b_sc]])
            P.act(ACTF(eS[k][:], psum[b_sc][:, 0:16], AF.Exp), r=[Bps[b_sc]], w=[BeS[k]])
            if pass2:
                b_gm = nb()
                P.pe(MM(psum[b_gm][:], DM, lgF[k][:]), r=[Bcst, BlgF[k]], w=[Bps[b_gm]])
                P.act(ACTF(eF[k][:], psum[b_gm][:], AF.Exp), r=[Bps[b_gm]], w=[BeF[k]])
                P.act(ACTF(e2F[k][:], psum[b_gm][:], AF.Exp, scale=-1.0), r=[Bps[b_gm]], w=[Be2F[k]])
                P.dve(TT(qd[k][:], qF[k][:], eF[k][:], ALU.mult), r=[BqF[k], BeF[k]], w=[Bqd[k]])
                P.dve(STT(kd[k][:], fF[k][:], -1.0, e2F[k][:], ALU.mult, ALU.mult), r=[BfF[k], Be2F[k]], w=[Bkd[k]])
                for src, Bsrc, dstT, BdT in ((qd[k], Bqd[k], qdT, BqdT), (kd[k], Bkd[k], kdT, BkdT)):
                    b = nbt()
                    pv = psum_b[b][:, 0:512].rearrange("p (a c) -> p a c", a=4)
                    for h in range(4):
                        P.pe(TR(pv[:, h, :], src[:, h * 128:(h + 1) * 128], ident_bf[:]), r=[Bsrc, Bident], w=[Bpt[b]])
                    P.dve(CP(dstT[:, :, tok], pv), r=[Bpt[b]], w=[BdT[s4]])
            seq_hg = P.end_defer()
            P.begin_defer()
            bank_pool[0] = [3, 4, 5]; tb_pool[0] = [1]
            g_ = mg[k]
            b_g = nb(); proj_tok(s4, C_MIG, 8, b_g)
            P.dve(TT(g_[:, 0:8], psum[b_g][:, 0:8], gb_sb, ALU.add), r=[Bps[b_g], Bpar], w=[Bmg[k]])
            P.act(ACTF(g_[:, 8:12], g_[:, 4:8], AF.Exp, scale=-1.0), r=[Bmg[k]], w=[Bmg[k]])
            P.act(ACTF(g_[:, 8:12], g_[:, 8:12], AF.Ln, bias=1.0), r=[Bmg[k]], w=[Bmg[k]])
            P.dve(TS(g_[:, 8:12], g_[:, 8:12], -1.0, None, ALU.mult), r=[Bmg[k]], w=[Bmg[k]])
            P.pe(MM(psum[b_g][:, 8:12], TRI, g_[:, 8:12]), r=[Bcst, Bmg[k]], w=[Bps[b_g]])
            P.pe(MM(psum[b_g][:, 12:16], EM, g_[:, 8:12]), r=[Bcst, Bmg[k]], w=[Bps[b_g]])
            P.dve(TT(g_[:, 12:16], psum[b_g][:, 12:16], g_[:, 0:4], ALU.add), r=[Bps[b_g], Bmg[k]], w=[Bmg[k]])
            P.act(ACTF(g_[:, 12:16], g_[:, 12:16], AF.Exp), r=[Bmg[k]], w=[Bmg[k]])
            if pass2:
                P.act(ACTF(g_[:, 16:20], psum[b_g][:, 8:12], AF.Exp), r=[Bps[b_g]], w=[Bmg[k]])
                P.dve(TS(g_[:, 16:20], g_[:, 16:20], 0.125, None, ALU.mult), r=[Bmg[k]], w=[Bmg[k]])
                P.dve(TT(g_[:, 20:24], g_[:, 0:4], psum[b_g][:, 8:12], ALU.subtract), r=[Bps[b_g], Bmg[k]], w=[Bmg[k]])
                P.act(ACTF(g_[:, 20:24], g_[:, 20:24], AF.Exp), r=[Bmg[k]], w=[Bmg[k]])
            P.dve(TS(g_[:, 24:28], g_[:, 8:12], MAB[:, 0:1], None, ALU.mult), r=[Bmg[k], Bcst], w=[Bmg[k]])
            P.dve(TS(g_[:, 28:32], g_[:, 8:12], MAB[:, 1:2], None, ALU.mult), r=[Bmg[k], Bcst], w=[Bmg[k]])
            P.pe(MM(psum[b_g][:, 16:24], ONES, g_[:, 24:32]), r=[Bcst, Bmg[k]], w=[Bps[b_g]])
            P.act(ACTF(ebL[k][:], psum[b_g][:, 16:24], AF.Exp), r=[Bps[b_g]], w=[BebL[k]])
            b_v = nb(); proj_tok(s4, C_MV, 512, b_v)
            P.pool(MS(vp[k][:, :, 128:129], 1.0), w=[Bvp[k]])
            P.act(lambda e, o=vp[k][:, :, 0:128], i=psum[b_v][:].rearrange("p (h v) -> p h v", h=4): e.copy(out=o, in_=i),
                  r=[Bps[b_v]], w=[Bvp[k]])
            for g in range(2):
                b = nbt()
                pv = psum_b[b][:, 0:128]
                P.pe(TR(pv, kT[:, g, tok], ident_bf[:]), r=[BkT, Bident], w=[Bpt[b]])
                for hh in range(2):
                    h = 2 * g + hh
                    P.dve(TS(kw[k][:, h, :], pv[:, hh * 64:(hh + 1) * 64], g_[:, 12 + h:13 + h], None, ALU.mult),
                          r=[Bpt[b], Bmg[k]], w=[Bkw[k]])
            seq_ml = P.end_defer()
            bank_pool[0] = list(range(6)); tb_pool[0] = [0, 1]
            P.replay(seq_hg, seq_ml)
            if after_prep is not None:
                after_prep()
            if pass2:
                b_s = nb()
                for h in range(4):
                    P.pe(MM(psum[b_s][:, h * 128:(h + 1) * 128], kdT[:, h, tok], qdT[:, h, tok]), r=[BkdT[s4], BqdT[s4]], w=[Bps[b_s]])
                for h in range(4):
                    P.dve(TT(scm[h][:], psum[b_s][:, h * 128:(h + 1) * 128], TRI, ALU.mult), r=[Bps[b_s], Bcst], w=[Bscm[h]])
            for c in range(2):
                cs = slice(c * 64, (c + 1) * 64)
                if pass2:
                    for h in range(4):
                        P.act(ACTF(Sb[h][c][:], S_sb[:, h, :], AF.Copy, scale=eS[k][:, 4 * h + c:4 * h + c + 1]),
                              r=[BS[h], BeS[k]], w=[BSb[h][c]])
                b_u = nb()
                for h in range(4):
                    hs = slice(h * 128, (h + 1) * 128)
                    P.pe(MM(psum[b_u][:, hs], kdd[k][cs, hs], vbf[k][cs, hs]), r=[Bkdd[k], Bvbf[k]], w=[Bps[b_u]])
                for h in range(4):
                    hs = slice(h * 128, (h + 1) * 128)
                    P.dve(STT(S_sb[:, h, :], S_sb[:, h, :], eS[k][:, 4 * h + 2 + c:4 * h + 3 + c], psum[b_u][:, hs], ALU.mult, ALU.add),
                          r=[BS[h], BeS[k], Bps[b_u]], w=[BS[h]])
            HP = [((h % 2) * 64, h // 2) for h in range(4)]
            if pass2:
                for h in range(4):
                    hp, g = HP[h]
                    P.act(AC(Cb[0][hp:hp + 64, h, :], C_sb[hp:hp + 64, h, :]), r=[BC[h]], w=[BCb[h][0]])
            for c in range(2):
                cs = slice(c * 64, (c + 1) * 64)
                bu = [nb(), nb()]
                for h in range(4):
                    hp, g = HP[h]
                    P.pe(MM(psum[bu[h // 2]][:, (h % 2) * 129:(h % 2) * 129 + 129],
                            kw[k][cs, g * 2:g * 2 + 2, :].rearrange("p a b -> p (a b)"), vp[k][cs, h, :]),
                         r=[Bkw[k], Bvp[k]], w=[Bps[bu[h // 2]]])
                for h in range(4):
                    hp, g = HP[h]
                    hsl = slice(hp, hp + 64)
                    P.dve(STT(C_sb[hsl, h, :], C_sb[hsl, h, :], ebL[k][hsl, 4 * c + h:4 * c + h + 1],
                              psum[bu[h // 2]][hsl, (h % 2) * 129:(h % 2) * 129 + 129], ALU.mult, ALU.add),
                          r=[BC[h], BebL[k], Bps[bu[h // 2]]], w=[BC[h]])
                if pass2 and c == 0:
                    for h in range(4):
                        hp, g = HP[h]
                        P.act(AC(Cb[1][hp:hp + 64, h, :], C_sb[hp:hp + 64, h, :]), r=[BC[h]], w=[BCb[h][1]])
            if pass2:
                b_o = nb()
                for h in range(4):
                    hs = slice(h * 128, (h + 1) * 128)
                    for c in range(2):
                        cs = slice(c * 64, (c + 1) * 64)
                        ctok = slice(s4 * 128 + c * 64, s4 * 128 + (c + 1) * 64)
                        oo = psum[b_o][:, h * 128 + c * 64:h * 128 + (c + 1) * 64]
                        P.pe(MM(oo, vbf[k][:, hs], scm[h][:, cs], start=True, stop=False), r=[Bvbf[k], Bscm[h]], w=[Bps[b_o]])
                        P.pe(MM(oo, Sb[h][c][:], qdT[:, h, ctok], start=False, stop=True), r=[BSb[h][c], BqdT[s4]], w=[Bps[b_o]])
                b_oo = [nb(), nb()]
                bsx = [nb(), nb()]
                for h in range(4):
                    hp, g = HP[h]
                    P.pe(MM(psum[bsx[h % 2]][:, g * 128:(g + 1) * 128], kT[hp:hp + 64, g, tok], qT[hp:hp + 64, g, tok]),
                         r=[BkT, BqT], w=[Bps[bsx[h % 2]]])
                for h in range(4):
                    hp, g = HP[h]
                    P.dve(STT(scm[h][:], psum[bsx[h % 2]][:, g * 128:(g + 1) * 128], g_[:, 20 + h:21 + h], TRI, ALU.mult, ALU.mult),
                          r=[Bps[bsx[h % 2]], Bmg[k], Bcst], w=[Bscm[h]])
                for h in range(4):
                    hp, g = HP[h]
                    oo = psum[b_oo[h // 2]][:, (h % 2) * 129:(h % 2) * 129 + 129]
                    P.pe(MM(oo, scm[h][:], vp[k][:, h, :], start=True, stop=False), r=[Bscm[h], Bvp[k]], w=[Bps[b_oo[h // 2]]])
                    P.pe(MM(oo, qTA[hp:hp + 64, g, tok], Cb[0][hp:hp + 64, h, :], start=False, stop=False),
                         r=[BqTAB, BCb[h][0]], w=[Bps[b_oo[h // 2]]])
                    P.pe(MM(oo, qTB[hp:hp + 64, g, tok], Cb[1][hp:hp + 64, h, :], start=False, stop=True),
                         r=[BqTAB, BCb[h][1]], w=[Bps[b_oo[h // 2]]])
            if pass2:
                P.begin_defer()
                P.act(ACTF(eF[k][:], psum[b_o][:], AF.Square), r=[Bps[b_o]], w=[BeF[k]])
                b_q = nb()
                P.pe(MM(psum[b_q][:], ones_bf[:], eF[k][:]), r=[Bident, BeF[k]], w=[Bps[b_q]])
                P.dve(TS(rsF[0][:], psum[b_q][:], 1.0 / 128, EPS, ALU.mult, ALU.add), r=[Bps[b_q]], w=[BrsF[0]])
                P.act(ACTF(rsF[0][:], rsF[0][:], AF.Ln), r=[BrsF[0]], w=[BrsF[0]])
                P.act(ACTF(rsF[0][:], rsF[0][:], AF.Exp, scale=-0.5), r=[BrsF[0]], w=[BrsF[0]])
                P.dve(TT(rsF[0][:], psum[b_o][:], rsF[0][:], ALU.mult), r=[Bps[b_o], BrsF[0]], w=[BrsF[0]])
                P.dve(TT(mixT[:, 0:4, tok], rsF[0][:].rearrange("p (h t) -> p h t", h=4), hgT[:, :, tok], ALU.mult),
                      r=[BrsF[0], BhgT], w=[BmixT[s4]])
            if pass2:
                seq_ho = P.end_defer()
                P.begin_defer()
                for h in range(4):
                    ob = psum[b_oo[h // 2]]
                    o0 = (h % 2) * 129
                    P.dve(TT(g_[:, 32 + h:33 + h], ob[:, o0 + 128:o0 + 129], g_[:, 16 + h:17 + h], ALU.mult),
                          r=[Bps[b_oo[h // 2]], Bmg[k]], w=[Bmg[k]])
                P.dve(TS(g_[:, 36:40], g_[:, 32:36], -1.0, None, ALU.mult), r=[Bmg[k]], w=[Bmg[k]])
                P.dve(TT(g_[:, 32:36], g_[:, 32:36], g_[:, 36:40], ALU.max), r=[Bmg[k]], w=[Bmg[k]])
                P.dve(TS(g_[:, 32:36], g_[:, 32:36], 1.0, None, ALU.max), r=[Bmg[k]], w=[Bmg[k]])
                P.dve(lambda e, o=g_[:, 32:36], i=g_[:, 32:36]: e.reciprocal(out=o, in_=i), r=[Bmg[k]], w=[Bmg[k]])
                P.dve(TT(g_[:, 32:36], g_[:, 32:36], g_[:, 16:20], ALU.mult), r=[Bmg[k]], w=[Bmg[k]])
                for h in range(4):
                    ob = psum[b_oo[h // 2]]
                    o0 = (h % 2) * 129
                    P.act(ACTF(hjunk, ob[:, o0:o0 + 128], AF.Square, scale=g_[:, 32 + h:33 + h], accum_out=g_[:, 36 + h:37 + h]),
                          r=[Bps[b_oo[h // 2]], Bmg[k]], w=[Bhjunk, Bmg[k]])
                P.dve(TS(g_[:, 36:40], g_[:, 36:40], 1.0 / 128, EPS, ALU.mult, ALU.add), r=[Bmg[k]], w=[Bmg[k]])
                P.act(ACTF(g_[:, 36:40], g_[:, 36:40], AF.Ln), r=[Bmg[k]], w=[Bmg[k]])
                P.act(ACTF(g_[:, 36:40], g_[:, 36:40], AF.Exp, scale=-0.5), r=[Bmg[k]], w=[Bmg[k]])
                P.dve(TT(g_[:, 36:40], g_[:, 36:40], g_[:, 32:36], ALU.mult), r=[Bmg[k]], w=[Bmg[k]])
                for h in range(4):
                    ob = psum[b_oo[h // 2]]
                    o0 = (h % 2) * 129
                    P.dve(STT(mlo[k][:, h * 128:(h + 1) * 128], ob[:, o0:o0 + 128], g_[:, 36 + h:37 + h], mln_sb[:, h * 128:(h + 1) * 128],
                              ALU.mult, ALU.mult), r=[Bps[b_oo[h // 2]], Bmg[k], Bpar], w=[Bmlo[k]])
                P.dve(TT(mlo[k][:], mlo[k][:], sgm[k][:], ALU.mult), r=[Bsgm[k], Bmlo[k]], w=[Bmlo[k]])
                seq_mo = P.end_defer()
                P.replay(seq_ho, seq_mo)
                b = nbt()
                pv = psum_b[b][:, 0:512].rearrange("p (a c) -> p a c", a=4)
                for h in range(4):
                    P.pe(TR(pv[:, h, :], mlo[k][:, h * 128:(h + 1) * 128], ident_bf[:]), r=[Bmlo[k], Bident], w=[Bpt[b]])
                P.act(lambda e, o=mixT[:, 4:8, tok], i=pv: e.copy(out=o, in_=i), r=[Bpt[b]], w=[BmixT[s4]])
                banks = [nb(), nb()]
                for hh in range(2):
                    for jc in range(8):
                        P.pe(MM(psum[banks[hh]][:], mixT[:, jc, tok], wmo[:, jc, hh * 512:(hh + 1) * 512], start=(jc == 0), stop=(jc == 7)),
                             r=[BmixT[s4], Bwmo], w=[Bps[banks[hh]]])
                postnorm_res(s, banks, 1.0, s4, tmp, Btmp)

        run_pass(False)
        P.barrier(MS(stat[:, 63:64], 0.0))
        P.dma("pool", wmo[:], wmo_d.rearrange("(c p) n -> p c n", p=128), w=[Bwmo])
        Bex = [Buf() for _ in range(12)]
        for h in range(4):
            P.dma("sp", cc_src[:, h * 128:(h + 1) * 128], S_sb[:, h, :], r=[BS[h]], w=[Bex[3 * h]])
            P.dma("sp", cc_src[:, 512 + h * 129:512 + (h + 1) * 129], C_sb[:, h, :], r=[BC[h]], w=[Bex[3 * h + 1]])
            P.dma("sp", cc_src[:, 1028 + 3 * h:1028 + 3 * h + 3], qk_pre[:, h, 0:3], r=[Bqk[h]], w=[Bex[3 * h + 2]])
        Bdst = Buf()
        P.cc(lambda e: e.collective_compute("AllGather", ALU.bypass, replica_groups=[[0, 1], [2, 3], [4, 5], [6, 7]],
                                            ins=[cc_src.opt()], outs=[cc_dst.opt()]), Bex, [Bdst])
        P.dma("sp", gath[:], cc_dst[0:128, :], r=[Bdst], w=[Bgath])
        run_pass(True)

    outs = []
    if stage >= 1:
        o1 = ffn(w1i_d, w1o_d, 0, stage == 1)
        outs = o1
    if stage >= 2:
        P.barrier(MS(stat[:, 63:64], 0.0))
        mixer()
        if stage == 2:
            outs = [P.dma("sp", ov[:, s, :], x_sb[:, s, :], r=[Bx[s]]) for s in range(16)]
    if stage >= 3:
        P.barrier(MS(stat[:, 63:64], 0.0))
        outs = ffn(w2i_d, w2o_d, 2, True)
    P.emit(final_wait_ops=outs)
    print("ops", P.stats)
    return nc


def _consts():
    c = np.zeros((128, NCST), np.float32)
    s = np.arange(128)[:, None]
    t = np.arange(128)[None, :]
    same = (s // 64) == (t // 64)
    c[:, 0:128] = np.eye(128, dtype=np.float32)
    tri = (same & (s <= t)).astype(np.float32)
    sel = (same & ((s % 64) <= 31)).astype(np.float32)
    c[:, 128:256] = tri
    c[:, 256:384] = tri - sel
    c[:, 384:512] = (same & (s > t)).astype(np.float32)
    c[:, 512:640] = 1.0
    sv = np.arange(128)
    c[:, 640] = (sv <= 31)
    c[:, 641] = (sv >= 64) & (sv <= 95)
    c[:, 642] = (sv < 64)
    c[:, 643] = (sv >= 64)
    c[:, 644] = (sv < 64)
    c[:, 645] = (sv >= 64)
    return c


_NC_CACHE = {}


def _prep_inputs(inputs):
    f = lambda a: np.ascontiguousarray(np.asarray(a, dtype=np.float32))
    x = f(inputs["x"])
    rep = lambda v: np.ascontiguousarray(np.broadcast_to(np.asarray(v, np.float32).reshape(1, -1), (128, np.asarray(v).size)))
    common = {
        "w1i": f(inputs["ffn1_w_in"][0]), "w1o": f(inputs["ffn1_w_out"][0]),
        "w2i": f(inputs["ffn2_w_in"][0]), "w2o": f(inputs["ffn2_w_out"][0]),
        "wmi": f(inputs["w_mix_in"][0]), "wmo": f(inputs["w_mix_out"][0]),
        "gpre": rep(inputs["norm_pre"][0]), "gpost": rep(inputs["norm_post"][0]),
        "lbl": rep(inputs["hgrn_lb_logits"]),
        "hgn": f(np.asarray(inputs["hgrn_norm"][0]).reshape(4, 128).T),
        "mln": rep(inputs["mlstm_norm"][0]),
        "cw": f(np.asarray(inputs["conv_w"][0]).reshape(4, 4, 128).transpose(2, 1, 0).reshape(128, 16)),
        "cb": f(np.asarray(inputs["conv_b"][0]).reshape(4, 128).T),
        "gb": rep(inputs["mlstm_gate_bias"][0]),
        "cst": _consts(),
    }
    in_maps = []
    for c in range(8):
        b, half = c // 2, c % 2
        m = dict(common)
        m["x"] = np.ascontiguousarray(x[b, half * NTOK:(half + 1) * NTOK, :])
        m["flag"] = np.full((128, 1), float(half), np.float32)
        in_maps.append(m)
    return in_maps


def kernel(**inputs):
    stage = int(inputs.pop("_stage", 3)) if "_stage" in inputs else 3
    if stage not in _NC_CACHE:
        _NC_CACHE[stage] = build_program(stage)
    nc = _NC_CACHE[stage]
    in_maps = _prep_inputs(inputs)
    res = run_bass_kernel_spmd(nc, in_maps, core_ids=list(range(8)))
    out = np.empty((4, 2 * NTOK, D), np.float32)
    for c in range(8):
        b, half = c // 2, c % 2
        out[b, half * NTOK:(half + 1) * NTOK, :] = res.results[c]["out"]
    return out
```

```python
import contextlib
import numpy as np
import concourse.bass as bass
import concourse.mybir as mybir
from concourse.bass_utils import run_bass_kernel_spmd

F32 = mybir.dt.float32
BF16 = mybir.dt.bfloat16
AF = mybir.ActivationFunctionType
ALU = mybir.AluOpType

ENGS = ("pe", "act", "dve", "pool", "sp")
EPS = 1e-6
NTOK = 2048
D = 1024
DFF = 2816
NJ = DFF // 128
DIN = 3592
C_HQ, C_HF, C_HI, C_HG, C_MQ, C_MK, C_MV, C_MO, C_MIG, C_MFG = 0, 512, 1024, 1536, 2048, 2304, 2560, 3072, 3584, 3588
NST = 4 * 128 + 4 * 129 + 12
NCST = 648
TTK = 256
NS4 = TTK // 128


class Buf:
    __slots__ = ("name", "last_w", "readers")

    def __init__(self, name=""):
        self.name = name
        self.last_w = None
        self.readers = {}


class Op:
    __slots__ = ("idx", "eng", "fn", "dma", "deps", "signal", "count", "waits", "dsem", "dval", "cc")

    def __init__(self, idx, eng, fn, dma):
        self.idx = idx
        self.eng = eng
        self.fn = fn
        self.dma = dma
        self.deps = ()
        self.signal = False
        self.count = 0
        self.waits = []
        self.dsem = None
        self.dval = 0
        self.cc = False


class Prog:
    def __init__(self, nc, n_dma_sems=8):
        self.nc = nc
        self.ops = []
        self.K = n_dma_sems
        self.phase = Buf("phase")
        self._rec = None

    def begin_defer(self):
        self._rec = []

    def end_defer(self):
        r, self._rec = self._rec, None
        return r

    def replay(self, *seqs):
        seqs = [q for q in seqs if q]
        pos = [0] * len(seqs)
        while any(pos[i] < len(q) for i, q in enumerate(seqs)):
            i = min((i for i, q in enumerate(seqs) if pos[i] < len(q)), key=lambda i: pos[i] / len(seqs[i]))
            self.add(*seqs[i][pos[i]])
            pos[i] += 1

    def add(self, eng, fn, reads=(), writes=(), dma=False):
        if self._rec is not None:
            self._rec.append((eng, fn, tuple(reads), tuple(writes), dma))
            return None
        idx = len(self.ops)
        op = Op(idx, eng, fn, dma)
        deps = set()
        if self.phase not in writes:
            reads = list(reads) + [self.phase]
        for b in reads:
            if b.last_w is not None:
                deps.add(b.last_w)
        for b in writes:
            if b.last_w is not None:
                deps.add(b.last_w)
            deps.update(b.readers.values())
        deps.discard(idx)
        op.deps = deps
        for b in reads:
            b.readers[idx if dma else eng] = idx
        for b in writes:
            b.last_w = idx
            b.readers = {}
        self.ops.append(op)
        return op

    def pe(self, fn, r=(), w=()):
        return self.add("pe", fn, r, w)

    def act(self, fn, r=(), w=()):
        return self.add("act", fn, r, w)

    def dve(self, fn, r=(), w=()):
        return self.add("dve", fn, r, w)

    def pool(self, fn, r=(), w=()):
        return self.add("pool", fn, r, w)

    def dma(self, eng, out, in_, r=(), w=(), **kw):
        return self.add(eng, lambda e: e.dma_start(out=out, in_=in_, **kw), r, w, dma=True)

    def barrier(self, fn):
        return self.add("dve", fn, (), [self.phase])

    def cc(self, fn, r=(), w=()):
        op = self.add("pool", fn, r, w, dma=True)
        op.cc = True
        return op

    def emit(self, final_wait_ops=()):
        nc = self.nc
        ops = self.ops
        for op in ops:
            for d in op.deps:
                Dp = ops[d]
                if Dp.eng == "pe" and op.eng == "pe" and not Dp.dma:
                    continue
                Dp.signal = True
        for o in final_wait_ops:
            o.signal = True
        cnt = {e: 0 for e in ENGS}
        dcnt = {e: 0 for e in ENGS}
        for op in ops:
            if op.cc:
                op.dsem = ("c", op.idx)
                op.dval = 1
            elif op.dma:
                n = dcnt[op.eng]
                dcnt[op.eng] += 1
                op.dsem = (op.eng, n % self.K)
                op.dval = 16 * (n // self.K + 1)
            elif op.signal:
                cnt[op.eng] += 1
                op.count = cnt[op.eng]
        waited = {e: {} for e in ENGS}
        for op in ops:
            need = {}
            for d in op.deps:
                Dp = ops[d]
                if Dp.dma:
                    key = ("d",) + Dp.dsem
                    val = Dp.dval
                else:
                    if Dp.eng == "pe" and op.eng == "pe":
                        continue
                    key = ("e", Dp.eng)
                    val = Dp.count
                if need.get(key, 0) < val:
                    need[key] = val
            if op.dma and not op.cc:
                prev = op.dval - 16
                if prev > 0:
                    key = ("d",) + op.dsem
                    if need.get(key, 0) < prev:
                        need[key] = prev
            w = waited[op.eng]
            for key, val in need.items():
                if w.get(key, 0) < val:
                    w[key] = val
                    op.waits.append((key, val))
        fence = []
        w = waited["sp"]
        for o in final_wait_ops:
            if o.dma:
                key = ("d",) + o.dsem
                val = o.dval
            else:
                key = ("e", o.eng)
                val = o.count
            if w.get(key, 0) < val:
                w[key] = val
                fence.append((key, val))
        self.stats = {e: sum(1 for o in ops if o.eng == e) for e in ENGS}
        self.stats["waits"] = sum(len(o.waits) for o in ops)
        self.stats["maxcnt"] = dict(cnt)
        with contextlib.ExitStack() as st:
            sems = {}
            for e in ENGS:
                sems[("e", e)] = st.enter_context(nc.semaphore("s_" + e))
            for e in ENGS:
                for k in range(min(self.K, dcnt[e])):
                    sems[("d", e, k)] = st.enter_context(nc.semaphore("d_%s_%d" % (e, k)))
            for op in ops:
                if op.cc:
                    sems[("d",) + op.dsem] = st.enter_context(nc.semaphore("c_%d" % op.idx))
            block = st.enter_context(nc.Block())

            def run(engobj, ename, extra=()):
                for op in ops:
                    if op.eng != ename:
                        continue
                    for key, val in op.waits:
                        engobj.wait_ge(sems[key], val)
                    if op.fn is None:
                        continue
                    ins = op.fn(engobj)
                    if op.cc:
                        ins.then_inc(sems[("d",) + op.dsem], 1)
                    elif op.dma:
                        ins.then_inc(sems[("d",) + op.dsem], 16)
                    elif op.signal:
                        ins.then_inc(sems[("e", ename)], 1)
                for key, val in extra:
                    engobj.wait_ge(sems[key], val)

            @block.tensor
            def _(e):
                run(e, "pe")

            @block.scalar
            def _(e):
                run(e, "act")

            @block.vector
            def _(e):
                run(e, "dve")

            @block.gpsimd
            def _(e):
                run(e, "pool")

            @block.sync
            def _(e):
                run(e, "sp", fence)


class Arena:
    def __init__(self, nc, base, limit):
        self.nc = nc
        self.off = base
        self.limit = limit
        self.n = 0

    def alloc(self, shape, dt, name=None):
        nb = int(np.prod(shape[1:])) * (4 if dt == F32 else 2)
        nb = (nb + 31) // 32 * 32
        off = self.off
        self.off += nb
        assert self.off <= self.limit, ("SBUF overflow", name, self.off, self.limit)
        self.n += 1
        return self.nc.alloc_sbuf_tensor_at("%s_%d_%d" % (name or "t", off, self.n), list(shape), dt, offset=off)

    def fork(self):
        return Arena(self.nc, self.off, self.limit)


def MM(out, lhsT, rhs, start=True, stop=True):
    return lambda e: e.matmul(out, lhsT=lhsT, rhs=rhs, start=start, stop=stop)


def TR(out, in_, ident):
    return lambda e: e.transpose(out=out, in_=in_, identity=ident)


def ACTF(out, in_, func, **kw):
    return lambda e: e.activation(out=out, in_=in_, func=func, **kw)


def TT(out, a, b, op):
    return lambda e: e.tensor_tensor(out=out, in0=a, in1=b, op=op)


def TS(out, a, s1, s2, op0, op1=None):
    if op1 is None:
        return lambda e: e.tensor_scalar(out=out, in0=a, scalar1=s1, scalar2=None, op0=op0)
    return lambda e: e.tensor_scalar(out=out, in0=a, scalar1=s1, scalar2=s2, op0=op0, op1=op1)


def STT(out, in0, scalar, in1, op0, op1):
    return lambda e: e.scalar_tensor_tensor(out=out, in0=in0, scalar=scalar, in1=in1, op0=op0, op1=op1)


def CP(out, in_):
    return lambda e: e.tensor_copy(out=out, in_=in_)


def AC(out, in_):
    return lambda e: e.copy(out=out, in_=in_)


def MS(out, val):
    return lambda e: e.memset(out, val)


def build_program(stage=3):
    nc = bass.Bass("TRN2", target_bir_lowering=False)
    di = lambda n, s: nc.dram_tensor(n, list(s), F32, kind="ExternalInput").ap()
    x_d = di("x", [NTOK, D])
    w1i_d = di("w1i", [D, 2 * DFF]); w1o_d = di("w1o", [DFF, D])
    w2i_d = di("w2i", [D, 2 * DFF]); w2o_d = di("w2o", [DFF, D])
    wmi_d = di("wmi", [D, DIN]); wmo_d = di("wmo", [D, D])
    gpre_d = di("gpre", [128, 3 * D]); gpost_d = di("gpost", [128, 3 * D])
    lbl_d = di("lbl", [128, 2 * 512]); hgn_d = di("hgn", [128, 4]); mln_d = di("mln", [128, 512])
    cw_d = di("cw", [128, 16]); cb_d = di("cb", [128, 4]); gb_d = di("gb", [128, 8]); flag_d = di("flag", [128, 1])
    cst_d = di("cst", [128, NCST])
    out_d = nc.dram_tensor("out", [NTOK, D], F32, kind="ExternalOutput").ap()
    cc_src = nc.dram_tensor("cc_src", [128, NST], F32, kind="Internal").ap()
    cc_dst = nc.dram_tensor("cc_dst", [256, NST], F32, kind="Internal").ap()

    P = Prog(nc)
    A = Arena(nc, 16512, 229376)
    x_sb = A.alloc([128, 16, D], F32, "x")
    gpre_sb = A.alloc([128, D], F32, "gpre"); gpost_sb = A.alloc([128, D], F32, "gpost")
    cst_sb = A.alloc([128, NCST], F32, "cst")
    ident_bf = A.alloc([128, 128], BF16, "ident")
    ones_bf = A.alloc([128, 128], BF16, "ones_bf")
    junk = A.alloc([128, D], BF16, "junk")
    ybf = [A.alloc([128, D], BF16, "ybf%d" % i) for i in range(2)]
    stat = A.alloc([128, 64], F32, "stat")
    Bx = [Buf("x%d" % s) for s in range(16)]
    Bgpre, Bgpost, Bcst, Bident, Bjunk = Buf(), Buf(), Buf(), Buf(), Buf()
    Bybf = [Buf(), Buf()]
    Bstat = [Buf() for _ in range(16)]
    TRI = cst_sb[:, 128:256]; DM = cst_sb[:, 256:384]; EM = cst_sb[:, 384:512]; ONES = cst_sb[:, 512:640]
    SELC = cst_sb[:, 640:644]; MAB = cst_sb[:, 644:646]
    psum = [nc.alloc_psum_tensor("ps%d" % i, [128, 512], F32) for i in range(6)]
    psum_b = [nc.alloc_psum_tensor("psb%d" % i, [128, 1024], BF16) for i in range(2)]
    Bpt = [Buf("pt0"), Buf("pt1")]
    tb_ctr = [0]
    tb_pool = [[0, 1]]

    def nbt():
        pl = tb_pool[0]
        i = pl[tb_ctr[0] % len(pl)]
        tb_ctr[0] += 1
        return i

    Bps = [Buf("ps%d" % i) for i in range(8)]
    bank_ctr = [0]
    bank_pool = [list(range(6))]

    def nb():
        pl = bank_pool[0]
        i = pl[bank_ctr[0] % len(pl)]
        bank_ctr[0] += 1
        return i

    P.dma("sp", cst_sb[:], cst_d[:], w=[Bcst])
    xv = x_d.rearrange("(s p) d -> p s d", p=128)
    ov = out_d.rearrange("(s p) d -> p s d", p=128)
    for q in range(4):
        P.dma("sp", x_sb[:, 4 * q:4 * q + 4, :], xv[:, 4 * q:4 * q + 4, :], w=Bx[4 * q:4 * q + 4])
    P.dve(CP(ident_bf[:], cst_sb[:, 0:128]), r=[Bcst], w=[Bident])
    P.dve(CP(ones_bf[:], cst_sb[:, 512:640]), r=[Bcst], w=[Bident])

    def prenorm_T(s, dstT, dcols, Bdst, k):
        yb = ybf[k % 2]; By = Bybf[k % 2]
        st_ = stat[:, 4 * s:4 * s + 4]
        P.act(ACTF(junk[:], x_sb[:, s, :], AF.Square, accum_out=st_[:, 0:1]), r=[Bx[s]], w=[Bjunk, Bstat[s]])
        P.dve(TS(st_[:, 1:2], st_[:, 0:1], 1.0 / D, EPS, ALU.mult, ALU.add), r=[Bstat[s]], w=[Bstat[s]])
        P.act(ACTF(st_[:, 2:3], st_[:, 1:2], AF.Ln), r=[Bstat[s]], w=[Bstat[s]])
        P.act(ACTF(st_[:, 2:3], st_[:, 2:3], AF.Exp, scale=-0.5), r=[Bstat[s]], w=[Bstat[s]])
        P.dve(STT(yb[:], x_sb[:, s, :], st_[:, 2:3], gpre_sb[:], ALU.mult, ALU.mult), r=[Bx[s], Bstat[s], Bgpre], w=[By])
        b = nbt()
        pv = psum_b[b][:].rearrange("p (a c) -> p a c", a=8)
        for dc in range(8):
            P.pe(TR(pv[:, dc, :], yb[:, dc * 128:(dc + 1) * 128], ident_bf[:]), r=[By, Bident], w=[Bpt[b]])
        P.act(AC(dstT[:, :, dcols], pv), r=[Bpt[b]], w=[Bdst])

    def prenorm_batch(ss, dstT, Bdsts):
        s0, n = ss[0], len(ss)
        sv = stat[:, 0:64].rearrange("p (s c) -> p s c", c=4)
        Bst = [Bstat[s] for s in ss]
        for s in ss:
            P.act(ACTF(junk[:], x_sb[:, s, :], AF.Square, accum_out=stat[:, 4 * s:4 * s + 1]), r=[Bx[s]], w=[Bjunk, Bstat[s]])
        P.dve(TS(sv[:, s0:s0 + n, 1:2], sv[:, s0:s0 + n, 0:1], 1.0 / D, EPS, ALU.mult, ALU.add), r=Bst, w=Bst)
        P.act(ACTF(sv[:, s0:s0 + n, 2:3], sv[:, s0:s0 + n, 1:2], AF.Ln), r=Bst, w=Bst)
        P.act(ACTF(sv[:, s0:s0 + n, 2:3], sv[:, s0:s0 + n, 2:3], AF.Exp, scale=-0.5), r=Bst, w=Bst)
        for i, s in enumerate(ss):
            yb = ybf[s % 2]; By = Bybf[s % 2]
            P.dve(STT(yb[:], x_sb[:, s, :], stat[:, 4 * s + 2:4 * s + 3], gpre_sb[:], ALU.mult, ALU.mult),
                  r=[Bx[s], Bstat[s], Bgpre], w=[By])
            b = nbt()
            pv = psum_b[b][:].rearrange("p (a c) -> p a c", a=8)
            for dc in range(8):
                P.pe(TR(pv[:, dc, :], yb[:, dc * 128:(dc + 1) * 128], ident_bf[:]), r=[By, Bident], w=[Bpt[b]])
            P.act(AC(dstT[:, :, (s - s0 + ss[0] % 8) * 128:(s - s0 + ss[0] % 8 + 1) * 128], pv), r=[Bpt[b]], w=[Bdsts[i]])

    def postnorm_res(s, banks, gscale, k, tmp, Btmp):
        st_ = stat[:, 4 * s:4 * s + 4]
        for hh in range(2):
            P.act(ACTF(junk[:, 0:512], psum[banks[hh]][:], AF.Square, accum_out=st_[:, hh:hh + 1]),
                  r=[Bps[banks[hh]]], w=[Bjunk, Bstat[s]])
        P.dve(TT(st_[:, 2:3], st_[:, 0:1], st_[:, 1:2], ALU.add), r=[Bstat[s]], w=[Bstat[s]])
        P.dve(TS(st_[:, 2:3], st_[:, 2:3], 1.0 / D, EPS, ALU.mult, ALU.add), r=[Bstat[s]], w=[Bstat[s]])
        P.act(ACTF(st_[:, 3:4], st_[:, 2:3], AF.Ln), r=[Bstat[s]], w=[Bstat[s]])
        P.act(ACTF(st_[:, 3:4], st_[:, 3:4], AF.Exp, scale=-0.5), r=[Bstat[s]], w=[Bstat[s]])
        P.dve(TS(st_[:, 3:4], st_[:, 3:4], gscale, None, ALU.mult), r=[Bstat[s]], w=[Bstat[s]])
        for hh in range(2):
            t = tmp[(2 * k + hh) % len(tmp)]; Bt = Btmp[(2 * k + hh) % len(tmp)]
            P.dve(STT(t[:], psum[banks[hh]][:], st_[:, 3:4], gpost_sb[:, hh * 512:(hh + 1) * 512], ALU.mult, ALU.mult),
                  r=[Bps[banks[hh]], Bstat[s], Bgpost], w=[Bt])
            P.dve(TT(x_sb[:, s, hh * 512:(hh + 1) * 512], x_sb[:, s, hh * 512:(hh + 1) * 512], t[:], ALU.add),
                   r=[Bt, Bx[s]], w=[Bx[s]])

    def ffn(wi_d, wo_d, gi, is_last):
        bank_pool[0] = list(range(6))
        F = A.fork()
        yT = F.alloc([128, 8, 1024], BF16, "yT"); ByT = [Buf() for _ in range(8)]
        actT = F.alloc([128, NJ, 1024], BF16, "actT"); Bact = [[Buf(), Buf()] for _ in range(NJ)]
        wout = F.alloc([128, NJ, D], BF16, "wout"); Bwout = [Buf(), Buf()]
        win = [F.alloc([128, 8, 512], BF16, "win%d" % i) for i in range(2)]; Bwin = [[Buf(), Buf()], [Buf(), Buf()]]
        sg = [F.alloc([128, 512], BF16, "sg%d" % i) for i in range(2)]; Bsg = [Buf(), Buf()]
        tmp = [F.alloc([128, 512], F32, "tmp%d" % i) for i in range(2)]; Btmp = [Buf(), Buf()]
        P.dma("sp", gpre_sb[:], gpre_d[:, gi * D:(gi + 1) * D], w=[Bgpre])
        P.dma("sp", gpost_sb[:], gpost_d[:, gi * D:(gi + 1) * D], w=[Bgpost])
        wiv = wi_d.rearrange("(c p) n -> p c n", p=128)
        wov = wo_d.rearrange("(j p) n -> p j n", p=128)
        outs = []
        wout_loaded = False
        prenorm_batch([0, 1, 2, 3], yT, ByT[0:4])
        prenorm_batch([4, 5, 6, 7], yT, ByT[4:8])
        for ps_ in range(2):
            for jp in range(NJ // 2):
                k = ps_ * (NJ // 2) + jp
                w_ = win[k % 2]; Bw = Bwin[k % 2]
                P.dma("pool", w_[:, :, 0:256], wiv[:, :, jp * 256:(jp + 1) * 256], w=[Bw[0]])
                P.dma("pool", w_[:, :, 256:512], wiv[:, :, DFF + jp * 256:DFF + (jp + 1) * 256], w=[Bw[1]])
                if not wout_loaded and jp == 1:
                    P.dma("pool", wout[:, 0:11, :], wov[:, 0:11, :], w=[Bwout[0]])
                    P.dma("pool", wout[:, 11:22, :], wov[:, 11:22, :], w=[Bwout[1]])
                    wout_loaded = True
                for jj in range(2):
                    j = jp * 2 + jj
                    bg = [nb(), nb()]
                    bu = [nb(), nb()]
                    for which, banks, coff in ((0, bg, jj * 128), (1, bu, 256 + jj * 128)):
                        for tt in range(2):
                            for dc in range(8):
                                P.pe(MM(psum[banks[tt]][:], w_[:, dc, coff:coff + 128], yT[:, dc, tt * 512:(tt + 1) * 512],
                                        start=(dc == 0), stop=(dc == 7)),
                                     r=[Bw[which]] + ByT[tt * 4:tt * 4 + 4], w=[Bps[banks[tt]]])
                    for tt in range(2):
                        kk_ = (2 * j + tt) % 2
                        P.act(ACTF(sg[kk_][:], psum[bg[tt]][:], AF.Silu), r=[Bps[bg[tt]]], w=[Bsg[kk_]])
                        P.dve(TT(actT[:, j, tt * 512:(tt + 1) * 512], sg[kk_][:], psum[bu[tt]][:], ALU.mult),
                              r=[Bsg[kk_], Bps[bu[tt]]], w=[Bact[j][tt]])
            for s8 in range(8):
                s = ps_ * 8 + s8
                banks = [nb(), nb()]
                for hh in range(2):
                    for j in range(NJ):
                        P.pe(MM(psum[banks[hh]][:], actT[:, j, s8 * 128:(s8 + 1) * 128], wout[:, j, hh * 512:(hh + 1) * 512],
                                start=(j == 0), stop=(j == NJ - 1)),
                             r=[Bact[j][s8 // 4], Bwout[j // 11]], w=[Bps[banks[hh]]])
                if ps_ == 0:
                    prenorm_T(8 + s8, yT, slice(s8 * 128, (s8 + 1) * 128), ByT[s8], 8 + s8)
                postnorm_res(s, banks, 0.5, s8, tmp, Btmp)
                if is_last:
                    outs.append(P.dma("sp", ov[:, s, :], x_sb[:, s, :], r=[Bx[s]]))
        return outs

    def mixer():
        bank_pool[0] = list(range(6))
        M = A.fork()
        wmi = M.alloc([128, 8, DIN], BF16, "wmi"); Bwmi = [Buf() for _ in range(4)]
        wmo_off = M.off
        wmo = M.alloc([128, 8, D], BF16, "wmo"); Bwmo = Buf()
        ALT = Arena(nc, wmo_off, M.off)
        lb_bc = M.alloc([128, 512], F32, "lb")
        mln_sb = M.alloc([128, 512], F32, "mln")
        small = M.alloc([128, 64], F32, "small")
        hgn_sb = small[:, 0:4]; cw_sb = small[:, 4:20]; cb_sb = small[:, 20:24]; gb_sb = small[:, 24:32]; flag_sb = small[:, 32:33]
        Bpar = Buf()
        S_sb = M.alloc([128, 4, 128], F32, "S"); C_sb = M.alloc([128, 4, 129], F32, "C")
        BS = [Buf() for _ in range(4)]; BC = [Buf() for _ in range(4)]
        qk_pre = M.alloc([128, 4, TTK + 3], F32, "qkpre"); Bqk = [Buf() for _ in range(4)]
        Bgath = Buf()
        hT = M.alloc([128, 8, TTK], BF16, "hT"); BhT = [Buf() for _ in range(NS4)]
        NB = 1
        fF = [M.alloc([128, 512], F32, "fF0"), ALT.alloc([128, 512], F32, "fF1")]; BfF = [Buf(), Buf()]
        lgF = [M.alloc([128, 512], F32, "lgF0"), ALT.alloc([128, 512], F32, "lgF1")]; BlgF = [Buf(), Buf()]
        eF = [M.alloc([128, 512], BF16, "eF%d" % i) for i in range(NB)]; BeF = [Buf() for _ in range(NB)]
        e2F = [M.alloc([128, 512], BF16, "e2F%d" % i) for i in range(NB)]; Be2F = [Buf() for _ in range(NB)]
        qF = [M.alloc([128, 512], BF16, "qF%d" % i) for i in range(NB)]; BqF = [Buf() for _ in range(NB)]
        kdd = [M.alloc([128, 512], BF16, "kdd0"), ALT.alloc([128, 512], BF16, "kdd1")]; Bkdd = [Buf(), Buf()]
        qd, Bqd, kd, Bkd = qF, BqF, e2F, Be2F
        vbf = [M.alloc([128, 512], BF16, "vbf0"), ALT.alloc([128, 512], BF16, "vbf1")]; Bvbf = [Buf(), Buf()]
        goff = M.off
        qdT = M.alloc([128, 4, TTK], BF16, "qdT"); BqdT = [Buf() for _ in range(NS4)]
        kdT = M.alloc([128, 4, TTK], BF16, "kdT"); BkdT = [Buf() for _ in range(NS4)]
        hgT = M.alloc([128, 4, TTK], BF16, "hgT"); BhgT = Buf()
        assert M.off - goff >= NST * 4
        gath = nc.alloc_sbuf_tensor_at("gath_alias", [128, NST], F32, offset=goff)
        eS = [M.alloc([128, 16], F32, "eS0"), ALT.alloc([128, 16], F32, "eS1")]; BeS = [Buf(), Buf()]
        Sb = [[M.alloc([128, 128], BF16, "Sb%d_%d" % (i, c)) for c in range(2)] for i in range(4)]
        BSb = [[Buf(), Buf()] for _ in range(4)]
        scm = [M.alloc([128, 128], BF16, "scm%d" % i) for i in range(4)]; Bscm = [Buf() for _ in range(4)]
        sqF = [M.alloc([128, 512], F32, "sqF%d" % i) for i in range(1)]; BsqF = [Buf()]
        rsF = [M.alloc([128, 512], F32, "rsF%d" % i) for i in range(1)]; BrsF = [Buf()]
        mixT = M.alloc([128, 8, TTK], BF16, "mixT"); BmixT = [Buf() for _ in range(NS4)]
        tmp = [sqF[0], rsF[0]]; Btmp = [BsqF[0], BrsF[0]]
        qT = M.alloc([128, 2, TTK], BF16, "qT"); BqT = Buf()
        qTA = M.alloc([128, 2, TTK], BF16, "qTA"); qTB = M.alloc([128, 2, TTK], BF16, "qTB"); BqTAB = Buf()
        kT = M.alloc([128, 2, TTK], BF16, "kT"); BkT = Buf()
        caccs = [(sqF[0][:, 0:TTK], BsqF[0]), (rsF[0][:, 0:TTK], BrsF[0])]
        mg = [M.alloc([128, 40], F32, "mg0"), ALT.alloc([128, 40], F32, "mg1")]; Bmg = [Buf(), Buf()]
        vp = [M.alloc([128, 4, 129], BF16, "vp0"), ALT.alloc([128, 4, 129], BF16, "vp1")]; Bvp = [Buf(), Buf()]
        kw = [M.alloc([128, 4, 64], BF16, "kw0"), ALT.alloc([128, 4, 64], BF16, "kw1")]; Bkw = [Buf(), Buf()]
        sgm = [M.alloc([128, 512], BF16, "sgm%d" % i) for i in range(NB)]; Bsgm = [Buf() for _ in range(NB)]
        Cb = [M.alloc([128, 4, 129], BF16, "Cb%d" % c) for c in range(2)]; BCb = [[Buf(), Buf()] for _ in range(4)]
        ebL = [M.alloc([128, 8], F32, "ebL0"), ALT.alloc([128, 8], F32, "ebL1")]; BebL = [Buf(), Buf()]
        mlo = [M.alloc([128, 512], BF16, "mlo%d" % i) for i in range(NB)]; Bmlo = [Buf() for _ in range(NB)]
        hjunk = junk[:, 0:128]; Bhjunk = Bjunk
        print("mixer arena end", M.off, "limit", M.limit)

        P.dma("sp", gpre_sb[:], gpre_d[:, D:2 * D], w=[Bgpre])
        P.dma("sp", gpost_sb[:], gpost_d[:, D:2 * D], w=[Bgpost])
        P.dma("sp", fF[0][:], lbl_d[:, 0:512], w=[BfF[0]])
        P.dma("sp", lgF[0][:], lbl_d[:, 512:1024], w=[BlgF[0]])
        P.dma("sp", mln_sb[:], mln_d[:], w=[Bpar])
        P.dma("sp", hgn_sb, hgn_d[:], w=[Bpar]); P.dma("sp", cw_sb, cw_d[:], w=[Bpar]); P.dma("sp", cb_sb, cb_d[:], w=[Bpar])
        P.dma("sp", gb_sb, gb_d[:], w=[Bpar]); P.dma("sp", flag_sb, flag_d[:], w=[Bpar])
        wmiv = wmi_d.rearrange("(c p) n -> p c n", p=128)
        colsplit = [0, 1024, 2048, 3072, DIN]
        Bwchain = Buf("wmichain")
        for i in (2, 0, 1, 3):
            P.dma("pool", wmi[:, :, colsplit[i]:colsplit[i + 1]], wmiv[:, :, colsplit[i]:colsplit[i + 1]], w=[Bwmi[i], Bwchain])

        def wbuf(c0, c1):
            return [Bwmi[i] for i in range(4) if colsplit[i] < c1 and colsplit[i + 1] > c0]

        l0, l1, lm = fF[0], lgF[0], sqF[0]
        Bl0, Bl1, Blm = BfF[0], BlgF[0], BsqF[0]
        P.dve(TT(lm[:], l0[:], l1[:], ALU.max), r=[Bl0, Bl1], w=[Blm])
        P.dve(TT(l0[:], l0[:], lm[:], ALU.subtract), r=[Blm, Bl0], w=[Bl0])
        P.dve(TT(l1[:], l1[:], lm[:], ALU.subtract), r=[Blm, Bl1], w=[Bl1])
        P.act(ACTF(l0[:], l0[:], AF.Exp), r=[Bl0], w=[Bl0])
        P.act(ACTF(l1[:], l1[:], AF.Exp), r=[Bl1], w=[Bl1])
        P.dve(TT(l1[:], l0[:], l1[:], ALU.add), r=[Bl0, Bl1], w=[Bl1])
        P.dve(lambda e: e.reciprocal(out=l1[:], in_=l1[:]), r=[Bl1], w=[Bl1])
        P.dve(TT(lb_bc[:], l0[:], l1[:], ALU.mult), r=[Bl0, Bl1], w=[Bpar])

        def run_pass(pass2):
            if not pass2:
                for h in range(4):
                    P.pool(MS(S_sb[:, h, :], 0.0), w=[BS[h]])
                    P.pool(MS(C_sb[:, h, :], 0.0), w=[BC[h]])
                for g in range(4):
                    P.pool(MS(qk_pre[:, g, :], 0.0), w=[Bqk[g]])
            else:
                for h in range(4):
                    P.dve(TS(S_sb[:, h, :], gath[:, h * 128:(h + 1) * 128], flag_sb, None, ALU.mult), r=[Bgath, Bpar], w=[BS[h]])
                    P.dve(TS(C_sb[:, h, :], gath[:, 512 + h * 129:512 + (h + 1) * 129], flag_sb, None, ALU.mult), r=[Bgath, Bpar], w=[BC[h]])
                for g in range(4):
                    P.dve(TS(qk_pre[:, g, 0:3], gath[:, 1028 + 3 * g:1028 + 3 * g + 3], flag_sb, None, ALU.mult), r=[Bgath, Bpar], w=[Bqk[g]])
            for tt in range(NTOK // TTK):
                run_tile(tt, pass2)

        def proj_tok(s4, c0, n, bank, col0=0):
            for dc in range(8):
                P.pe(MM(psum[bank][:, col0:col0 + n], hT[:, dc, s4 * 128:(s4 + 1) * 128], wmi[:, dc, c0:c0 + n],
                        start=(dc == 0), stop=(dc == 7)),
                     r=[BhT[s4]] + wbuf(c0, c0 + n), w=[Bps[bank]])

        def proj_feat(c0, bank):
            for dc in range(8):
                P.pe(MM(psum[bank][:, 0:TTK], wmi[:, dc, c0:c0 + 128], hT[:, dc, :], start=(dc == 0), stop=(dc == 7)),
                     r=BhT + wbuf(c0, c0 + 128), w=[Bps[bank]])

        def tile_prenorm(tt):
            for s4 in range(NS4):
                s = tt * NS4 + s4
                prenorm_T(s, hT, slice(s4 * 128, (s4 + 1) * 128), BhT[s4], s)

        def run_tile(tt, pass2):
            if tt == 0:
                tile_prenorm(0)
            groups = (0, 1, 2, 3)
            for g in groups:
                if g < 2 and not pass2 and tt != NTOK // TTK - 1:
                    continue
                b = nb()
                proj_feat((C_MQ if g < 2 else C_MK) + (g % 2) * 128, b)
                P.act(lambda e, o=qk_pre[:, g, 3:TTK + 3], i=psum[b][:, 0:TTK]: e.copy(out=o, in_=i), r=[Bps[b]], w=[Bqk[g]])
                if g >= 2 or pass2:
                    cacc, Bcacc = caccs[g % 2]
                    P.dve(TS(cacc, qk_pre[:, g, 0:TTK], cw_sb[:, 4 * g:4 * g + 1], cb_sb[:, g:g + 1], ALU.mult, ALU.add),
                           r=[Bqk[g], Bpar], w=[Bcacc])
                    for j in range(1, 4):
                        P.dve(STT(cacc, qk_pre[:, g, j:j + TTK], cw_sb[:, 4 * g + j:4 * g + j + 1], cacc, ALU.mult, ALU.add),
                               r=[Bqk[g], Bpar, Bcacc], w=[Bcacc])
                    dst, Bd = (qT, BqT) if g < 2 else (kT, BkT)
                    P.act(ACTF(dst[:, g % 2, :], cacc, AF.Sigmoid), r=[Bcacc], w=[Bd])
                    P.dve(TT(dst[:, g % 2, :], cacc, dst[:, g % 2, :], ALU.mult), r=[Bcacc, Bd], w=[Bd])
                P.pool(CP(qk_pre[:, g, 0:3], qk_pre[:, g, TTK:TTK + 3]), r=[Bqk[g]], w=[Bqk[g]])
            if pass2:
                qv = qT[:].rearrange("p g (s c t) -> p g s c t", s=NS4, c=2)
                qav = qTA[:].rearrange("p g (s c t) -> p g s c t", s=NS4, c=2)
                qbv = qTB[:].rearrange("p g (s c t) -> p g s c t", s=NS4, c=2)
                for g in range(2):
                    P.pool(MS(qTA[:, g, :], 0.0), w=[BqTAB])
                    P.pool(MS(qTB[:, g, :], 0.0), w=[BqTAB])
                    P.pool(CP(qav[:, g, :, 0, :], qv[:, g, :, 0, :]), r=[BqT], w=[BqTAB])
                    P.pool(CP(qbv[:, g, :, 1, :], qv[:, g, :, 1, :]), r=[BqT], w=[BqTAB])
                for h in range(4):
                    b = nb()
                    proj_feat(C_HG + h * 128, b)
                    P.act(ACTF(tmp[h % 2][:, 0:TTK], psum[b][:, 0:TTK], AF.Sigmoid), r=[Bps[b]], w=[Btmp[h % 2]])
                    P.dve(STT(hgT[:, h, :], psum[b][:, 0:TTK], hgn_sb[:, h:h + 1], tmp[h % 2][:, 0:TTK], ALU.mult, ALU.mult),
                          r=[Bps[b], Btmp[h % 2], Bpar], w=[BhgT])
            for s4 in range(NS4):
                nxt = (lambda t=tt + 1: tile_prenorm(t)) if (s4 == NS4 - 1 and tt + 1 < NTOK // TTK) else None
                run_subtile(tt, s4, pass2, nxt)

        def run_subtile(tt, s4, pass2, after_prep=None):
            s = tt * NS4 + s4
            k = 0 if pass2 else s % 2
            tok = slice(s4 * 128, (s4 + 1) * 128)
            b_hf = nb(); proj_tok(s4, C_HF, 512, b_hf)
            P.act(ACTF(fF[k][:], psum[b_hf][:], AF.Sigmoid, scale=-1.0), r=[Bps[b_hf]], w=[BfF[k]])
            if pass2:
                b_hq = nb(); proj_tok(s4, C_HQ, 512, b_hq)
                P.act(ACTF(qF[k][:], psum[b_hq][:], AF.Sigmoid), r=[Bps[b_hq]], w=[BqF[k]])
                b_mo = nb(); proj_tok(s4, C_MO, 512, b_mo)
                P.act(ACTF(sgm[k][:], psum[b_mo][:], AF.Sigmoid), r=[Bps[b_mo]], w=[Bsgm[k]])
                P.dve(TT(qF[k][:], psum[b_hq][:], qF[k][:], ALU.mult), r=[Bps[b_hq], BqF[k]], w=[BqF[k]])
            P.begin_defer()
            bank_pool[0] = [0, 1, 2]; tb_pool[0] = [0]
            P.dve(STT(fF[k][:], lb_bc[:], 1.0, fF[k][:], ALU.subtract, ALU.mult), r=[Bpar, BfF[k]], w=[BfF[k]])
            P.act(ACTF(lgF[k][:], fF[k][:], AF.Ln, bias=1.0), r=[BfF[k]], w=[BlgF[k]])
            b_hi = nb(); proj_tok(s4, C_HI, 512, b_hi)
            P.act(lambda e, o=vbf[k][:], i=psum[b_hi][:]: e.copy(out=o, in_=i), r=[Bps[b_hi]], w=[Bvbf[k]])
            b_gl = nb()
            P.pe(MM(psum[b_gl][:], EM, lgF[k][:]), r=[Bcst, BlgF[k]], w=[Bps[b_gl]])
            P.act(ACTF(kdd[k][:], psum[b_gl][:], AF.Exp), r=[Bps[b_gl]], w=[Bkdd[k]])
            P.dve(STT(kdd[k][:], fF[k][:], -1.0, kdd[k][:], ALU.mult, ALU.mult), r=[BfF[k], Bkdd[k]], w=[Bkdd[k]])
            b_sc = nb()
            for h in range(4):
                P.pe(MM(psum[b_sc][:, 4 * h:4 * h + 4], lgF[k][:, h * 128:(h + 1) * 128], SELC), r=[Bcst, BlgF[k]], w=[Bps[b_sc]])
            P.act(ACTF(eS[k][:], psum[b_sc][:, 0:16], AF.Exp), r=[Bps[b_sc]], w=[BeS[k]])
            if pass2:
                b_gm = nb()
                P.pe(MM(psum[b_gm][:], DM, lgF[k][:]), r=[Bcst, BlgF[k]], w=[Bps[b_gm]])
                P.act(ACTF(eF[k][:], psum[b_gm][:], AF.Exp), r=[Bps[b_gm]], w=[BeF[k]])
                P.act(ACTF(e2F[k][:], psum[b_gm][:], AF.Exp, scale=-1.0), r=[Bps[b_gm]], w=[Be2F[k]])
                P.dve(TT(qd[k][:], qF[k][:], eF[k][:], ALU.mult), r=[BqF[k], BeF[k]], w=[Bqd[k]])
                P.dve(STT(kd[k][:], fF[k][:], -1.0, e2F[k][:], ALU.mult, ALU.mult), r=[BfF[k], Be2F[k]], w=[Bkd[k]])
                for src, Bsrc, dstT, BdT in ((qd[k], Bqd[k], qdT, BqdT), (kd[k], Bkd[k], kdT, BkdT)):
                    b = nbt()
                    pv = psum_b[b][:, 0:512].rearrange("p (a c) -> p a c", a=4)
                    for h in range(4):
                        P.pe(TR(pv[:, h, :], src[:, h * 128:(h + 1) * 128], ident_bf[:]), r=[Bsrc, Bident], w=[Bpt[b]])
                    P.dve(CP(dstT[:, :, tok], pv), r=[Bpt[b]], w=[BdT[s4]])
            seq_hg = P.end_defer()
            P.begin_defer()
            bank_pool[0] = [3, 4, 5]; tb_pool[0] = [1]
            g_ = mg[k]
            b_g = nb(); proj_tok(s4, C_MIG, 8, b_g)
            P.dve(TT(g_[:, 0:8], psum[b_g][:, 0:8], gb_sb, ALU.add), r=[Bps[b_g], Bpar], w=[Bmg[k]])
            P.act(ACTF(g_[:, 8:12], g_[:, 4:8], AF.Exp, scale=-1.0), r=[Bmg[k]], w=[Bmg[k]])
            P.act(ACTF(g_[:, 8:12], g_[:, 8:12], AF.Ln, bias=1.0), r=[Bmg[k]], w=[Bmg[k]])
            P.dve(TS(g_[:, 8:12], g_[:, 8:12], -1.0, None, ALU.mult), r=[Bmg[k]], w=[Bmg[k]])
            P.pe(MM(psum[b_g][:, 8:12], TRI, g_[:, 8:12]), r=[Bcst, Bmg[k]], w=[Bps[b_g]])
            P.pe(MM(psum[b_g][:, 12:16], EM, g_[:, 8:12]), r=[Bcst, Bmg[k]], w=[Bps[b_g]])
            P.dve(TT(g_[:, 12:16], psum[b_g][:, 12:16], g_[:, 0:4], ALU.add), r=[Bps[b_g], Bmg[k]], w=[Bmg[k]])
            P.act(ACTF(g_[:, 12:16], g_[:, 12:16], AF.Exp), r=[Bmg[k]], w=[Bmg[k]])
            if pass2:
                P.act(ACTF(g_[:, 16:20], psum[b_g][:, 8:12], AF.Exp), r=[Bps[b_g]], w=[Bmg[k]])
                P.dve(TS(g_[:, 16:20], g_[:, 16:20], 0.125, None, ALU.mult), r=[Bmg[k]], w=[Bmg[k]])
                P.dve(TT(g_[:, 20:24], g_[:, 0:4], psum[b_g][:, 8:12], ALU.subtract), r=[Bps[b_g], Bmg[k]], w=[Bmg[k]])
                P.act(ACTF(g_[:, 20:24], g_[:, 20:24], AF.Exp), r=[Bmg[k]], w=[Bmg[k]])
            P.dve(TS(g_[:, 24:28], g_[:, 8:12], MAB[:, 0:1], None, ALU.mult), r=[Bmg[k], Bcst], w=[Bmg[k]])
            P.dve(TS(g_[:, 28:32], g_[:, 8:12], MAB[:, 1:2], None, ALU.mult), r=[Bmg[k], Bcst], w=[Bmg[k]])
            P.pe(MM(psum[b_g][:, 16:24], ONES, g_[:, 24:32]), r=[Bcst, Bmg[k]], w=[Bps[b_g]])
            P.act(ACTF(ebL[k][:], psum[b_g][:, 16:24], AF.Exp), r=[Bps[b_g]], w=[BebL[k]])
            b_v = nb(); proj_tok(s4, C_MV, 512, b_v)
            P.pool(MS(vp[k][:, :, 128:129], 1.0), w=[Bvp[k]])
            P.act(lambda e, o=vp[k][:, :, 0:128], i=psum[b_v][:].rearrange("p (h v) -> p h v", h=4): e.copy(out=o, in_=i),
                  r=[Bps[b_v]], w=[Bvp[k]])
            for g in range(2):
                b = nbt()
                pv = psum_b[b][:, 0:128]
                P.pe(TR(pv, kT[:, g, tok], ident_bf[:]), r=[BkT, Bident], w=[Bpt[b]])
                for hh in range(2):
                    h = 2 * g + hh
                    P.dve(TS(kw[k][:, h, :], pv[:, hh * 64:(hh + 1) * 64], g_[:, 12 + h:13 + h], None, ALU.mult),
                          r=[Bpt[b], Bmg[k]], w=[Bkw[k]])
            seq_ml = P.end_defer()
            bank_pool[0] = list(range(6)); tb_pool[0] = [0, 1]
            P.replay(seq_hg, seq_ml)
            if after_prep is not None:
                after_prep()
            if pass2:
                b_s = nb()
                for h in range(4):
                    P.pe(MM(psum[b_s][:, h * 128:(h + 1) * 128], kdT[:, h, tok], qdT[:, h, tok]), r=[BkdT[s4], BqdT[s4]], w=[Bps[b_s]])
                for h in range(4):
                    P.dve(TT(scm[h][:], psum[b_s][:, h * 128:(h + 1) * 128], TRI, ALU.mult), r=[Bps[b_s], Bcst], w=[Bscm[h]])
            for c in range(2):
                cs = slice(c * 64, (c + 1) * 64)
                if pass2:
                    for h in range(4):
                        P.act(ACTF(Sb[h][c][:], S_sb[:, h, :], AF.Copy, scale=eS[k][:, 4 * h + c:4 * h + c + 1]),
                              r=[BS[h], BeS[k]], w=[BSb[h][c]])
                b_u = nb()
                for h in range(4):
                    hs = slice(h * 128, (h + 1) * 128)
                    P.pe(MM(psum[b_u][:, hs], kdd[k][cs, hs], vbf[k][cs, hs]), r=[Bkdd[k], Bvbf[k]], w=[Bps[b_u]])
                for h in range(4):
                    hs = slice(h * 128, (h + 1) * 128)
                    P.dve(STT(S_sb[:, h, :], S_sb[:, h, :], eS[k][:, 4 * h + 2 + c:4 * h + 3 + c], psum[b_u][:, hs], ALU.mult, ALU.add),
                          r=[BS[h], BeS[k], Bps[b_u]], w=[BS[h]])
            HP = [((h % 2) * 64, h // 2) for h in range(4)]
            if pass2:
                for h in range(4):
                    hp, g = HP[h]
                    P.act(AC(Cb[0][hp:hp + 64, h, :], C_sb[hp:hp + 64, h, :]), r=[BC[h]], w=[BCb[h][0]])
            for c in range(2):
                cs = slice(c * 64, (c + 1) * 64)
                bu = [nb(), nb()]
                for h in range(4):
                    hp, g = HP[h]
                    P.pe(MM(psum[bu[h // 2]][:, (h % 2) * 129:(h % 2) * 129 + 129],
                            kw[k][cs, g * 2:g * 2 + 2, :].rearrange("p a b -> p (a b)"), vp[k][cs, h, :]),
                         r=[Bkw[k], Bvp[k]], w=[Bps[bu[h // 2]]])
                for h in range(4):
                    hp, g = HP[h]
                    hsl = slice(hp, hp + 64)
                    P.dve(STT(C_sb[hsl, h, :], C_sb[hsl, h, :], ebL[k][hsl, 4 * c + h:4 * c + h + 1],
                              psum[bu[h // 2]][hsl, (h % 2) * 129:(h % 2) * 129 + 129], ALU.mult, ALU.add),
                          r=[BC[h], BebL[k], Bps[bu[h // 2]]], w=[BC[h]])
                if pass2 and c == 0:
                    for h in range(4):
                        hp, g = HP[h]
                        P.act(AC(Cb[1][hp:hp + 64, h, :], C_sb[hp:hp + 64, h, :]), r=[BC[h]], w=[BCb[h][1]])
            if pass2:
                b_o = nb()
                for h in range(4):
                    hs = slice(h * 128, (h + 1) * 128)
                    for c in range(2):
                        cs = slice(c * 64, (c + 1) * 64)
                        ctok = slice(s4 * 128 + c * 64, s4 * 128 + (c + 1) * 64)
                        oo = psum[b_o][:, h * 128 + c * 64:h * 128 + (c + 1) * 64]
                        P.pe(MM(oo, vbf[k][:, hs], scm[h][:, cs], start=True, stop=False), r=[Bvbf[k], Bscm[h]], w=[Bps[b_o]])
                        P.pe(MM(oo, Sb[h][c][:], qdT[:, h, ctok], start=False, stop=True), r=[BSb[h][c], BqdT[s4]], w=[Bps[b_o]])
                b_oo = [nb(), nb()]
                bsx = [nb(), nb()]
                for h in range(4):
                    hp, g = HP[h]
                    P.pe(MM(psum[bsx[h % 2]][:, g * 128:(g + 1) * 128], kT[hp:hp + 64, g, tok], qT[hp:hp + 64, g, tok]),
                         r=[BkT, BqT], w=[Bps[bsx[h % 2]]])
                for h in range(4):
                    hp, g = HP[h]
                    P.dve(STT(scm[h][:], psum[bsx[h % 2]][:, g * 128:(g + 1) * 128], g_[:, 20 + h:21 + h], TRI, ALU.mult, ALU.mult),
                          r=[Bps[bsx[h % 2]], Bmg[k], Bcst], w=[Bscm[h]])
                for h in range(4):
                    hp, g = HP[h]
                    oo = psum[b_oo[h // 2]][:, (h % 2) * 129:(h % 2) * 129 + 129]
                    P.pe(MM(oo, scm[h][:], vp[k][:, h, :], start=True, stop=False), r=[Bscm[h], Bvp[k]], w=[Bps[b_oo[h // 2]]])
                    P.pe(MM(oo, qTA[hp:hp + 64, g, tok], Cb[0][hp:hp + 64, h, :], start=False, stop=False),
                         r=[BqTAB, BCb[h][0]], w=[Bps[b_oo[h // 2]]])
                    P.pe(MM(oo, qTB[hp:hp + 64, g, tok], Cb[1][hp:hp + 64, h, :], start=False, stop=True),
                         r=[BqTAB, BCb[h][1]], w=[Bps[b_oo[h // 2]]])
            if pass2:
                P.begin_defer()
                P.act(ACTF(eF[k][:], psum[b_o][:], AF.Square), r=[Bps[b_o]], w=[BeF[k]])
                b_q = nb()
                P.pe(MM(psum[b_q][:], ones_bf[:], eF[k][:]), r=[Bident, BeF[k]], w=[Bps[b_q]])
                P.dve(TS(rsF[0][:], psum[b_q][:], 1.0 / 128, EPS, ALU.mult, ALU.add), r=[Bps[b_q]], w=[BrsF[0]])
                P.act(ACTF(rsF[0][:], rsF[0][:], AF.Ln), r=[BrsF[0]], w=[BrsF[0]])
                P.act(ACTF(rsF[0][:], rsF[0][:], AF.Exp, scale=-0.5), r=[BrsF[0]], w=[BrsF[0]])
                P.dve(TT(rsF[0][:], psum[b_o][:], rsF[0][:], ALU.mult), r=[Bps[b_o], BrsF[0]], w=[BrsF[0]])
                P.dve(TT(mixT[:, 0:4, tok], rsF[0][:].rearrange("p (h t) -> p h t", h=4), hgT[:, :, tok], ALU.mult),
                      r=[BrsF[0], BhgT], w=[BmixT[s4]])
            if pass2:
                seq_ho = P.end_defer()
                P.begin_defer()
                for h in range(4):
                    ob = psum[b_oo[h // 2]]
                    o0 = (h % 2) * 129
                    P.dve(TT(g_[:, 32 + h:33 + h], ob[:, o0 + 128:o0 + 129], g_[:, 16 + h:17 + h], ALU.mult),
                          r=[Bps[b_oo[h // 2]], Bmg[k]], w=[Bmg[k]])
                P.dve(TS(g_[:, 36:40], g_[:, 32:36], -1.0, None, ALU.mult), r=[Bmg[k]], w=[Bmg[k]])
                P.dve(TT(g_[:, 32:36], g_[:, 32:36], g_[:, 36:40], ALU.max), r=[Bmg[k]], w=[Bmg[k]])
                P.dve(TS(g_[:, 32:36], g_[:, 32:36], 1.0, None, ALU.max), r=[Bmg[k]], w=[Bmg[k]])
                P.dve(lambda e, o=g_[:, 32:36], i=g_[:, 32:36]: e.reciprocal(out=o, in_=i), r=[Bmg[k]], w=[Bmg[k]])
                P.dve(TT(g_[:, 32:36], g_[:, 32:36], g_[:, 16:20], ALU.mult), r=[Bmg[k]], w=[Bmg[k]])
                for h in range(4):
                    ob = psum[b_oo[h // 2]]
                    o0 = (h % 2) * 129
                    P.act(ACTF(hjunk, ob[:, o0:o0 + 128], AF.Square, scale=g_[:, 32 + h:33 + h], accum_out=g_[:, 36 + h:37 + h]),
                          r=[Bps[b_oo[h // 2]], Bmg[k]], w=[Bhjunk, Bmg[k]])
                P.dve(TS(g_[:, 36:40], g_[:, 36:40], 1.0 / 128, EPS, ALU.mult, ALU.add), r=[Bmg[k]], w=[Bmg[k]])
                P.act(ACTF(g_[:, 36:40], g_[:, 36:40], AF.Ln), r=[Bmg[k]], w=[Bmg[k]])
                P.act(ACTF(g_[:, 36:40], g_[:, 36:40], AF.Exp, scale=-0.5), r=[Bmg[k]], w=[Bmg[k]])
                P.dve(TT(g_[:, 36:40], g_[:, 36:40], g_[:, 32:36], ALU.mult), r=[Bmg[k]], w=[Bmg[k]])
                for h in range(4):
                    ob = psum[b_oo[h // 2]]
                    o0 = (h % 2) * 129
                    P.dve(STT(mlo[k][:, h * 128:(h + 1) * 128], ob[:, o0:o0 + 128], g_[:, 36 + h:37 + h], mln_sb[:, h * 128:(h + 1) * 128],
                              ALU.mult, ALU.mult), r=[Bps[b_oo[h // 2]], Bmg[k], Bpar], w=[Bmlo[k]])
                P.dve(TT(mlo[k][:], mlo[k][:], sgm[k][:], ALU.mult), r=[Bsgm[k], Bmlo[k]], w=[Bmlo[k]])
                seq_mo = P.end_defer()
                P.replay(seq_ho, seq_mo)
                b = nbt()
                pv = psum_b[b][:, 0:512].rearrange("p (a c) -> p a c", a=4)
                for h in range(4):
                    P.pe(TR(pv[:, h, :], mlo[k][:, h * 128:(h + 1) * 128], ident_bf[:]), r=[Bmlo[k], Bident], w=[Bpt[b]])
                P.act(lambda e, o=mixT[:, 4:8, tok], i=pv: e.copy(out=o, in_=i), r=[Bpt[b]], w=[BmixT[s4]])
                banks = [nb(), nb()]
                for hh in range(2):
                    for jc in range(8):
                        P.pe(MM(psum[banks[hh]][:], mixT[:, jc, tok], wmo[:, jc, hh * 512:(hh + 1) * 512], start=(jc == 0), stop=(jc == 7)),
                             r=[BmixT[s4], Bwmo], w=[Bps[banks[hh]]])
                postnorm_res(s, banks, 1.0, s4, tmp, Btmp)

        run_pass(False)
        P.barrier(MS(stat[:, 63:64], 0.0))
        P.dma("pool", wmo[:], wmo_d.rearrange("(c p) n -> p c n", p=128), w=[Bwmo])
        Bex = [Buf() for _ in range(12)]
        for h in range(4):
            P.dma("sp", cc_src[:, h * 128:(h + 1) * 128], S_sb[:, h, :], r=[BS[h]], w=[Bex[3 * h]])
            P.dma("sp", cc_src[:, 512 + h * 129:512 + (h + 1) * 129], C_sb[:, h, :], r=[BC[h]], w=[Bex[3 * h + 1]])
            P.dma("sp", cc_src[:, 1028 + 3 * h:1028 + 3 * h + 3], qk_pre[:, h, 0:3], r=[Bqk[h]], w=[Bex[3 * h + 2]])
        Bdst = Buf()
        P.cc(lambda e: e.collective_compute("AllGather", ALU.bypass, replica_groups=[[0, 1], [2, 3], [4, 5], [6, 7]],
                                            ins=[cc_src.opt()], outs=[cc_dst.opt()]), Bex, [Bdst])
        P.dma("sp", gath[:], cc_dst[0:128, :], r=[Bdst], w=[Bgath])
        run_pass(True)

    outs = []
    if stage >= 1:
        o1 = ffn(w1i_d, w1o_d, 0, stage == 1)
        outs = o1
    if stage >= 2:
        P.barrier(MS(stat[:, 63:64], 0.0))
        mixer()
        if stage == 2:
            outs = [P.dma("sp", ov[:, s, :], x_sb[:, s, :], r=[Bx[s]]) for s in range(16)]
    if stage >= 3:
        P.barrier(MS(stat[:, 63:64], 0.0))
        outs = ffn(w2i_d, w2o_d, 2, True)
    P.emit(final_wait_ops=outs)
    print("ops", P.stats)
    return nc


def _consts():
    c = np.zeros((128, NCST), np.float32)
    s = np.arange(128)[:, None]
    t = np.arange(128)[None, :]
    same = (s // 64) == (t // 64)
    c[:, 0:128] = np.eye(128, dtype=np.float32)
    tri = (same & (s <= t)).astype(np.float32)
    sel = (same & ((s % 64) <= 31)).astype(np.float32)
    c[:, 128:256] = tri
    c[:, 256:384] = tri - sel
    c[:, 384:512] = (same & (s > t)).astype(np.float32)
    c[:, 512:640] = 1.0
    sv = np.arange(128)
    c[:, 640] = (sv <= 31)
    c[:, 641] = (sv >= 64) & (sv <= 95)
    c[:, 642] = (sv < 64)
    c[:, 643] = (sv >= 64)
    c[:, 644] = (sv < 64)
    c[:, 645] = (sv >= 64)
    return c


_NC_CACHE = {}


def _prep_inputs(inputs):
    f = lambda a: np.ascontiguousarray(np.asarray(a, dtype=np.float32))
    x = f(inputs["x"])
    rep = lambda v: np.ascontiguousarray(np.broadcast_to(np.asarray(v, np.float32).reshape(1, -1), (128, np.asarray(v).size)))
    common = {
        "w1i": f(inputs["ffn1_w_in"][0]), "w1o": f(inputs["ffn1_w_out"][0]),
        "w2i": f(inputs["ffn2_w_in"][0]), "w2o": f(inputs["ffn2_w_out"][0]),
        "wmi": f(inputs["w_mix_in"][0]), "wmo": f(inputs["w_mix_out"][0]),
        "gpre": rep(inputs["norm_pre"][0]), "gpost": rep(inputs["norm_post"][0]),
        "lbl": rep(inputs["hgrn_lb_logits"]),
        "hgn": f(np.asarray(inputs["hgrn_norm"][0]).reshape(4, 128).T),
        "mln": rep(inputs["mlstm_norm"][0]),
        "cw": f(np.asarray(inputs["conv_w"][0]).reshape(4, 4, 128).transpose(2, 1, 0).reshape(128, 16)),
        "cb": f(np.asarray(inputs["conv_b"][0]).reshape(4, 128).T),
        "gb": rep(inputs["mlstm_gate_bias"][0]),
        "cst": _consts(),
    }
    in_maps = []
    for c in range(8):
        b, half = c // 2, c % 2
        m = dict(common)
        m["x"] = np.ascontiguousarray(x[b, half * NTOK:(half + 1) * NTOK, :])
        m["flag"] = np.full((128, 1), float(half), np.float32)
        in_maps.append(m)
    return in_maps


def kernel(**inputs):
    stage = int(inputs.pop("_stage", 3)) if "_stage" in inputs else 3
    if stage not in _NC_CACHE:
        _NC_CACHE[stage] = build_program(stage)
    nc = _NC_CACHE[stage]
    in_maps = _prep_inputs(inputs)
    res = run_bass_kernel_spmd(nc, in_maps, core_ids=list(range(8)))
    out = np.empty((4, 2 * NTOK, D), np.float32)
    for c in range(8):
        b, half = c // 2, c % 2
        out[b, half * NTOK:(half + 1) * NTOK, :] = res.results[c]["out"]
    return out
```

```python
import contextlib
import numpy as np
import concourse.bass as bass
import concourse.mybir as mybir
from concourse.bass_utils import run_bass_kernel_spmd

F32 = mybir.dt.float32
BF16 = mybir.dt.bfloat16
AF = mybir.ActivationFunctionType
ALU = mybir.AluOpType

ENGS = ("pe", "act", "dve", "pool", "sp")
EPS = 1e-6
NTOK = 2048
D = 1024
DFF = 2816
NJ = DFF // 128
DIN = 3592
C_HQ, C_HF, C_HI, C_HG, C_MQ, C_MK, C_MV, C_MO, C_MIG, C_MFG = 0, 512, 1024, 1536, 2048, 2304, 2560, 3072, 3584, 3588
NST = 4 * 128 + 4 * 129 + 12
NCST = 648
TTK = 256
NS4 = TTK // 128


class Buf:
    __slots__ = ("name", "last_w", "readers")

    def __init__(self, name=""):
        self.name = name
        self.last_w = None
        self.readers = {}


class Op:
    __slots__ = ("idx", "eng", "fn", "dma", "deps", "signal", "count", "waits", "dsem", "dval", "cc")

    def __init__(self, idx, eng, fn, dma):
        self.idx = idx
        self.eng = eng
        self.fn = fn
        self.dma = dma
        self.deps = ()
        self.signal = False
        self.count = 0
        self.waits = []
        self.dsem = None
        self.dval = 0
        self.cc = False


class Prog:
    def __init__(self, nc, n_dma_sems=8):
        self.nc = nc
        self.ops = []
        self.K = n_dma_sems
        self.phase = Buf("phase")
        self._rec = None

    def begin_defer(self):
        self._rec = []

    def end_defer(self):
        r, self._rec = self._rec, None
        return r

    def replay(self, *seqs):
        seqs = [q for q in seqs if q]
        pos = [0] * len(seqs)
        while any(pos[i] < len(q) for i, q in enumerate(seqs)):
            i = min((i for i, q in enumerate(seqs) if pos[i] < len(q)), key=lambda i: pos[i] / len(seqs[i]))
            self.add(*seqs[i][pos[i]])
            pos[i] += 1

    def add(self, eng, fn, reads=(), writes=(), dma=False):
        if self._rec is not None:
            self._rec.append((eng, fn, tuple(reads), tuple(writes), dma))
            return None
        idx = len(self.ops)
        op = Op(idx, eng, fn, dma)
        deps = set()
        if self.phase not in writes:
            reads = list(reads) + [self.phase]
        for b in reads:
            if b.last_w is not None:
                deps.add(b.last_w)
        for b in writes:
            if b.last_w is not None:
                deps.add(b.last_w)
            deps.update(b.readers.values())
        deps.discard(idx)
        op.deps = deps
        for b in reads:
            b.readers[idx if dma else eng] = idx
        for b in writes:
            b.last_w = idx
            b.readers = {}
        self.ops.append(op)
        return op

    def pe(self, fn, r=(), w=()):
        return self.add("pe", fn, r, w)

    def act(self, fn, r=(), w=()):
        return self.add("act", fn, r, w)

    def dve(self, fn, r=(), w=()):
        return self.add("dve", fn, r, w)

    def pool(self, fn, r=(), w=()):
        return self.add("pool", fn, r, w)

    def dma(self, eng, out, in_, r=(), w=(), **kw):
        return self.add(eng, lambda e: e.dma_start(out=out, in_=in_, **kw), r, w, dma=True)

    def barrier(self, fn):
        return self.add("dve", fn, (), [self.phase])

    def cc(self, fn, r=(), w=()):
        op = self.add("pool", fn, r, w, dma=True)
        op.cc = True
        return op

    def emit(self, final_wait_ops=()):
        nc = self.nc
        ops = self.ops
        for op in ops:
            for d in op.deps:
                Dp = ops[d]
                if Dp.eng == "pe" and op.eng == "pe" and not Dp.dma:
                    continue
                Dp.signal = True
        for o in final_wait_ops:
            o.signal = True
        cnt = {e: 0 for e in ENGS}
        dcnt = {e: 0 for e in ENGS}
        for op in ops:
            if op.cc:
                op.dsem = ("c", op.idx)
                op.dval = 1
            elif op.dma:
                n = dcnt[op.eng]
                dcnt[op.eng] += 1
                op.dsem = (op.eng, n % self.K)
                op.dval = 16 * (n // self.K + 1)
            elif op.signal:
                cnt[op.eng] += 1
                op.count = cnt[op.eng]
        waited = {e: {} for e in ENGS}
        for op in ops:
            need = {}
            for d in op.deps:
                Dp = ops[d]
                if Dp.dma:
                    key = ("d",) + Dp.dsem
                    val = Dp.dval
                else:
                    if Dp.eng == "pe" and op.eng == "pe":
                        continue
                    key = ("e", Dp.eng)
                    val = Dp.count
                if need.get(key, 0) < val:
                    need[key] = val
            if op.dma and not op.cc:
                prev = op.dval - 16
                if prev > 0:
                    key = ("d",) + op.dsem
                    if need.get(key, 0) < prev:
                        need[key] = prev
            w = waited[op.eng]
            for key, val in need.items():
                if w.get(key, 0) < val:
                    w[key] = val
                    op.waits.append((key, val))
        fence = []
        w = waited["sp"]
        for o in final_wait_ops:
            if o.dma:
                key = ("d",) + o.dsem
                val = o.dval
            else:
                key = ("e", o.eng)
                val = o.count
            if w.get(key, 0) < val:
                w[key] = val
                fence.append((key, val))
        self.stats = {e: sum(1 for o in ops if o.eng == e) for e in ENGS}
        self.stats["waits"] = sum(len(o.waits) for o in ops)
        self.stats["maxcnt"] = dict(cnt)
        with contextlib.ExitStack() as st:
            sems = {}
            for e in ENGS:
                sems[("e", e)] = st.enter_context(nc.semaphore("s_" + e))
            for e in ENGS:
                for k in range(min(self.K, dcnt[e])):
                    sems[("d", e, k)] = st.enter_context(nc.semaphore("d_%s_%d" % (e, k)))
            for op in ops:
                if op.cc:
                    sems[("d",) + op.dsem] = st.enter_context(nc.semaphore("c_%d" % op.idx))
            block = st.enter_context(nc.Block())

            def run(engobj, ename, extra=()):
                for op in ops:
                    if op.eng != ename:
                        continue
                    for key, val in op.waits:
                        engobj.wait_ge(sems[key], val)
                    if op.fn is None:
                        continue
                    ins = op.fn(engobj)
                    if op.cc:
                        ins.then_inc(sems[("d",) + op.dsem], 1)
                    elif op.dma:
                        ins.then_inc(sems[("d",) + op.dsem], 16)
                    elif op.signal:
                        ins.then_inc(sems[("e", ename)], 1)
                for key, val in extra:
                    engobj.wait_ge(sems[key], val)

            @block.tensor
            def _(e):
                run(e, "pe")

            @block.scalar
            def _(e):
                run(e, "act")

            @block.vector
            def _(e):
                run(e, "dve")

            @block.gpsimd
            def _(e):
                run(e, "pool")

            @block.sync
            def _(e):
                run(e, "sp", fence)


class Arena:
    def __init__(self, nc, base, limit):
        self.nc = nc
        self.off = base
        self.limit = limit
        self.n = 0

    def alloc(self, shape, dt, name=None):
        nb = int(np.prod(shape[1:])) * (4 if dt == F32 else 2)
        nb = (nb + 31) // 32 * 32
        off = self.off
        self.off += nb
        assert self.off <= self.limit, ("SBUF overflow", name, self.off, self.limit)
        self.n += 1
        return self.nc.alloc_sbuf_tensor_at("%s_%d_%d" % (name or "t", off, self.n), list(shape), dt, offset=off)

    def fork(self):
        return Arena(self.nc, self.off, self.limit)


def MM(out, lhsT, rhs, start=True, stop=True):
    return lambda e: e.matmul(out, lhsT=lhsT, rhs=rhs, start=start, stop=stop)


def TR(out, in_, ident):
    return lambda e: e.transpose(out=out, in_=in_, identity=ident)


def ACTF(out, in_, func, **kw):
    return lambda e: e.activation(out=out, in_=in_, func=func, **kw)


def TT(out, a, b, op):
    return lambda e: e.tensor_tensor(out=out, in0=a, in1=b, op=op)


def TS(out, a, s1, s2, op0, op1=None):
    if op1 is None:
        return lambda e: e.tensor_scalar(out=out, in0=a, scalar1=s1, scalar2=None, op0=op0)
    return lambda e: e.tensor_scalar(out=out, in0=a, scalar1=s1, scalar2=s2, op0=op0, op1=op1)


def STT(out, in0, scalar, in1, op0, op1):
    return lambda e: e.scalar_tensor_tensor(out=out, in0=in0, scalar=scalar, in1=in1, op0=op0, op1=op1)


def CP(out, in_):
    return lambda e: e.tensor_copy(out=out, in_=in_)


def AC(out, in_):
    return lambda e: e.copy(out=out, in_=in_)


def MS(out, val):
    return lambda e: e.memset(out, val)


def build_program(stage=3):
    nc = bass.Bass("TRN2", target_bir_lowering=False)
    di = lambda n, s: nc.dram_tensor(n, list(s), F32, kind="ExternalInput").ap()
    x_d = di("x", [NTOK, D])
    w1i_d = di("w1i", [D, 2 * DFF]); w1o_d = di("w1o", [DFF, D])
    w2i_d = di("w2i", [D, 2 * DFF]); w2o_d = di("w2o", [DFF, D])
    wmi_d = di("wmi", [D, DIN]); wmo_d = di("wmo", [D, D])
    gpre_d = di("gpre", [128, 3 * D]); gpost_d = di("gpost", [128, 3 * D])
    lbl_d = di("lbl", [128, 2 * 512]); hgn_d = di("hgn", [128, 4]); mln_d = di("mln", [128, 512])
    cw_d = di("cw", [128, 16]); cb_d = di("cb", [128, 4]); gb_d = di("gb", [128, 8]); flag_d = di("flag", [128, 1])
    cst_d = di("cst", [128, NCST])
    out_d = nc.dram_tensor("out", [NTOK, D], F32, kind="ExternalOutput").ap()
    cc_src = nc.dram_tensor("cc_src", [128, NST], F32, kind="Internal").ap()
    cc_dst = nc.dram_tensor("cc_dst", [256, NST], F32, kind="Internal").ap()

    P = Prog(nc)
    A = Arena(nc, 16512, 229376)
    x_sb = A.alloc([128, 16, D], F32, "x")
    gpre_sb = A.alloc([128, D], F32, "gpre"); gpost_sb = A.alloc([128, D], F32, "gpost")
    cst_sb = A.alloc([128, NCST], F32, "cst")
    ident_bf = A.alloc([128, 128], BF16, "ident")
    ones_bf = A.alloc([128, 128], BF16, "ones_bf")
    junk = A.alloc([128, D], BF16, "junk")
    ybf = [A.alloc([128, D], BF16, "ybf%d" % i) for i in range(2)]
    stat = A.alloc([128, 64], F32, "stat")
    Bx = [Buf("x%d" % s) for s in range(16)]
    Bgpre, Bgpost, Bcst, Bident, Bjunk = Buf(), Buf(), Buf(), Buf(), Buf()
    Bybf = [Buf(), Buf()]
    Bstat = [Buf() for _ in range(16)]
    TRI = cst_sb[:, 128:256]; DM = cst_sb[:, 256:384]; EM = cst_sb[:, 384:512]; ONES = cst_sb[:, 512:640]
    SELC = cst_sb[:, 640:644]; MAB = cst_sb[:, 644:646]
    psum = [nc.alloc_psum_tensor("ps%d" % i, [128, 512], F32) for i in range(6)]
    psum_b = [nc.alloc_psum_tensor("psb%d" % i, [128, 1024], BF16) for i in range(2)]
    Bpt = [Buf("pt0"), Buf("pt1")]
    tb_ctr = [0]
    tb_pool = [[0, 1]]

    def nbt():
        pl = tb_pool[0]
        i = pl[tb_ctr[0] % len(pl)]
        tb_ctr[0] += 1
        return i

    Bps = [Buf("ps%d" % i) for i in range(8)]
    bank_ctr = [0]
    bank_pool = [list(range(6))]

    def nb():
        pl = bank_pool[0]
        i = pl[bank_ctr[0] % len(pl)]
        bank_ctr[0] += 1
        return i

    P.dma("sp", cst_sb[:], cst_d[:], w=[Bcst])
    xv = x_d.rearrange("(s p) d -> p s d", p=128)
    ov = out_d.rearrange("(s p) d -> p s d", p=128)
    for q in range(4):
        P.dma("sp", x_sb[:, 4 * q:4 * q + 4, :], xv[:, 4 * q:4 * q + 4, :], w=Bx[4 * q:4 * q + 4])
    P.dve(CP(ident_bf[:], cst_sb[:, 0:128]), r=[Bcst], w=[Bident])
    P.dve(CP(ones_bf[:], cst_sb[:, 512:640]), r=[Bcst], w=[Bident])

    def prenorm_T(s, dstT, dcols, Bdst, k):
        yb = ybf[k % 2]; By = Bybf[k % 2]
        st_ = stat[:, 4 * s:4 * s + 4]
        P.act(ACTF(junk[:], x_sb[:, s, :], AF.Square, accum_out=st_[:, 0:1]), r=[Bx[s]], w=[Bjunk, Bstat[s]])
        P.dve(TS(st_[:, 1:2], st_[:, 0:1], 1.0 / D, EPS, ALU.mult, ALU.add), r=[Bstat[s]], w=[Bstat[s]])
        P.act(ACTF(st_[:, 2:3], st_[:, 1:2], AF.Ln), r=[Bstat[s]], w=[Bstat[s]])
        P.act(ACTF(st_[:, 2:3], st_[:, 2:3], AF.Exp, scale=-0.5), r=[Bstat[s]], w=[Bstat[s]])
        P.dve(STT(yb[:], x_sb[:, s, :], st_[:, 2:3], gpre_sb[:], ALU.mult, ALU.mult), r=[Bx[s], Bstat[s], Bgpre], w=[By])
        b = nbt()
        pv = psum_b[b][:].rearrange("p (a c) -> p a c", a=8)
        for dc in range(8):
            P.pe(TR(pv[:, dc, :], yb[:, dc * 128:(dc + 1) * 128], ident_bf[:]), r=[By, Bident], w=[Bpt[b]])
        P.act(AC(dstT[:, :, dcols], pv), r=[Bpt[b]], w=[Bdst])

    def prenorm_batch(ss, dstT, Bdsts, col0=None):
        s0, n = ss[0], len(ss)
        sv = stat[:, 0:64].rearrange("p (s c) -> p s c", c=4)
        Bst = [Bstat[s] for s in ss]
        for s in ss:
            P.act(ACTF(junk[:], x_sb[:, s, :], AF.Square, accum_out=stat[:, 4 * s:4 * s + 1]), r=[Bx[s]], w=[Bjunk, Bstat[s]])
        P.dve(TS(sv[:, s0:s0 + n, 1:2], sv[:, s0:s0 + n, 0:1], 1.0 / D, EPS, ALU.mult, ALU.add), r=Bst, w=Bst)
        P.act(ACTF(sv[:, s0:s0 + n, 2:3], sv[:, s0:s0 + n, 1:2], AF.Ln), r=Bst, w=Bst)
        P.act(ACTF(sv[:, s0:s0 + n, 2:3], sv[:, s0:s0 + n, 2:3], AF.Exp, scale=-0.5), r=Bst, w=Bst)
        for i, s in enumerate(ss):
            yb = ybf[s % 2]; By = Bybf[s % 2]
            P.dve(STT(yb[:], x_sb[:, s, :], stat[:, 4 * s + 2:4 * s + 3], gpre_sb[:], ALU.mult, ALU.mult),
                  r=[Bx[s], Bstat[s], Bgpre], w=[By])
            b = nbt()
            pv = psum_b[b][:].rearrange("p (a c) -> p a c", a=8)
            for dc in range(8):
                P.pe(TR(pv[:, dc, :], yb[:, dc * 128:(dc + 1) * 128], ident_bf[:]), r=[By, Bident], w=[Bpt[b]])
            c0 = (s - s0 + (ss[0] % 8 if col0 is None else col0)) * 128
            P.act(AC(dstT[:, :, c0:c0 + 128], pv), r=[Bpt[b]], w=[Bdsts[i]])

    def postnorm_res(s, banks, gscale, k, tmp, Btmp):
        st_ = stat[:, 4 * s:4 * s + 4]
        for hh in range(2):
            P.act(ACTF(junk[:, 0:512], psum[banks[hh]][:], AF.Square, accum_out=st_[:, hh:hh + 1]),
                  r=[Bps[banks[hh]]], w=[Bjunk, Bstat[s]])
        P.dve(TT(st_[:, 2:3], st_[:, 0:1], st_[:, 1:2], ALU.add), r=[Bstat[s]], w=[Bstat[s]])
        P.dve(TS(st_[:, 2:3], st_[:, 2:3], 1.0 / D, EPS, ALU.mult, ALU.add), r=[Bstat[s]], w=[Bstat[s]])
        P.act(ACTF(st_[:, 3:4], st_[:, 2:3], AF.Ln), r=[Bstat[s]], w=[Bstat[s]])
        P.act(ACTF(st_[:, 3:4], st_[:, 3:4], AF.Exp, scale=-0.5), r=[Bstat[s]], w=[Bstat[s]])
        P.dve(TS(st_[:, 3:4], st_[:, 3:4], gscale, None, ALU.mult), r=[Bstat[s]], w=[Bstat[s]])
        for hh in range(2):
            t = tmp[(2 * k + hh) % len(tmp)]; Bt = Btmp[(2 * k + hh) % len(tmp)]
            P.dve(STT(t[:], psum[banks[hh]][:], st_[:, 3:4], gpost_sb[:, hh * 512:(hh + 1) * 512], ALU.mult, ALU.mult),
                  r=[Bps[banks[hh]], Bstat[s], Bgpost], w=[Bt])
            P.dve(TT(x_sb[:, s, hh * 512:(hh + 1) * 512], x_sb[:, s, hh * 512:(hh + 1) * 512], t[:], ALU.add),
                   r=[Bt, Bx[s]], w=[Bx[s]])

    def ffn(wi_d, wo_d, gi, is_last):
        bank_pool[0] = list(range(6))
        F = A.fork()
        yT = F.alloc([128, 8, 1024], BF16, "yT"); ByT = [Buf() for _ in range(8)]
        actT = F.alloc([128, NJ, 1024], BF16, "actT"); Bact = [[Buf(), Buf()] for _ in range(NJ)]
        wout = F.alloc([128, NJ, D], BF16, "wout"); Bwout = [Buf(), Buf()]
        win = [F.alloc([128, 8, 512], BF16, "win%d" % i) for i in range(2)]; Bwin = [[Buf(), Buf()], [Buf(), Buf()]]
        sg = [F.alloc([128, 512], BF16, "sg%d" % i) for i in range(2)]; Bsg = [Buf(), Buf()]
        tmp = [F.alloc([128, 512], F32, "tmp%d" % i) for i in range(2)]; Btmp = [Buf(), Buf()]
        P.dma("sp", gpre_sb[:], gpre_d[:, gi * D:(gi + 1) * D], w=[Bgpre])
        P.dma("sp", gpost_sb[:], gpost_d[:, gi * D:(gi + 1) * D], w=[Bgpost])
        wiv = wi_d.rearrange("(c p) n -> p c n", p=128)
        wov = wo_d.rearrange("(j p) n -> p j n", p=128)
        outs = []
        wout_loaded = False
        prenorm_batch([0, 1, 2, 3], yT, ByT[0:4])
        prenorm_batch([4, 5, 6, 7], yT, ByT[4:8])
        for ps_ in range(2):
            for jp in range(NJ // 2):
                k = ps_ * (NJ // 2) + jp
                w_ = win[k % 2]; Bw = Bwin[k % 2]
                P.dma("pool", w_[:, :, 0:256], wiv[:, :, jp * 256:(jp + 1) * 256], w=[Bw[0]])
                P.dma("pool", w_[:, :, 256:512], wiv[:, :, DFF + jp * 256:DFF + (jp + 1) * 256], w=[Bw[1]])
                if not wout_loaded and jp == 1:
                    P.dma("pool", wout[:, 0:11, :], wov[:, 0:11, :], w=[Bwout[0]])
                    P.dma("pool", wout[:, 11:22, :], wov[:, 11:22, :], w=[Bwout[1]])
                    wout_loaded = True
                for jj in range(2):
                    j = jp * 2 + jj
                    bg = [nb(), nb()]
                    bu = [nb(), nb()]
                    for which, banks, coff in ((0, bg, jj * 128), (1, bu, 256 + jj * 128)):
                        for tt in range(2):
                            for dc in range(8):
                                P.pe(MM(psum[banks[tt]][:], w_[:, dc, coff:coff + 128], yT[:, dc, tt * 512:(tt + 1) * 512],
                                        start=(dc == 0), stop=(dc == 7)),
                                     r=[Bw[which]] + ByT[tt * 4:tt * 4 + 4], w=[Bps[banks[tt]]])
                    for tt in range(2):
                        kk_ = (2 * j + tt) % 2
                        P.act(ACTF(sg[kk_][:], psum[bg[tt]][:], AF.Silu), r=[Bps[bg[tt]]], w=[Bsg[kk_]])
                        P.dve(TT(actT[:, j, tt * 512:(tt + 1) * 512], sg[kk_][:], psum[bu[tt]][:], ALU.mult),
                              r=[Bsg[kk_], Bps[bu[tt]]], w=[Bact[j][tt]])
            for s8 in range(8):
                s = ps_ * 8 + s8
                banks = [nb(), nb()]
                for hh in range(2):
                    for j in range(NJ):
                        P.pe(MM(psum[banks[hh]][:], actT[:, j, s8 * 128:(s8 + 1) * 128], wout[:, j, hh * 512:(hh + 1) * 512],
                                start=(j == 0), stop=(j == NJ - 1)),
                             r=[Bact[j][s8 // 4], Bwout[j // 11]], w=[Bps[banks[hh]]])
                if ps_ == 0:
                    prenorm_T(8 + s8, yT, slice(s8 * 128, (s8 + 1) * 128), ByT[s8], 8 + s8)
                postnorm_res(s, banks, 0.5, s8, tmp, Btmp)
                if is_last:
                    outs.append(P.dma("sp", ov[:, s, :], x_sb[:, s, :], r=[Bx[s]]))
        return outs

    def mixer():
        bank_pool[0] = list(range(6))
        M = A.fork()
        wmi = M.alloc([128, 8, DIN], BF16, "wmi"); Bwmi = [Buf() for _ in range(4)]
        wmo_off = M.off
        wmo = M.alloc([128, 8, D], BF16, "wmo"); Bwmo = Buf()
        ALT = Arena(nc, wmo_off, M.off)
        lb_bc = M.alloc([128, 512], F32, "lb")
        mln_sb = M.alloc([128, 512], F32, "mln")
        small = M.alloc([128, 64], F32, "small")
        hgn_sb = small[:, 0:4]; cw_sb = small[:, 4:20]; cb_sb = small[:, 20:24]; gb_sb = small[:, 24:32]; flag_sb = small[:, 32:33]
        Bpar = Buf()
        S_sb = M.alloc([128, 4, 128], F32, "S"); C_sb = M.alloc([128, 4, 129], F32, "C")
        BS = [Buf() for _ in range(4)]; BC = [Buf() for _ in range(4)]
        qk_pre = M.alloc([128, 4, TTK + 3], F32, "qkpre"); Bqk = [Buf() for _ in range(4)]
        Bgath = Buf()
        hT = M.alloc([128, 8, TTK], BF16, "hT"); BhT = [Buf() for _ in range(NS4)]
        NB = 1
        fF = [M.alloc([128, 512], F32, "fF0"), ALT.alloc([128, 512], F32, "fF1")]; BfF = [Buf(), Buf()]
        lgF = [M.alloc([128, 512], F32, "lgF0"), ALT.alloc([128, 512], F32, "lgF1")]; BlgF = [Buf(), Buf()]
        eF = [M.alloc([128, 512], BF16, "eF%d" % i) for i in range(NB)]; BeF = [Buf() for _ in range(NB)]
        e2F = [M.alloc([128, 512], BF16, "e2F%d" % i) for i in range(NB)]; Be2F = [Buf() for _ in range(NB)]
        qF = [M.alloc([128, 512], BF16, "qF%d" % i) for i in range(NB)]; BqF = [Buf() for _ in range(NB)]
        kdd = [M.alloc([128, 512], BF16, "kdd0"), ALT.alloc([128, 512], BF16, "kdd1")]; Bkdd = [Buf(), Buf()]
        qd, Bqd, kd, Bkd = qF, BqF, e2F, Be2F
        vbf = [M.alloc([128, 512], BF16, "vbf0"), ALT.alloc([128, 512], BF16, "vbf1")]; Bvbf = [Buf(), Buf()]
        goff = M.off
        qdT = M.alloc([128, 4, TTK], BF16, "qdT"); BqdT = [Buf() for _ in range(NS4)]
        kdT = M.alloc([128, 4, TTK], BF16, "kdT"); BkdT = [Buf() for _ in range(NS4)]
        hgT = M.alloc([128, 4, TTK], BF16, "hgT"); BhgT = Buf()
        assert M.off - goff >= NST * 4
        gath = nc.alloc_sbuf_tensor_at("gath_alias", [128, NST], F32, offset=goff)
        eS = [M.alloc([128, 16], F32, "eS0"), ALT.alloc([128, 16], F32, "eS1")]; BeS = [Buf(), Buf()]
        Sb = [[M.alloc([128, 128], BF16, "Sb%d_%d" % (i, c)) for c in range(2)] for i in range(4)]
        BSb = [[Buf(), Buf()] for _ in range(4)]
        scm = [M.alloc([128, 128], BF16, "scm%d" % i) for i in range(4)]; Bscm = [Buf() for _ in range(4)]
        sqF = [M.alloc([128, 512], F32, "sqF%d" % i) for i in range(1)]; BsqF = [Buf()]
        rsF = [M.alloc([128, 512], F32, "rsF%d" % i) for i in range(1)]; BrsF = [Buf()]
        mixT = M.alloc([128, 8, TTK], BF16, "mixT"); BmixT = [Buf() for _ in range(NS4)]
        tmp = [sqF[0], rsF[0]]; Btmp = [BsqF[0], BrsF[0]]
        qT = M.alloc([128, 2, TTK], BF16, "qT"); BqT = Buf()
        qTA = M.alloc([128, 2, TTK], BF16, "qTA"); qTB = M.alloc([128, 2, TTK], BF16, "qTB"); BqTAB = Buf()
        kT = M.alloc([128, 2, TTK], BF16, "kT"); BkT = Buf()
        caccs = [(sqF[0][:, 0:TTK], BsqF[0]), (rsF[0][:, 0:TTK], BrsF[0])]
        mg = [M.alloc([128, 40], F32, "mg0"), ALT.alloc([128, 40], F32, "mg1")]; Bmg = [Buf(), Buf()]
        vp = [M.alloc([128, 4, 129], BF16, "vp0"), ALT.alloc([128, 4, 129], BF16, "vp1")]; Bvp = [Buf(), Buf()]
        kw = [M.alloc([128, 4, 64], BF16, "kw0"), ALT.alloc([128, 4, 64], BF16, "kw1")]; Bkw = [Buf(), Buf()]
        sgm = [M.alloc([128, 512], BF16, "sgm%d" % i) for i in range(NB)]; Bsgm = [Buf() for _ in range(NB)]
        Cb = [M.alloc([128, 4, 129], BF16, "Cb%d" % c) for c in range(2)]; BCb = [[Buf(), Buf()] for _ in range(4)]
        ebL = [M.alloc([128, 8], F32, "ebL0"), ALT.alloc([128, 8], F32, "ebL1")]; BebL = [Buf(), Buf()]
        mlo = [M.alloc([128, 512], BF16, "mlo%d" % i) for i in range(NB)]; Bmlo = [Buf() for _ in range(NB)]
        hjunk = junk[:, 0:128]; Bhjunk = Bjunk
        print("mixer arena end", M.off, "limit", M.limit)

        P.dma("sp", gpre_sb[:], gpre_d[:, D:2 * D], w=[Bgpre])
        P.dma("sp", gpost_sb[:], gpost_d[:, D:2 * D], w=[Bgpost])
        P.dma("sp", fF[0][:], lbl_d[:, 0:512], w=[BfF[0]])
        P.dma("sp", lgF[0][:], lbl_d[:, 512:1024], w=[BlgF[0]])
        P.dma("sp", mln_sb[:], mln_d[:], w=[Bpar])
        P.dma("sp", hgn_sb, hgn_d[:], w=[Bpar]); P.dma("sp", cw_sb, cw_d[:], w=[Bpar]); P.dma("sp", cb_sb, cb_d[:], w=[Bpar])
        P.dma("sp", gb_sb, gb_d[:], w=[Bpar]); P.dma("sp", flag_sb, flag_d[:], w=[Bpar])
        wmiv = wmi_d.rearrange("(c p) n -> p c n", p=128)
        colsplit = [0, 1024, 2048, 3072, DIN]
        Bwchain = Buf("wmichain")
        for i in (2, 0, 1, 3):
            P.dma("pool", wmi[:, :, colsplit[i]:colsplit[i + 1]], wmiv[:, :, colsplit[i]:colsplit[i + 1]], w=[Bwmi[i], Bwchain])

        def wbuf(c0, c1):
            return [Bwmi[i] for i in range(4) if colsplit[i] < c1 and colsplit[i + 1] > c0]

        l0, l1, lm = fF[0], lgF[0], sqF[0]
        Bl0, Bl1, Blm = BfF[0], BlgF[0], BsqF[0]
        P.dve(TT(lm[:], l0[:], l1[:], ALU.max), r=[Bl0, Bl1], w=[Blm])
        P.dve(TT(l0[:], l0[:], lm[:], ALU.subtract), r=[Blm, Bl0], w=[Bl0])
        P.dve(TT(l1[:], l1[:], lm[:], ALU.subtract), r=[Blm, Bl1], w=[Bl1])
        P.act(ACTF(l0[:], l0[:], AF.Exp), r=[Bl0], w=[Bl0])
        P.act(ACTF(l1[:], l1[:], AF.Exp), r=[Bl1], w=[Bl1])
        P.dve(TT(l1[:], l0[:], l1[:], ALU.add), r=[Bl0, Bl1], w=[Bl1])
        P.dve(lambda e: e.reciprocal(out=l1[:], in_=l1[:]), r=[Bl1], w=[Bl1])
        P.dve(TT(lb_bc[:], l0[:], l1[:], ALU.mult), r=[Bl0, Bl1], w=[Bpar])

        def run_pass(pass2):
            if not pass2:
                for h in range(4):
                    P.pool(MS(S_sb[:, h, :], 0.0), w=[BS[h]])
                    P.pool(MS(C_sb[:, h, :], 0.0), w=[BC[h]])
                for g in range(4):
                    P.pool(MS(qk_pre[:, g, :], 0.0), w=[Bqk[g]])
            else:
                tile_prenorm(0)
                pre_done.add(0)
                for h in range(4):
                    P.dve(TS(S_sb[:, h, :], gath[:, h * 128:(h + 1) * 128], flag_sb, None, ALU.mult), r=[Bgath, Bpar], w=[BS[h]])
                    P.dve(TS(C_sb[:, h, :], gath[:, 512 + h * 129:512 + (h + 1) * 129], flag_sb, None, ALU.mult), r=[Bgath, Bpar], w=[BC[h]])
                for g in range(4):
                    P.dve(TS(qk_pre[:, g, 0:3], gath[:, 1028 + 3 * g:1028 + 3 * g + 3], flag_sb, None, ALU.mult), r=[Bgath, Bpar], w=[Bqk[g]])
            for tt in range(NTOK // TTK):
                run_tile(tt, pass2)

        def proj_tok(s4, c0, n, bank, col0=0):
            for dc in range(8):
                P.pe(MM(psum[bank][:, col0:col0 + n], hT[:, dc, s4 * 128:(s4 + 1) * 128], wmi[:, dc, c0:c0 + n],
                        start=(dc == 0), stop=(dc == 7)),
                     r=[BhT[s4]] + wbuf(c0, c0 + n), w=[Bps[bank]])

        def proj_feat(c0, bank):
            for dc in range(8):
                P.pe(MM(psum[bank][:, 0:TTK], wmi[:, dc, c0:c0 + 128], hT[:, dc, :], start=(dc == 0), stop=(dc == 7)),
                     r=BhT + wbuf(c0, c0 + 128), w=[Bps[bank]])

        def tile_prenorm(tt):
            prenorm_batch([tt * NS4 + s4 for s4 in range(NS4)], hT, BhT, col0=0)

        pre_done = set()

        def run_tile(tt, pass2):
            if tt == 0 and 0 not in pre_done:
                tile_prenorm(0)
            pre_done.discard(0)
            groups = (0, 1, 2, 3)
            for g in groups:
                if g < 2 and not pass2 and tt != NTOK // TTK - 1:
                    continue
                b = nb()
                proj_feat((C_MQ if g < 2 else C_MK) + (g % 2) * 128, b)
                P.act(lambda e, o=qk_pre[:, g, 3:TTK + 3], i=psum[b][:, 0:TTK]: e.copy(out=o, in_=i), r=[Bps[b]], w=[Bqk[g]])
                if g >= 2 or pass2:
                    cacc, Bcacc = caccs[g % 2]
                    P.dve(TS(cacc, qk_pre[:, g, 0:TTK], cw_sb[:, 4 * g:4 * g + 1], cb_sb[:, g:g + 1], ALU.mult, ALU.add),
                           r=[Bqk[g], Bpar], w=[Bcacc])
                    for j in range(1, 4):
                        P.dve(STT(cacc, qk_pre[:, g, j:j + TTK], cw_sb[:, 4 * g + j:4 * g + j + 1], cacc, ALU.mult, ALU.add),
                               r=[Bqk[g], Bpar, Bcacc], w=[Bcacc])
                    dst, Bd = (qT, BqT) if g < 2 else (kT, BkT)
                    P.act(ACTF(dst[:, g % 2, :], cacc, AF.Sigmoid), r=[Bcacc], w=[Bd])
                    P.dve(TT(dst[:, g % 2, :], cacc, dst[:, g % 2, :], ALU.mult), r=[Bcacc, Bd], w=[Bd])
                P.pool(CP(qk_pre[:, g, 0:3], qk_pre[:, g, TTK:TTK + 3]), r=[Bqk[g]], w=[Bqk[g]])
            if pass2:
                qv = qT[:].rearrange("p g (s c t) -> p g s c t", s=NS4, c=2)
                qav = qTA[:].rearrange("p g (s c t) -> p g s c t", s=NS4, c=2)
                qbv = qTB[:].rearrange("p g (s c t) -> p g s c t", s=NS4, c=2)
                for g in range(2):
                    P.pool(MS(qTA[:, g, :], 0.0), w=[BqTAB])
                    P.pool(MS(qTB[:, g, :], 0.0), w=[BqTAB])
                    P.pool(CP(qav[:, g, :, 0, :], qv[:, g, :, 0, :]), r=[BqT], w=[BqTAB])
                    P.pool(CP(qbv[:, g, :, 1, :], qv[:, g, :, 1, :]), r=[BqT], w=[BqTAB])
                for h in range(4):
                    b = nb()
                    proj_feat(C_HG + h * 128, b)
                    P.act(ACTF(tmp[h % 2][:, 0:TTK], psum[b][:, 0:TTK], AF.Sigmoid), r=[Bps[b]], w=[Btmp[h % 2]])
                    P.dve(STT(hgT[:, h, :], psum[b][:, 0:TTK], hgn_sb[:, h:h + 1], tmp[h % 2][:, 0:TTK], ALU.mult, ALU.mult),
                          r=[Bps[b], Btmp[h % 2], Bpar], w=[BhgT])
            for s4 in range(NS4):
                nxt = (lambda t=tt + 1: tile_prenorm(t)) if (s4 == NS4 - 1 and tt + 1 < NTOK // TTK) else None
                run_subtile(tt, s4, pass2, nxt)

        def run_subtile(tt, s4, pass2, after_prep=None):
            s = tt * NS4 + s4
            k = 0 if pass2 else s % 2
            tok = slice(s4 * 128, (s4 + 1) * 128)
            b_hf = nb(); proj_tok(s4, C_HF, 512, b_hf)
            P.act(ACTF(fF[k][:], psum[b_hf][:], AF.Sigmoid, scale=-1.0), r=[Bps[b_hf]], w=[BfF[k]])
            if pass2:
                b_hq = nb(); proj_tok(s4, C_HQ, 512, b_hq)
                P.act(ACTF(qF[k][:], psum[b_hq][:], AF.Sigmoid), r=[Bps[b_hq]], w=[BqF[k]])
                b_mo = nb(); proj_tok(s4, C_MO, 512, b_mo)
                P.act(ACTF(sgm[k][:], psum[b_mo][:], AF.Sigmoid), r=[Bps[b_mo]], w=[Bsgm[k]])
                P.dve(TT(qF[k][:], psum[b_hq][:], qF[k][:], ALU.mult), r=[Bps[b_hq], BqF[k]], w=[BqF[k]])
            P.begin_defer()
            bank_pool[0] = [0, 1, 2]; tb_pool[0] = [0]
            P.dve(STT(fF[k][:], lb_bc[:], 1.0, fF[k][:], ALU.subtract, ALU.mult), r=[Bpar, BfF[k]], w=[BfF[k]])
            P.act(ACTF(lgF[k][:], fF[k][:], AF.Ln, bias=1.0), r=[BfF[k]], w=[BlgF[k]])
            b_hi = nb(); proj_tok(s4, C_HI, 512, b_hi)
            P.act(lambda e, o=vbf[k][:], i=psum[b_hi][:]: e.copy(out=o, in_=i), r=[Bps[b_hi]], w=[Bvbf[k]])
            b_gl = nb()
            P.pe(MM(psum[b_gl][:], EM, lgF[k][:]), r=[Bcst, BlgF[k]], w=[Bps[b_gl]])
            P.act(ACTF(kdd[k][:], psum[b_gl][:], AF.Exp), r=[Bps[b_gl]], w=[Bkdd[k]])
            P.dve(STT(kdd[k][:], fF[k][:], -1.0, kdd[k][:], ALU.mult, ALU.mult), r=[BfF[k], Bkdd[k]], w=[Bkdd[k]])
            b_sc = nb()
            for h in range(4):
                P.pe(MM(psum[b_sc][:, 4 * h:4 * h + 4], lgF[k][:, h * 128:(h + 1) * 128], SELC), r=[Bcst, BlgF[k]], w=[Bps[b_sc]])
            P.act(ACTF(eS[k][:], psum[b_sc][:, 0:16], AF.Exp), r=[Bps[b_sc]], w=[BeS[k]])
            if pass2:
                b_gm = nb()
                P.pe(MM(psum[b_gm][:], DM, lgF[k][:]), r=[Bcst, BlgF[k]], w=[Bps[b_gm]])
                P.act(ACTF(eF[k][:], psum[b_gm][:], AF.Exp), r=[Bps[b_gm]], w=[BeF[k]])
                P.act(ACTF(e2F[k][:], psum[b_gm][:], AF.Exp, scale=-1.0), r=[Bps[b_gm]], w=[Be2F[k]])
                P.dve(TT(qd[k][:], qF[k][:], eF[k][:], ALU.mult), r=[BqF[k], BeF[k]], w=[Bqd[k]])
                P.dve(STT(kd[k][:], fF[k][:], -1.0, e2F[k][:], ALU.mult, ALU.mult), r=[BfF[k], Be2F[k]], w=[Bkd[k]])
                for src, Bsrc, dstT, BdT in ((qd[k], Bqd[k], qdT, BqdT), (kd[k], Bkd[k], kdT, BkdT)):
                    b = nbt()
                    pv = psum_b[b][:, 0:512].rearrange("p (a c) -> p a c", a=4)
                    for h in range(4):
                        P.pe(TR(pv[:, h, :], src[:, h * 128:(h + 1) * 128], ident_bf[:]), r=[Bsrc, Bident], w=[Bpt[b]])
                    P.dve(CP(dstT[:, :, tok], pv), r=[Bpt[b]], w=[BdT[s4]])
            seq_hg = P.end_defer()
            P.begin_defer()
            bank_pool[0] = [3, 4, 5]; tb_pool[0] = [1]
            g_ = mg[k]
            b_g = nb(); proj_tok(s4, C_MIG, 8, b_g)
            P.dve(TT(g_[:, 0:8], psum[b_g][:, 0:8], gb_sb, ALU.add), r=[Bps[b_g], Bpar], w=[Bmg[k]])
            P.act(ACTF(g_[:, 8:12], g_[:, 4:8], AF.Exp, scale=-1.0), r=[Bmg[k]], w=[Bmg[k]])
            P.act(ACTF(g_[:, 8:12], g_[:, 8:12], AF.Ln, bias=1.0), r=[Bmg[k]], w=[Bmg[k]])
            P.dve(TS(g_[:, 8:12], g_[:, 8:12], -1.0, None, ALU.mult), r=[Bmg[k]], w=[Bmg[k]])
            P.pe(MM(psum[b_g][:, 8:12], TRI, g_[:, 8:12]), r=[Bcst, Bmg[k]], w=[Bps[b_g]])
            P.pe(MM(psum[b_g][:, 12:16], EM, g_[:, 8:12]), r=[Bcst, Bmg[k]], w=[Bps[b_g]])
            P.dve(TT(g_[:, 12:16], psum[b_g][:, 12:16], g_[:, 0:4], ALU.add), r=[Bps[b_g], Bmg[k]], w=[Bmg[k]])
            P.act(ACTF(g_[:, 12:16], g_[:, 12:16], AF.Exp), r=[Bmg[k]], w=[Bmg[k]])
            if pass2:
                P.act(ACTF(g_[:, 16:20], psum[b_g][:, 8:12], AF.Exp), r=[Bps[b_g]], w=[Bmg[k]])
                P.dve(TS(g_[:, 16:20], g_[:, 16:20], 0.125, None, ALU.mult), r=[Bmg[k]], w=[Bmg[k]])
                P.dve(TT(g_[:, 20:24], g_[:, 0:4], psum[b_g][:, 8:12], ALU.subtract), r=[Bps[b_g], Bmg[k]], w=[Bmg[k]])
                P.act(ACTF(g_[:, 20:24], g_[:, 20:24], AF.Exp), r=[Bmg[k]], w=[Bmg[k]])
            P.dve(TS(g_[:, 24:28], g_[:, 8:12], MAB[:, 0:1], None, ALU.mult), r=[Bmg[k], Bcst], w=[Bmg[k]])
            P.dve(TS(g_[:, 28:32], g_[:, 8:12], MAB[:, 1:2], None, ALU.mult), r=[Bmg[k], Bcst], w=[Bmg[k]])
            P.pe(MM(psum[b_g][:, 16:24], ONES, g_[:, 24:32]), r=[Bcst, Bmg[k]], w=[Bps[b_g]])
            P.act(ACTF(ebL[k][:], psum[b_g][:, 16:24], AF.Exp), r=[Bps[b_g]], w=[BebL[k]])
            b_v = nb(); proj_tok(s4, C_MV, 512, b_v)
            P.pool(MS(vp[k][:, :, 128:129], 1.0), w=[Bvp[k]])
            P.act(lambda e, o=vp[k][:, :, 0:128], i=psum[b_v][:].rearrange("p (h v) -> p h v", h=4): e.copy(out=o, in_=i),
                  r=[Bps[b_v]], w=[Bvp[k]])
            for g in range(2):
                b = nbt()
                pv = psum_b[b][:, 0:128]
                P.pe(TR(pv, kT[:, g, tok], ident_bf[:]), r=[BkT, Bident], w=[Bpt[b]])
                for hh in range(2):
                    h = 2 * g + hh
                    P.dve(TS(kw[k][:, h, :], pv[:, hh * 64:(hh + 1) * 64], g_[:, 12 + h:13 + h], None, ALU.mult),
                          r=[Bpt[b], Bmg[k]], w=[Bkw[k]])
            seq_ml = P.end_defer()
            bank_pool[0] = list(range(6)); tb_pool[0] = [0, 1]
            P.replay(seq_hg, seq_ml)
            if after_prep is not None:
                after_prep()
            if pass2:
                b_s = nb()
                for h in range(4):
                    P.pe(MM(psum[b_s][:, h * 128:(h + 1) * 128], kdT[:, h, tok], qdT[:, h, tok]), r=[BkdT[s4], BqdT[s4]], w=[Bps[b_s]])
                for h in range(4):
                    P.dve(TT(scm[h][:], psum[b_s][:, h * 128:(h + 1) * 128], TRI, ALU.mult), r=[Bps[b_s], Bcst], w=[Bscm[h]])
            for c in range(2):
                cs = slice(c * 64, (c + 1) * 64)
                if pass2:
                    for h in range(4):
                        P.act(ACTF(Sb[h][c][:], S_sb[:, h, :], AF.Copy, scale=eS[k][:, 4 * h + c:4 * h + c + 1]),
                              r=[BS[h], BeS[k]], w=[BSb[h][c]])
                b_u = nb()
                for h in range(4):
                    hs = slice(h * 128, (h + 1) * 128)
                    P.pe(MM(psum[b_u][:, hs], kdd[k][cs, hs], vbf[k][cs, hs]), r=[Bkdd[k], Bvbf[k]], w=[Bps[b_u]])
                for h in range(4):
                    hs = slice(h * 128, (h + 1) * 128)
                    P.dve(STT(S_sb[:, h, :], S_sb[:, h, :], eS[k][:, 4 * h + 2 + c:4 * h + 3 + c], psum[b_u][:, hs], ALU.mult, ALU.add),
                          r=[BS[h], BeS[k], Bps[b_u]], w=[BS[h]])
            HP = [((h % 2) * 64, h // 2) for h in range(4)]
            if pass2:
                for h in range(4):
                    hp, g = HP[h]
                    P.act(AC(Cb[0][hp:hp + 64, h, :], C_sb[hp:hp + 64, h, :]), r=[BC[h]], w=[BCb[h][0]])
            for c in range(2):
                cs = slice(c * 64, (c + 1) * 64)
                bu = [nb(), nb()]
                for h in range(4):
                    hp, g = HP[h]
                    P.pe(MM(psum[bu[h // 2]][:, (h % 2) * 129:(h % 2) * 129 + 129],
                            kw[k][cs, g * 2:g * 2 + 2, :].rearrange("p a b -> p (a b)"), vp[k][cs, h, :]),
                         r=[Bkw[k], Bvp[k]], w=[Bps[bu[h // 2]]])
                for h in range(4):
                    hp, g = HP[h]
                    hsl = slice(hp, hp + 64)
                    P.dve(STT(C_sb[hsl, h, :], C_sb[hsl, h, :], ebL[k][hsl, 4 * c + h:4 * c + h + 1],
                              psum[bu[h // 2]][hsl, (h % 2) * 129:(h % 2) * 129 + 129], ALU.mult, ALU.add),
                          r=[BC[h], BebL[k], Bps[bu[h // 2]]], w=[BC[h]])
                if pass2 and c == 0:
                    for h in range(4):
                        hp, g = HP[h]
                        P.act(AC(Cb[1][hp:hp + 64, h, :], C_sb[hp:hp + 64, h, :]), r=[BC[h]], w=[BCb[h][1]])
            if pass2:
                b_o = nb()
                for h in range(4):
                    hs = slice(h * 128, (h + 1) * 128)
                    for c in range(2):
                        cs = slice(c * 64, (c + 1) * 64)
                        ctok = slice(s4 * 128 + c * 64, s4 * 128 + (c + 1) * 64)
                        oo = psum[b_o][:, h * 128 + c * 64:h * 128 + (c + 1) * 64]
                        P.pe(MM(oo, vbf[k][:, hs], scm[h][:, cs], start=True, stop=False), r=[Bvbf[k], Bscm[h]], w=[Bps[b_o]])
                        P.pe(MM(oo, Sb[h][c][:], qdT[:, h, ctok], start=False, stop=True), r=[BSb[h][c], BqdT[s4]], w=[Bps[b_o]])
                b_oo = [nb(), nb()]
                bsx = [nb(), nb()]
                for h in range(4):
                    hp, g = HP[h]
                    P.pe(MM(psum[bsx[h % 2]][:, g * 128:(g + 1) * 128], kT[hp:hp + 64, g, tok], qT[hp:hp + 64, g, tok]),
                         r=[BkT, BqT], w=[Bps[bsx[h % 2]]])
                for h in range(4):
                    hp, g = HP[h]
                    P.dve(STT(scm[h][:], psum[bsx[h % 2]][:, g * 128:(g + 1) * 128], g_[:, 20 + h:21 + h], TRI, ALU.mult, ALU.mult),
                          r=[Bps[bsx[h % 2]], Bmg[k], Bcst], w=[Bscm[h]])
                for h in range(4):
                    hp, g = HP[h]
                    oo = psum[b_oo[h // 2]][:, (h % 2) * 129:(h % 2) * 129 + 129]
                    P.pe(MM(oo, scm[h][:], vp[k][:, h, :], start=True, stop=False), r=[Bscm[h], Bvp[k]], w=[Bps[b_oo[h // 2]]])
                    P.pe(MM(oo, qTA[hp:hp + 64, g, tok], Cb[0][hp:hp + 64, h, :], start=False, stop=False),
                         r=[BqTAB, BCb[h][0]], w=[Bps[b_oo[h // 2]]])
                    P.pe(MM(oo, qTB[hp:hp + 64, g, tok], Cb[1][hp:hp + 64, h, :], start=False, stop=True),
                         r=[BqTAB, BCb[h][1]], w=[Bps[b_oo[h // 2]]])
            if pass2:
                P.begin_defer()
                P.act(ACTF(eF[k][:], psum[b_o][:], AF.Square), r=[Bps[b_o]], w=[BeF[k]])
                b_q = nb()
                P.pe(MM(psum[b_q][:], ones_bf[:], eF[k][:]), r=[Bident, BeF[k]], w=[Bps[b_q]])
                P.dve(TS(rsF[0][:], psum[b_q][:], 1.0 / 128, EPS, ALU.mult, ALU.add), r=[Bps[b_q]], w=[BrsF[0]])
                P.act(ACTF(rsF[0][:], rsF[0][:], AF.Ln), r=[BrsF[0]], w=[BrsF[0]])
                P.act(ACTF(rsF[0][:], rsF[0][:], AF.Exp, scale=-0.5), r=[BrsF[0]], w=[BrsF[0]])
                P.dve(TT(rsF[0][:], psum[b_o][:], rsF[0][:], ALU.mult), r=[Bps[b_o], BrsF[0]], w=[BrsF[0]])
                P.dve(TT(mixT[:, 0:4, tok], rsF[0][:].rearrange("p (h t) -> p h t", h=4), hgT[:, :, tok], ALU.mult),
                      r=[BrsF[0], BhgT], w=[BmixT[s4]])
            if pass2:
                seq_ho = P.end_defer()
                P.begin_defer()
                for h in range(4):
                    ob = psum[b_oo[h // 2]]
                    o0 = (h % 2) * 129
                    P.dve(TT(g_[:, 32 + h:33 + h], ob[:, o0 + 128:o0 + 129], g_[:, 16 + h:17 + h], ALU.mult),
                          r=[Bps[b_oo[h // 2]], Bmg[k]], w=[Bmg[k]])
                P.dve(TS(g_[:, 36:40], g_[:, 32:36], -1.0, None, ALU.mult), r=[Bmg[k]], w=[Bmg[k]])
                P.dve(TT(g_[:, 32:36], g_[:, 32:36], g_[:, 36:40], ALU.max), r=[Bmg[k]], w=[Bmg[k]])
                P.dve(TS(g_[:, 32:36], g_[:, 32:36], 1.0, None, ALU.max), r=[Bmg[k]], w=[Bmg[k]])
                P.dve(lambda e, o=g_[:, 32:36], i=g_[:, 32:36]: e.reciprocal(out=o, in_=i), r=[Bmg[k]], w=[Bmg[k]])
                P.dve(TT(g_[:, 32:36], g_[:, 32:36], g_[:, 16:20], ALU.mult), r=[Bmg[k]], w=[Bmg[k]])
                for h in range(4):
                    ob = psum[b_oo[h // 2]]
                    o0 = (h % 2) * 129
                    P.act(ACTF(hjunk, ob[:, o0:o0 + 128], AF.Square, scale=g_[:, 32 + h:33 + h], accum_out=g_[:, 36 + h:37 + h]),
                          r=[Bps[b_oo[h // 2]], Bmg[k]], w=[Bhjunk, Bmg[k]])
                P.dve(TS(g_[:, 36:40], g_[:, 36:40], 1.0 / 128, EPS, ALU.mult, ALU.add), r=[Bmg[k]], w=[Bmg[k]])
                P.act(ACTF(g_[:, 36:40], g_[:, 36:40], AF.Ln), r=[Bmg[k]], w=[Bmg[k]])
                P.act(ACTF(g_[:, 36:40], g_[:, 36:40], AF.Exp, scale=-0.5), r=[Bmg[k]], w=[Bmg[k]])
                P.dve(TT(g_[:, 36:40], g_[:, 36:40], g_[:, 32:36], ALU.mult), r=[Bmg[k]], w=[Bmg[k]])
                for h in range(4):
                    ob = psum[b_oo[h // 2]]
                    o0 = (h % 2) * 129
                    P.dve(STT(mlo[k][:, h * 128:(h + 1) * 128], ob[:, o0:o0 + 128], g_[:, 36 + h:37 + h], mln_sb[:, h * 128:(h + 1) * 128],
                              ALU.mult, ALU.mult), r=[Bps[b_oo[h // 2]], Bmg[k], Bpar], w=[Bmlo[k]])
                P.dve(TT(mlo[k][:], mlo[k][:], sgm[k][:], ALU.mult), r=[Bsgm[k], Bmlo[k]], w=[Bmlo[k]])
                seq_mo = P.end_defer()
                P.replay(seq_ho, seq_mo)
                b = nbt()
                pv = psum_b[b][:, 0:512].rearrange("p (a c) -> p a c", a=4)
                for h in range(4):
                    P.pe(TR(pv[:, h, :], mlo[k][:, h * 128:(h + 1) * 128], ident_bf[:]), r=[Bmlo[k], Bident], w=[Bpt[b]])
                P.act(lambda e, o=mixT[:, 4:8, tok], i=pv: e.copy(out=o, in_=i), r=[Bpt[b]], w=[BmixT[s4]])
                banks = [nb(), nb()]
                for hh in range(2):
                    for jc in range(8):
                        P.pe(MM(psum[banks[hh]][:], mixT[:, jc, tok], wmo[:, jc, hh * 512:(hh + 1) * 512], start=(jc == 0), stop=(jc == 7)),
                             r=[BmixT[s4], Bwmo], w=[Bps[banks[hh]]])
                postnorm_res(s, banks, 1.0, s4, tmp, Btmp)

        run_pass(False)
        P.barrier(MS(stat[:, 63:64], 0.0))
        P.dma("pool", wmo[:], wmo_d.rearrange("(c p) n -> p c n", p=128), w=[Bwmo])
        Bex = [Buf() for _ in range(12)]
        for h in range(4):
            P.dma("sp", cc_src[:, h * 128:(h + 1) * 128], S_sb[:, h, :], r=[BS[h]], w=[Bex[3 * h]])
            P.dma("sp", cc_src[:, 512 + h * 129:512 + (h + 1) * 129], C_sb[:, h, :], r=[BC[h]], w=[Bex[3 * h + 1]])
            P.dma("sp", cc_src[:, 1028 + 3 * h:1028 + 3 * h + 3], qk_pre[:, h, 0:3], r=[Bqk[h]], w=[Bex[3 * h + 2]])
        Bdst = Buf()
        P.cc(lambda e: e.collective_compute("AllGather", ALU.bypass, replica_groups=[[0, 1], [2, 3], [4, 5], [6, 7]],
                                            ins=[cc_src.opt()], outs=[cc_dst.opt()]), Bex, [Bdst])
        P.dma("sp", gath[:], cc_dst[0:128, :], r=[Bdst], w=[Bgath])
        run_pass(True)

    outs = []
    if stage >= 1:
        o1 = ffn(w1i_d, w1o_d, 0, stage == 1)
        outs = o1
    if stage >= 2:
        P.barrier(MS(stat[:, 63:64], 0.0))
        mixer()
        if stage == 2:
            outs = [P.dma("sp", ov[:, s, :], x_sb[:, s, :], r=[Bx[s]]) for s in range(16)]
    if stage >= 3:
        P.barrier(MS(stat[:, 63:64], 0.0))
        outs = ffn(w2i_d, w2o_d, 2, True)
    P.emit(final_wait_ops=outs)
    print("ops", P.stats)
    return nc


def _consts():
    c = np.zeros((128, NCST), np.float32)
    s = np.arange(128)[:, None]
    t = np.arange(128)[None, :]
    same = (s // 64) == (t // 64)
    c[:, 0:128] = np.eye(128, dtype=np.float32)
    tri = (same & (s <= t)).astype(np.float32)
    sel = (same & ((s % 64) <= 31)).astype(np.float32)
    c[:, 128:256] = tri
    c[:, 256:384] = tri - sel
    c[:, 384:512] = (same & (s > t)).astype(np.float32)
    c[:, 512:640] = 1.0
    sv = np.arange(128)
    c[:, 640] = (sv <= 31)
    c[:, 641] = (sv >= 64) & (sv <= 95)
    c[:, 642] = (sv < 64)
    c[:, 643] = (sv >= 64)
    c[:, 644] = (sv < 64)
    c[:, 645] = (sv >= 64)
    return c


_NC_CACHE = {}


def _prep_inputs(inputs):
    f = lambda a: np.ascontiguousarray(np.asarray(a, dtype=np.float32))
    x = f(inputs["x"])
    rep = lambda v: np.ascontiguousarray(np.broadcast_to(np.asarray(v, np.float32).reshape(1, -1), (128, np.asarray(v).size)))
    common = {
        "w1i": f(inputs["ffn1_w_in"][0]), "w1o": f(inputs["ffn1_w_out"][0]),
        "w2i": f(inputs["ffn2_w_in"][0]), "w2o": f(inputs["ffn2_w_out"][0]),
        "wmi": f(inputs["w_mix_in"][0]), "wmo": f(inputs["w_mix_out"][0]),
        "gpre": rep(inputs["norm_pre"][0]), "gpost": rep(inputs["norm_post"][0]),
        "lbl": rep(inputs["hgrn_lb_logits"]),
        "hgn": f(np.asarray(inputs["hgrn_norm"][0]).reshape(4, 128).T),
        "mln": rep(inputs["mlstm_norm"][0]),
        "cw": f(np.asarray(inputs["conv_w"][0]).reshape(4, 4, 128).transpose(2, 1, 0).reshape(128, 16)),
        "cb": f(np.asarray(inputs["conv_b"][0]).reshape(4, 128).T),
        "gb": rep(inputs["mlstm_gate_bias"][0]),
        "cst": _consts(),
    }
    in_maps = []
    for c in range(8):
        b, half = c // 2, c % 2
        m = dict(common)
        m["x"] = np.ascontiguousarray(x[b, half * NTOK:(half + 1) * NTOK, :])
        m["flag"] = np.full((128, 1), float(half), np.float32)
        in_maps.append(m)
    return in_maps


def kernel(**inputs):
    stage = int(inputs.pop("_stage", 3)) if "_stage" in inputs else 3
    if stage not in _NC_CACHE:
        _NC_CACHE[stage] = build_program(stage)
    nc = _NC_CACHE[stage]
    in_maps = _prep_inputs(inputs)
    res = run_bass_kernel_spmd(nc, in_maps, core_ids=list(range(8)))
    out = np.empty((4, 2 * NTOK, D), np.float32)
    for c in range(8):
        b, half = c // 2, c % 2
        out[b, half * NTOK:(half + 1) * NTOK, :] = res.results[c]["out"]
    return out
```
